# Optimizing a Trainium2 kernel written in Bass

```python
import math
import jax, jax.numpy as jnp
from jax import lax
import numpy as np

D_MODEL = 1024
BATCH = 32
SEQ = 256
DEPTH = 4
DEC_BATCH = 2
DEC_SEQ = 4096
PAST_LEN = 256

GRID_W = 64
EPS = 1e-6
ROPE_BASE = 10000.0
BRANCH_W = D_MODEL
M_HEADDIM = 64
M_HEADS = BRANCH_W // M_HEADDIM
M_INNER = M_HEADS * M_HEADDIM
M_GROUPS = 2
M_STATE = 64
M_CONV = 3
M_CHUNK = 128
M_CONV_CH = M_INNER + 2 * M_GROUPS * M_STATE
DA_QK = 64
DA_VD = 2 * DA_QK
DA_HEADS = BRANCH_W // DA_VD
DA_INNER = DA_HEADS * DA_VD
Q_BLOCK = 128
SG_WIDTH = BRANCH_W
SG_GROUPS = 8
SG_GDIM = SG_WIDTH // SG_GROUPS
SG_CHUNK = 128
SPLIT_SIZES = (M_INNER, M_CONV_CH, 2 * M_HEADS,
               DA_INNER, DA_INNER, DA_INNER, DA_INNER,
               SG_WIDTH, SG_WIDTH, SG_WIDTH,
               3 * D_MODEL)
N_IN = sum(SPLIT_SIZES)

kernel_name = "hybrid_diffusion_ssd_diffattn_sgmlp_step"

F32 = jnp.float32


def rmsnorm(x, w):
    xf = x.astype(F32)
    y = xf * lax.rsqrt(jnp.mean(xf * xf, -1, keepdims=True) + EPS)
    return (y * w.astype(F32)).astype(x.dtype)


def layernorm(x, w):
    xf = x.astype(F32)
    xc = xf - jnp.mean(xf, -1, keepdims=True)
    y = xc * lax.rsqrt(jnp.mean(xc * xc, -1, keepdims=True) + EPS)
    return (y * w.astype(F32)).astype(x.dtype)


def axial_rope(n_tokens):
    n_rows = n_tokens // GRID_W
    rows = jnp.repeat(jnp.arange(n_rows, dtype=F32), GRID_W)
    cols = jnp.tile(jnp.arange(GRID_W, dtype=F32), n_rows)
    n_freq = DA_QK // 4
    inv = ROPE_BASE ** (-jnp.arange(n_freq, dtype=F32) / n_freq)
    ang = jnp.concatenate([rows[:, None] * inv, cols[:, None] * inv], -1)
    return jnp.cos(ang), jnp.sin(ang)


def apply_rope(x, cos, sin):
    x1, x2 = jnp.split(x.astype(F32), 2, axis=-1)
    c = cos[:, None, None, :]
    s = sin[:, None, None, :]
    return jnp.concatenate([x1 * c - x2 * s, x1 * s + x2 * c], -1).astype(x.dtype)


def depthwise_conv(x, w, b):
    kern = jnp.transpose(w)[:, None, :].astype(x.dtype)
    y = lax.conv_general_dilated(x, kern, window_strides=(1,), padding=[(M_CONV // 2, M_CONV // 2)],
                                 dimension_numbers=("NWC", "WIO", "NWC"), feature_group_count=x.shape[-1])
    return y + b.astype(x.dtype)


def segsum(a):
    T = a.shape[-1]
    cs = jnp.cumsum(a, -1)
    s = cs[..., :, None] - cs[..., None, :]
    return jnp.where(jnp.tril(jnp.ones((T, T), bool)), s, -jnp.inf)


def ssd_scan(xdt, a, b, c, init_state):
    bt, T, H, P = xdt.shape
    N = b.shape[-1]
    nc = T // M_CHUNK
    xdt = xdt.astype(F32).reshape(bt, nc, M_CHUNK, H, P)
    b = b.astype(F32).reshape(bt, nc, M_CHUNK, H, N)
    c = c.astype(F32).reshape(bt, nc, M_CHUNK, H, N)
    a = a.astype(F32).reshape(bt, nc, M_CHUNK, H).transpose(0, 1, 3, 2)
    a_cs = jnp.cumsum(a, -1)
    decay_in = jnp.exp(segsum(a))
    cb = jnp.einsum("bclhn,bcshn->bchls", c, b)
    y_diag = jnp.einsum("bchls,bcshp->bclhp", cb * decay_in, xdt)
    decay_to_end = jnp.exp(a_cs[..., -1:] - a_cs).transpose(0, 1, 3, 2)
    chunk_states = jnp.einsum("bclhn,bclhp->bchpn", b * decay_to_end[..., None], xdt)
    states = jnp.concatenate([init_state.astype(F32)[:, None], chunk_states], 1)
    chunk_a = jnp.pad(a_cs[..., -1].transpose(0, 2, 1), ((0, 0), (0, 0), (1, 0)))
    decay_chunk = jnp.exp(segsum(chunk_a))
    states = jnp.einsum("bhzc,bchpn->bzhpn", decay_chunk, states)
    decay_from_start = jnp.exp(a_cs).transpose(0, 1, 3, 2)
    y_off = jnp.einsum("bclhn,bchpn->bclhp", c * decay_from_start[..., None], states[:, :-1])
    return (y_diag + y_off).reshape(bt, T, H, P), states[:, -1]


def _orient(t, d):
    return jnp.flip(t, 1) if d == 1 else t


def ssd_mixer(z, xbc, dt_raw, conv_w, conv_b, a_log, dt_bias, d_skip, norm_w, init_states):
    bt, T, _ = xbc.shape
    xbc = jax.nn.silu(depthwise_conv(xbc, conv_w, conv_b))
    xs, bs, cs = jnp.split(xbc, [M_INNER, M_INNER + M_GROUPS * M_STATE], -1)
    xh = xs.reshape(bt, T, M_HEADS, M_HEADDIM).astype(F32)
    rep = M_HEADS // M_GROUPS
    bh = jnp.repeat(bs.reshape(bt, T, M_GROUPS, M_STATE), rep, axis=2)
    ch = jnp.repeat(cs.reshape(bt, T, M_GROUPS, M_STATE), rep, axis=2)
    dt = jax.nn.softplus(dt_raw.reshape(bt, T, 2, M_HEADS).astype(F32) + dt_bias.astype(F32))
    A = -jnp.exp(a_log.astype(F32))
    ys, finals = [], []
    for d in range(2):
        dt_d = dt[:, :, d]
        y_d, fin = ssd_scan(_orient(xh * dt_d[..., None], d), _orient(dt_d * A[d], d),
                            _orient(bh, d), _orient(ch, d), init_states[:, d])
        ys.append(_orient(y_d, d) + d_skip[d].astype(F32)[:, None] * xh)
        finals.append(fin)
    y = (ys[0] + ys[1]).reshape(bt, T, M_INNER)
    y = rmsnorm(y * jax.nn.silu(z.astype(F32)), norm_w).astype(z.dtype)
    return y, jnp.stack(finals, 1).astype(z.dtype)


def diff_attention(q, k, v, lam):
    bt, Tq = q.shape[:2]
    nb = Tq // Q_BLOCK
    qb = q.reshape(bt, nb, Q_BLOCK, DA_HEADS, 2, DA_QK).transpose(1, 0, 2, 3, 4, 5)
    scale = DA_QK ** -0.5
    vf = v.astype(F32)

    def block(qi):
        s = jnp.einsum("bqhmd,bkhmd->bhmqk", qi, k).astype(F32) * scale
        p = jax.nn.softmax(s, axis=-1)
        w = p[:, :, 0] - lam * p[:, :, 1]
        return jnp.einsum("bhqk,bkhd->bqhd", w, vf)

    o = lax.map(block, qb)
    return o.transpose(1, 0, 2, 3, 4).reshape(bt, Tq, DA_HEADS, DA_VD).astype(q.dtype)


def chunk_sgmlp(u, v, vnorm_w, ws, bs):
    bt, T, _ = v.shape
    u = jax.nn.gelu(u)
    v = layernorm(jax.nn.gelu(v), vnorm_w)
    vc = v.reshape(bt, T // SG_CHUNK, SG_CHUNK, SG_GROUPS, SG_GDIM)
    mixed = jnp.einsum("gts,bcsge->bctge", ws, vc) + jnp.transpose(bs)[None, None, :, :, None]
    return u * mixed.reshape(bt, T, SG_WIDTH)


def trunk_layer(x, shift, scale, gate, p, layer_idx, rope, ctx_k, ctx_v, ssm_init):
    (pre_w, post_w, w_in, conv_w, conv_b, a_log, dt_bias, d_skip, m_norm_w,
     lam_vecs, head_norm_w, vnorm_w, ws, bs, w_branch, w_out) = p
    bt, T, _ = x.shape
    latent = ctx_k is not None
    h = rmsnorm(x, pre_w) * (1 + scale) + shift
    proj = h @ w_in
    idx = [int(i) for i in np.cumsum(SPLIT_SIZES)[:-1]]
    z, xbc, dt_raw, q, k, v, g_b, u, sv, g_c, mg = jnp.split(proj, idx, -1)

    if ssm_init is None:
        ssm_init = jnp.zeros((bt, 2, M_HEADS, M_HEADDIM, M_STATE), x.dtype)
    y_a, ssm_final = ssd_mixer(z, xbc, dt_raw, conv_w, conv_b, a_log, dt_bias, d_skip, m_norm_w, ssm_init)

    q = q.reshape(bt, T, DA_HEADS, 2, DA_QK)
    k = k.reshape(bt, T, DA_HEADS, 2, DA_QK)
    v = v.reshape(bt, T, DA_HEADS, DA_VD)
    if latent:
        cos, sin = rope
        L = ctx_k.shape[1]
        k_all = jnp.concatenate([ctx_k.reshape(bt, L, DA_HEADS, 2, DA_QK).astype(k.dtype), apply_rope(k, cos, sin)], 1)
        v_all = jnp.concatenate([ctx_v.astype(v.dtype), v], 1)
        q = apply_rope(q, cos, sin)
    else:
        k_all, v_all = k, v
    lam_init = 0.8 - 0.6 * math.exp(-0.3 * layer_idx)
    lv = lam_vecs.astype(F32)
    lam = jnp.exp(jnp.sum(lv[0] * lv[1])) - jnp.exp(jnp.sum(lv[2] * lv[3])) + lam_init
    o = diff_attention(q, k_all, v_all, lam)
    o = rmsnorm(o, head_norm_w) * (1 - lam_init)
    y_b = o.reshape(bt, T, DA_INNER) * jax.nn.silu(g_b)

    y_c = chunk_sgmlp(u, sv, vnorm_w, ws, bs) * jax.nn.silu(g_c)

    ga, gb, gc = jnp.split(jax.nn.sigmoid(mg), 3, -1)
    merged = ga * (y_a @ w_branch[0]) + gb * (y_b @ w_branch[1]) + gc * (y_c @ w_branch[2])
    out = rmsnorm(merged @ w_out, post_w)
    x = x + gate * out
    return x, k.reshape(bt, T, DA_HEADS, 2 * DA_QK), v, ssm_final


def setup_inputs(seed: int = 0) -> dict:
    key = jax.random.key(seed)
    ks = jax.random.split(key, 32)

    def nrm(k, shape, s):
        return jax.random.normal(k, shape, F32) * s

    dt0 = jnp.exp(jax.random.uniform(ks[15], (DEPTH, 2, M_HEADS), F32, math.log(1e-3), math.log(1e-1)))
    return {
        "x_prompt": nrm(ks[0], (BATCH, SEQ, D_MODEL), 1.0),
        "x_sample": nrm(ks[1], (DEC_BATCH, DEC_SEQ, D_MODEL), 1.0),
        "cache_k": nrm(ks[2], (DEC_BATCH, DEPTH, PAST_LEN, DA_HEADS, 2 * DA_QK), 1.0),
        "cache_v": nrm(ks[3], (DEC_BATCH, DEPTH, PAST_LEN, DA_HEADS, DA_VD), 1.0),
        "state_ssm": nrm(ks[4], (DEC_BATCH, DEPTH, 2, M_HEADS, M_HEADDIM, M_STATE), 0.5),
        "c": nrm(ks[5], (DEC_BATCH, D_MODEL), 1.0),
        "c_ctx": nrm(ks[6], (D_MODEL,), 1.0),
        "pre_norm_w": 1.0 + nrm(ks[7], (DEPTH, D_MODEL), 0.02),
        "post_norm_w": 1.0 + nrm(ks[8], (DEPTH, D_MODEL), 0.02),
        "w_mod": nrm(ks[9], (DEPTH, D_MODEL, 3 * D_MODEL), 0.5 * D_MODEL ** -0.5),
        "b_mod": nrm(ks[10], (DEPTH, 3 * D_MODEL), 0.01),
        "w_in": nrm(ks[11], (DEPTH, D_MODEL, N_IN), D_MODEL ** -0.5),
        "m_conv_w": nrm(ks[12], (DEPTH, M_CONV_CH, M_CONV), M_CONV ** -0.5),
        "m_conv_b": nrm(ks[13], (DEPTH, M_CONV_CH), 0.01),
        "m_A_log": jnp.log(jax.random.uniform(ks[14], (DEPTH, 2, M_HEADS), F32, 1.0, 16.0)),
        "m_dt_bias": dt0 + jnp.log(-jnp.expm1(-dt0)),
        "m_D": 1.0 + nrm(ks[16], (DEPTH, 2, M_HEADS), 0.1),
        "m_norm_w": 1.0 + nrm(ks[17], (DEPTH, M_INNER), 0.02),
        "da_lambda": nrm(ks[18], (DEPTH, 4, DA_QK), 0.1),
        "da_head_norm_w": 1.0 + nrm(ks[19], (DEPTH, DA_VD), 0.02),
        "sg_vnorm_w": 1.0 + nrm(ks[20], (DEPTH, SG_WIDTH), 0.02),
        "sg_spatial_w": nrm(ks[21], (DEPTH, SG_GROUPS, SG_CHUNK, SG_CHUNK), SG_CHUNK ** -0.5),
        "sg_spatial_b": nrm(ks[22], (DEPTH, SG_GROUPS, SG_CHUNK), 0.01),
        "w_branch": nrm(ks[23], (DEPTH, 3, BRANCH_W, D_MODEL), BRANCH_W ** -0.5),
        "w_out": nrm(ks[24], (DEPTH, D_MODEL, D_MODEL), D_MODEL ** -0.5),
    }


def reference(x_prompt, x_sample, cache_k, cache_v, state_ssm, c, c_ctx, pre_norm_w, post_norm_w,
              w_mod, b_mod, w_in, m_conv_w, m_conv_b, m_A_log, m_dt_bias, m_D, m_norm_w,
              da_lambda, da_head_norm_w, sg_vnorm_w, sg_spatial_w, sg_spatial_b, w_branch, w_out):
    rope = axial_rope(x_sample.shape[1])
    silu_ctx = jax.nn.silu(c_ctx)
    silu_lat = jax.nn.silu(c)
    xp, xs = x_prompt, x_sample
    ks_out, vs_out, ss_out = [], [], []
    for l in range(DEPTH):
        p = (pre_norm_w[l], post_norm_w[l], w_in[l], m_conv_w[l], m_conv_b[l], m_A_log[l], m_dt_bias[l],
             m_D[l], m_norm_w[l], da_lambda[l], da_head_norm_w[l], sg_vnorm_w[l], sg_spatial_w[l],
             sg_spatial_b[l], w_branch[l], w_out[l])
        sh, sc, gt = jnp.split(silu_ctx @ w_mod[l] + b_mod[l], 3, -1)
        xp, k_l, v_l, s_l = trunk_layer(xp, sh, sc, gt, p, l, None, None, None, None)
        ks_out.append(k_l)
        vs_out.append(v_l)
        ss_out.append(s_l)
        sh, sc, gt = jnp.split(silu_lat @ w_mod[l] + b_mod[l], 3, -1)
        xs = trunk_layer(xs, sh[:, None], sc[:, None], gt[:, None], p, l, rope,
                         cache_k[:, l], cache_v[:, l], state_ssm[:, l])[0]
    new_cache_k = jnp.stack(ks_out, 1)
    new_cache_v = jnp.stack(vs_out, 1)
    new_state_ssm = jnp.stack(ss_out, 1)
    return (xp, xs, new_cache_k, new_cache_v, new_state_ssm)
```

```python
import contextlib
import math
import os
import numpy as np
import concourse.bass as bass
import concourse.mybir as mybir
from concourse.bass_utils import run_bass_kernel_spmd

F32 = mybir.dt.float32
BF16 = mybir.dt.bfloat16
AF = mybir.ActivationFunctionType
ALU = mybir.AluOpType
AX = mybir.AxisListType

ENGS = ("pe", "act", "dve", "pool", "sp")
NLANES = 32
DEPTH = 4
D = 1024
NIN = 12576
EPS = 1e-6
C_Z, C_XBC, C_DT, C_Q, C_K, C_V, C_GB, C_U, C_SV, C_GC, C_MG = (
    0, 1024, 2304, 2336, 3360, 4384, 5408, 6432, 7456, 8480, 9504)
GROUPS4 = [[0, 1, 2, 3], [4, 5, 6, 7]]


class Buf:
    __slots__ = ("name", "w", "r", "excl")

    def __init__(self, name):
        self.name = name
        self.w = None
        self.r = {}
        self.excl = False


class _Rec:
    def __init__(self):
        self.call = None

    def __getattr__(self, name):
        def f(*a, **k):
            assert self.call is None
            self.call = (name, a, k)
        return f


def _freeze(fn):
    r = _Rec()
    fn(r)
    name, a, k = r.call
    return lambda e: getattr(e, name)(*a, **k)


class Prog:
    def __init__(self, nc):
        self.nc = nc
        self.ops = {e: [] for e in ENGS}
        self.cnt = {e: 0 for e in ENGS}
        self.seen = {e: {} for e in ENGS}
        self.lane_cnt = [0] * NLANES
        self.lane_i = 0
        self.lane_q = {}
        self.cc_cnt = 0
        self.sem = {}
        self.stack = contextlib.ExitStack()
        self.nbuf = 0

    def sb(self, name, shape, dt):
        return self.stack.enter_context(self.nc.sbuf_tensor("sb_" + name, list(shape), dt))

    def ps(self, name, shape, dt):
        return self.stack.enter_context(self.nc.psum_tensor(name, list(shape), dt))

    def buf(self, name=None):
        self.nbuf += 1
        return Buf(name or f"b{self.nbuf}")

    def _waits(self, eng, reads, writes):
        need = {}

        def add(k, v):
            if k == "pe" and eng == "pe":
                return
            if need.get(k, 0) < v:
                need[k] = v
        for b in reads:
            if b.w is not None:
                add(*b.w)
            if b.excl:
                for k, v in b.r.items():
                    if k != eng:
                        add(k, v)
        for b in writes:
            if b.w is not None:
                add(*b.w)
            for k, v in b.r.items():
                add(k, v)
        out = []
        seen = self.seen[eng]
        for k, v in need.items():
            if seen.get(k, 0) < v:
                seen[k] = v
                out.append((k, v))
        return out

    def _mark(self, key, val, reads, writes):
        for b in reads:
            if b.r.get(key, 0) < val:
                b.r[key] = val
        for b in writes:
            b.w = (key, val)
            b.r = {}

    def op(self, eng, fn, reads=(), writes=()):
        waits = self._waits(eng, reads, writes)
        self.cnt[eng] += 1
        self.ops[eng].append((waits, _freeze(fn), (eng, 1)))
        self._mark(eng, self.cnt[eng], reads, writes)

    def mm(self, fns, reads=(), writes=()):
        waits = self._waits("pe", reads, writes)
        self.cnt["pe"] += 1
        n = len(fns)
        for i, fn in enumerate(fns):
            self.ops["pe"].append((waits if i == 0 else [], _freeze(fn), ("pe", 1) if i == n - 1 else None))
        self._mark("pe", self.cnt["pe"], reads, writes)

    def dma(self, q, out, in_, reads=(), writes=(), **kw):
        waits = self._waits(q, reads, writes)
        lo, hi = (0, 20) if q == "sp" else (20, NLANES)
        li = self.lane_q.get(q, 0)
        self.lane_q[q] = li + 1
        lane = lo + li % (hi - lo)
        key = f"d{lane}"
        prev = self.lane_cnt[lane]
        if prev > 0 and self.seen[q].get(key, 0) < prev:
            self.seen[q][key] = prev
            waits = waits + [(key, prev)]
        self.lane_cnt[lane] += 16
        self.ops[q].append((waits, lambda e: e.dma_start(out=out, in_=in_, **kw), (key, 16)))
        self._mark(key, self.lane_cnt[lane], reads, writes)

    def cc(self, kind, groups, in_ap, out_ap, reads=(), writes=()):
        waits = self._waits("pool", reads, writes)
        self.cc_cnt += 1
        self.ops["pool"].append((waits, lambda e: e.collective_compute(
            kind, ALU.bypass, replica_groups=groups, ins=[in_ap], outs=[out_ap]), ("cc", 1)))
        self._mark("cc", self.cc_cnt, reads, writes)

    def emit(self):
        nc = self.nc
        st = self.stack
        keys = list(ENGS[:4]) + [f"d{i}" for i in range(NLANES)] + ["cc"]
        for k in keys:
            self.sem[k] = st.enter_context(nc.semaphore("s_" + k))
        fin = [(f"d{i}", self.lane_cnt[i]) for i in range(NLANES) if self.lane_cnt[i] > 0]
        fin += [(e, self.cnt[e]) for e in ENGS[:4] if self.cnt[e] > 0]
        if self.cc_cnt:
            fin.append(("cc", self.cc_cnt))
        block = st.enter_context(nc.Block())
        sem = self.sem

        def run(e, lst, final=None):
            for waits, fn, inc in lst:
                for k, v in waits:
                    e.wait_ge(sem[k], v)
                ins = fn(e)
                if inc is not None:
                    if inc[0] == "cc":
                        ins.then_inc(sem["cc"])
                    else:
                        ins.then_inc(sem[inc[0]], inc[1])
            if final:
                for k, v in final:
                    e.wait_ge(sem[k], v)

        @block.tensor
        def _(e):
            run(e, self.ops["pe"])

        @block.scalar
        def _(e):
            run(e, self.ops["act"])

        @block.vector
        def _(e):
            run(e, self.ops["dve"])

        @block.gpsimd
        def _(e):
            run(e, self.ops["pool"])

        @block.sync
        def _(e):
            run(e, self.ops["sp"], fin)


class _Stop(Exception):
    pass


def build(stop=None):
    nc = bass.Bass("TRN2", target_bir_lowering=False)
    P = Prog(nc)
    ckn = [0]
    dbg_on = bool(os.environ.get("KDBG"))
    dbg_i = [0]

    def dbg(name, ap, bufs, shape, dt=BF16):
        if not dbg_on:
            return
        dbg_i[0] += 1
        d = nc.dram_tensor(f"dbg_{name}", list(shape), dt, kind="ExternalOutput").ap()
        P.dma("sp", d, ap, reads=bufs)

    sub = int(os.environ["KSUB"]) if os.environ.get("KSUB") else None
    subn = [0]

    def ck2(tag):
        subn[0] += 1
        if sub is not None and subn[0] == sub:
            print("SUBSTOP at", subn[0], tag)
            raise _Stop()

    def ck(tag):
        ckn[0] += 1
        if stop is not None and ckn[0] == stop:
            print("STOP at", ckn[0], tag)
            raise _Stop()

    def din(name, shape, dt=F32):
        return nc.dram_tensor(name, list(shape), dt, kind="ExternalInput").ap()

    def dout(name, shape):
        return nc.dram_tensor(name, list(shape), F32, kind="ExternalOutput").ap()

    xin = [din("xp", [1024, D]), din("xs", [1024, D])]
    ck_d = din("ck", [DEPTH, 256, D])
    cv_d = din("cv", [DEPTH, 256, D])
    sst_d = din("sst", [DEPTH, 2, 16, 64, 64])
    cvec_d = din("cvec", [2, D])
    pre_d = din("pre_norm_w", [DEPTH, D])
    post_d = din("post_norm_w", [DEPTH, D])
    wmod_d = din("w_mod", [DEPTH, D, 3 * D])
    bmod_d = din("b_mod", [DEPTH, 3 * D])
    win_d = din("w_in", [DEPTH, D, NIN])
    cw_d = din("m_conv_w", [DEPTH, 1280, 3])
    cb_d = din("m_conv_b", [DEPTH, 1280])
    alog_d = din("m_A_log", [DEPTH, 32])
    dtb_d = din("m_dt_bias", [DEPTH, 32])
    mD_d = din("m_D", [DEPTH, 32])
    mnw_d = din("m_norm_w", [DEPTH, D])
    lamv_d = din("da_lambda", [DEPTH, 256])
    hnw_d = din("da_head_norm_w", [DEPTH, 128])
    vnw_d = din("sg_vnorm_w", [DEPTH, D])
    sgw_d = din("sg_spatial_w", [DEPTH, 8, 128, 128])
    sgb_d = din("sg_spatial_b", [DEPTH, 1024])
    wbr_d = din("w_branch", [DEPTH, 3, D, D])
    wout_d = din("w_out", [DEPTH, D, D])
    cst_d = din("cst", [128, 1024])
    rkc_d = din("rkc", [128, 16])
    rope_d = din("ropec", [128, 2, 1024])

    yout = [dout("yp", [1024, D]), dout("ys", [1024, D])]
    nk_d = dout("nk", [4, DEPTH, 256, D])
    nv_d = dout("nv", [4, DEPTH, 256, D])
    nst_d = dout("nst", [4, DEPTH, 2, 16, 64, 64])

    xres = [nc.dram_tensor(f"xres{g}", [1024, D], F32).ap() for g in range(2)]
    bh = [nc.dram_tensor(f"bh{i}", [128, 20], BF16).ap() for i in range(2)]
    gh = [nc.dram_tensor(f"gh{i}", [512, 20], BF16).ap() for i in range(2)]
    bkv = [[nc.dram_tensor(f"bkv{i}_{c}", [512, 1024], BF16).ap() for c in range(4)] for i in range(2)]
    gkv = [[nc.dram_tensor(f"gkv{i}_{c}", [2048, 1024], BF16).ap() for c in range(4)] for i in range(2)]
    bst = [nc.dram_tensor(f"bst{i}", [128, 1056], F32).ap() for i in range(2)]
    gst = [nc.dram_tensor(f"gst{i}", [512, 1056], F32).ap() for i in range(2)]
    Bd = {n: P.buf(n) for n in ["xres0", "xres1", "bh0", "bh1", "gh0", "gh1", "bkv0", "bkv1",
                                 "gkv0", "gkv1", "bst0", "bst1", "gst0", "gst1"]}

    def T(name, shape, dt=F32):
        return P.sb(name, shape, dt), P.buf(name)

    cst, Bcst = T("cst", [128, 1024])
    rkc, Brkc = T("rkc", [128, 16])
    ropeb, Brope = T("ropeb", [128, 2, 1024], BF16)
    identb, Bidb = T("identb", [128, 128], BF16)
    rmtb, Brmt = T("rmtb", [128, 128], BF16)
    onesb, Bonesb = T("onesb", [128, 128], BF16)
    identf = cst[:, 0:128]
    LE = cst[:, 128:256]
    GE = cst[:, 256:384]
    GT = cst[:, 384:512]
    LT = cst[:, 512:640]
    onesf = cst[:, 768:896]
    c_one = cst[:, 896:897]
    c_eps = cst[:, 897:898]
    c_zero = cst[:, 898:899]

    scT, BscT = T("scT", [128, 8, 2], BF16)
    scRep, BscRep = T("scRep", [128, 2, 8, 128], BF16)
    modT, BmodT = T("modT", [128, 16, 2])
    bmT, BbmT = T("bmT", [128, 24])
    preT, BpreT = T("preT", [128, 8])
    gmul, Bgmul = T("gmul", [128, 8, 2])
    shiftT, Bshift = T("shiftT", [128, 8, 2])
    lamt, Blamt = T("lamt", [128, 264])
    cwT, BcwT = T("cwT", [128, 10, 3])
    cbT, BcbT = T("cbT", [128, 10])
    arow, Barow = T("arow", [128, 32])
    dtbrow, Bdtb = T("dtbrow", [128, 32])
    drow, Bdrow = T("drow", [128, 48])
    gpw, Bgpw = T("gpw", [128, 1024])
    hwT, BhwT = T("hwT", [128, 2])
    wsT, BwsT = T("wsT", [128, 8, 128], BF16)

    hT, _ = T("hT", [128, 8, 1024], BF16)
    BhT = [P.buf(f"hT{j}") for j in range(8)]
    RA, BRA = T("RA", [128, 10, 1040], BF16)
    RY, BRY = T("RY", [128, 8, 1024], BF16)
    RK, BRK = T("RK", [128, 8, 1024], BF16)
    RV, BRV = T("RV", [128, 8, 1024], BF16)
    RG, BRG = T("RG", [128, 8, 1024], BF16)
    NW = 2
    wb = [T(f"wb{i}", [128, 8, 512], BF16) for i in range(NW)]
    stg, Bstg = T("stg", [128, 2, 1024])
    stg2 = [T(f"stg2_{i}", [128, 1024]) for i in range(2)]
    tmpb = [T(f"tmpb{i}", [128, 2, 1024], BF16) for i in range(2)]
    tf = [T(f"tf{i}", [128, 512]) for i in range(4)]
    small, Bsmall = T("small", [128, 64])
    dtt, Bdtt = T("dtt", [128, 8, 32])
    at, Bat = T("at", [128, 8, 32])
    Et, BEt = T("Et", [128, 8, 64])
    dch, Bdch = T("dch", [128, 8, 32])
    totl, Btotl = T("totl", [128, 8, 32])
    state, Bstate = T("state", [128, 2, 512])
    stateb, Bstateb = T("stateb", [128, 2, 512], BF16)
    cbm = [T(f"cbm{i}", [128, 4, 128]) for i in range(1)]
    lhs = [T(f"lhs{i}", [128, 128]) for i in range(2)]
    ldec = [T(f"ldec{i}", [128, 128]) for i in range(2)]
    wmat = [T(f"wmat{i}", [128, 128], BF16) for i in range(4)]
    hin, Bhin = T("hin", [128, 2, 512])
    gall, Bgall = stg[:].rearrange("p a t -> p (a t)")[:, 0:1056], Bstg
    halo, Bhalo = T("halo", [128, 4, 20], BF16)
    halof, Bhalof = T("halof", [128, 2, 10])
    ptb = [T(f"ptb{i}", [128, 2, 512], BF16) for i in range(2)]
    xeT, Bxe = T("xeT", [128, 2, 1024], BF16)
    rowA, BrowA = T("rowA", [128, 1024])
    rowB, BrowB = T("rowB", [128, 1024])

    banks = []
    for i in range(8):
        t = P.ps(f"pb{i}", [128, 512], F32)
        _b = P.buf(f"pb{i}")
        _b.excl = True
        banks.append((t, [_b, _b, _b, _b]))

    def bkb(i):
        return banks[i][1]

    bank_rr = [0]

    def nextbank(lo=0, hi=8):
        i = lo + bank_rr[0] % (hi - lo)
        bank_rr[0] += 1
        return i

    w_i = [0]

    def wload(src_ap, ncols):
        t, b = wb[w_i[0] % NW]
        w_i[0] += 1
        P.dma("pool", t[:, :, 0:ncols], src_ap.rearrange("(kc p) c -> p kc c", p=128), writes=[b])
        return t, b

    def rstd_from(ssq_ap, n, out_ap, rb, wbuf):
        P.op("act", lambda e: e.activation(out_ap, ssq_ap, AF.Sqrt, bias=c_eps, scale=1.0 / n),
             reads=rb + [Bcst], writes=[wbuf])
        P.op("dve", lambda e: e.reciprocal(out_ap, out_ap), reads=[wbuf], writes=[wbuf])

    P.dma("sp", cst[:], cst_d, writes=[Bcst])
    P.dma("sp", rkc[:], rkc_d, writes=[Brkc])
    P.dma("pool", ropeb[:], rope_d, writes=[Brope])
    P.op("dve", lambda e: e.tensor_copy(identb[:], identf), reads=[Bcst], writes=[Bidb])
    P.op("dve", lambda e: e.tensor_copy(rmtb[:], cst[:, 640:768]), reads=[Bcst], writes=[Brmt])
    P.op("dve", lambda e: e.tensor_copy(onesb[:], onesf), reads=[Bcst], writes=[Bonesb])
    for c in range(2):
        P.dma("sp", modT[:, 0:8, c], cvec_d[c].rearrange("(kc p) -> p kc", p=128), writes=[BmodT],
              allow_slow_non_contiguous=True)
    P.op("act", lambda e: e.activation(scT[:], modT[:, 0:8, :], AF.Silu), reads=[BmodT], writes=[BscT])
    for c in range(2):
        P.op("dve", lambda e, c=c: e.tensor_copy(scRep[:, c], scT[:, :, c:c + 1].to_broadcast([128, 8, 128])),
             reads=[BscT], writes=[BscRep])

    def layer_prep(l):
        lam_init = 0.8 - 0.6 * math.exp(-0.3 * l)
        P.dma("sp", bmT[:], bmod_d[l].rearrange("(b p) -> p b", p=128), writes=[BbmT], allow_slow_non_contiguous=True)
        P.dma("sp", preT[:], pre_d[l].rearrange("(b p) -> p b", p=128), writes=[BpreT], allow_slow_non_contiguous=True)
        for wblk in range(4):
            wt, wbf = wload(wmod_d[l][:, wblk * 512:(wblk + 1) * 512], 512)
            for s in range(4):
                blk = wblk * 4 + s
                bi = nextbank()
                pt = banks[bi][0]
                P.mm([lambda e, kc=kc, s=s, pt=pt, wt=wt: e.matmul(pt[:, 0:2], wt[:, kc, s * 128:(s + 1) * 128],
                                                                  scT[:, kc, :], start=(kc == 0), stop=(kc == 7))
                      for kc in range(8)], reads=[wbf, BscT], writes=[bkb(bi)[0]])
                P.op("dve", lambda e, blk=blk, pt=pt: e.tensor_single_scalar(modT[:, blk, :], pt[:, 0:2], bmT[:, blk:blk + 1], ALU.add),
                     reads=[bkb(bi)[0], BbmT], writes=[BmodT])
        P.op("dve", lambda e: e.tensor_copy(shiftT[:], modT[:, 0:8, :]), reads=[BmodT], writes=[Bshift])
        P.op("dve", lambda e: e.tensor_single_scalar(gmul[:], modT[:, 8:16, :], 1.0, ALU.add), reads=[BmodT], writes=[Bgmul])
        P.op("dve", lambda e: e.tensor_tensor(gmul[:], gmul[:], preT[:].unsqueeze(2).to_broadcast([128, 8, 2]), ALU.mult),
             reads=[Bgmul, BpreT], writes=[Bgmul])
        P.dma("sp", lamt[:, 0:256], lamv_d[l:l + 1, :].partition_broadcast(128), writes=[Blamt])
        P.op("dve", lambda e: e.tensor_tensor(lamt[:, 0:64], lamt[:, 0:64], lamt[:, 64:128], ALU.mult), reads=[Blamt], writes=[Blamt])
        P.op("dve", lambda e: e.tensor_tensor(lamt[:, 128:192], lamt[:, 128:192], lamt[:, 192:256], ALU.mult), reads=[Blamt], writes=[Blamt])
        P.op("dve", lambda e: e.reduce_sum(lamt[:, 256:257], lamt[:, 0:64], axis=AX.X), reads=[Blamt], writes=[Blamt])
        P.op("dve", lambda e: e.reduce_sum(lamt[:, 257:258], lamt[:, 128:192], axis=AX.X), reads=[Blamt], writes=[Blamt])
        P.op("act", lambda e: e.activation(lamt[:, 258:260], lamt[:, 256:258], AF.Exp), reads=[Blamt], writes=[Blamt])
        P.op("dve", lambda e: e.tensor_tensor(lamt[:, 260:261], lamt[:, 259:260], lamt[:, 258:259], ALU.subtract), reads=[Blamt], writes=[Blamt])
        P.op("dve", lambda e: e.tensor_single_scalar(lamt[:, 261:262], lamt[:, 260:261], -lam_init, ALU.add), reads=[Blamt], writes=[Blamt])
        P.dma("sp", cwT[:], cw_d[l].rearrange("(kc p) k -> p kc k", p=128), writes=[BcwT], allow_slow_non_contiguous=True)
        P.dma("sp", cbT[:], cb_d[l].rearrange("(kc p) -> p kc", p=128), writes=[BcbT], allow_slow_non_contiguous=True)
        P.dma("sp", arow[:], alog_d[l:l + 1, :].partition_broadcast(128), writes=[Barow])
        P.op("act", lambda e: e.activation(arow[:], arow[:], AF.Exp), reads=[Barow], writes=[Barow])
        P.op("dve", lambda e: e.tensor_single_scalar(arow[:], arow[:], -1.0, ALU.mult), reads=[Barow], writes=[Barow])
        P.dma("sp", dtbrow[:], dtb_d[l:l + 1, :].partition_broadcast(128), writes=[Bdtb])
        P.dma("sp", drow[:, 0:32], mD_d[l:l + 1, :].partition_broadcast(128), writes=[Bdrow])
        P.op("dve", lambda e: e.tensor_tensor(drow[:, 32:48], drow[:, 0:16], drow[:, 16:32], ALU.add), reads=[Bdrow], writes=[Bdrow])
        P.dma("sp", hwT[:, 0:1], hnw_d[l].rearrange("(d o) -> d o", o=1), writes=[BhwT], allow_slow_non_contiguous=True)
        P.op("dve", lambda e: e.tensor_single_scalar(hwT[:, 1:2], hwT[:, 0:1], 1.0 - lam_init, ALU.mult), reads=[BhwT], writes=[BhwT])
        P.dma("sp", stg[:, 0, :].rearrange("p (g s) -> p g s", g=8), sgw_d[l].rearrange("g t s -> t g s"), writes=[Bstg])
        for g in range(8):
            bi = nextbank()
            pt = banks[bi][0]
            P.mm([lambda e, g=g, pt=pt: e.transpose(pt[:, 0:128], stg[:, 0, g * 128:(g + 1) * 128], identf)],
                 reads=[Bstg, Bcst], writes=[bkb(bi)[0]])
            P.op("act", lambda e, g=g, pt=pt: e.copy(wsT[:, g, :], pt[:, 0:128]), reads=[bkb(bi)[0]], writes=[BwsT])

    def run_group(l, kind):
        par = l % 2
        lam_init = 0.8 - 0.6 * math.exp(-0.3 * l)
        xsrc = xin[kind] if l == 0 else xres[kind]
        xdst = yout[kind] if l == DEPTH - 1 else xres[kind]
        Bxres = Bd[f"xres{kind}"]
        c = kind
        nseq = 4 if kind == 0 else 1
        cps = 2 if kind == 0 else 8
        RAv = RA[:].rearrange("p k (s t) -> p k s t", s=4)
        RQ = RA
        def ctk_ap(h, a, b):
            return RA[:, 8 + h // 4, (h % 4) * 256 + a:(h % 4) * 256 + b]
        ctv = tmpb[1][0]
        Bctv = tmpb[1][1]
        SFb = RK[:, :, 0:512]
        RSb = RK[:, :, 512:1024]
        YA = RV
        MG = RY

        def xc(kc, j, p0=0, p1=128):
            return RAv[p0:p1, kc, j // 2, 2 + (j % 2) * 128: 2 + (j % 2) * 128 + 128]

        for j in range(8):
            P.dma("sp", stg[:, 0, :], xsrc[j * 128:(j + 1) * 128, :], reads=[Bxres] if l > 0 else [], writes=[Bstg])
            P.op("act", lambda e: e.activation(stg[:, 1, :], stg[:, 0, :], AF.Square, accum_out=small[:, 0:1]),
                 reads=[Bstg], writes=[Bstg, Bsmall])
            rstd_from(small[:, 0:1], 1024, small[:, 1:2], [Bsmall], Bsmall)
            tb, tbb = tmpb[j % 2]
            P.op("dve", lambda e, tb=tb: e.tensor_single_scalar(tb[:, 0, :], stg[:, 0, :], small[:, 1:2], ALU.mult),
                 reads=[Bstg, Bsmall], writes=[tbb])
            bi = nextbank()
            ptv = banks[bi][0][:].bitcast(BF16)
            P.mm([lambda e, kc=kc, tb=tb, ptv=ptv: e.transpose(ptv[:, kc * 128:(kc + 1) * 128], tb[:, 0, kc * 128:(kc + 1) * 128], identb[:])
                  for kc in range(8)], reads=[tbb, Bidb], writes=bkb(bi))
            for kc in range(8):
                P.op("act", lambda e, kc=kc, j=j, ptv=ptv: e.activation(hT[:, kc, j * 128:(j + 1) * 128], ptv[:, kc * 128:(kc + 1) * 128],
                                                                       AF.Identity, scale=gmul[:, kc, c:c + 1], bias=shiftT[:, kc, c:c + 1]),
                     reads=bkb(bi) + [Bgmul, Bshift], writes=[BhT[j]])

        ck(f"P0{l}{kind}")

        def proj_fm(c0, ncols, evac):
            done = 0
            while done < ncols:
                n = min(512, ncols - done)
                wt, wbf = wload(win_d[l][:, c0 + done:c0 + done + n], n)
                for s in range((n + 127) // 128):
                    m = min(128, n - s * 128)
                    for nb in range(2):
                        bi = nextbank()
                        pt = banks[bi][0]
                        P.mm([lambda e, kc=kc, s=s, m=m, nb=nb, pt=pt, wt=wt: e.matmul(
                            pt[0:m, :], wt[:, kc, s * 128:s * 128 + m], hT[:, kc, nb * 512:(nb + 1) * 512],
                            start=(kc == 0), stop=(kc == 7)) for kc in range(8)],
                            reads=[wbf] + BhT[nb * 4:(nb + 1) * 4], writes=bkb(bi))
                        evac((done // 128) + s, nb, pt, bi)
                done += n

        def proj_tm(c0, ncols, evac):
            done = 0
            while done < ncols:
                n = min(512, ncols - done)
                wt, wbf = wload(win_d[l][:, c0 + done:c0 + done + n], n)
                for j in range(8):
                    bi = nextbank()
                    pt = banks[bi][0]
                    P.mm([lambda e, kc=kc, j=j, n=n, pt=pt, wt=wt: e.matmul(
                        pt[:, 0:n], hT[:, kc, j * 128:(j + 1) * 128], wt[:, kc, 0:n],
                        start=(kc == 0), stop=(kc == 7)) for kc in range(8)],
                        reads=[wbf, BhT[j]], writes=bkb(bi))
                    evac(j, done // 512, pt, bi, n)
                done += n

        def ev_rope(dst, Bdst):
            def ev(cb, nb, pt, bi):
                tb, tbb = tmpb[0]
                P.op("act", lambda e: e.copy(tb[:, 0, 0:512], pt[:]), reads=bkb(bi), writes=[tbb])
                b2 = nextbank()
                p2 = banks[b2][0]
                P.mm([lambda e: e.matmul(p2[:], rmtb[:], tb[:, 0, 0:512], start=True, stop=True)],
                     reads=[tbb, Brmt], writes=bkb(b2))
                t1, t1b = tf[0]
                t2, t2b = tf[1]
                P.op("dve", lambda e: e.tensor_tensor(t1[:], pt[:], ropeb[:, 0, nb * 512:(nb + 1) * 512], ALU.mult),
                     reads=bkb(bi) + [Brope], writes=[t1b])
                P.op("dve", lambda e: e.tensor_tensor(t2[:], p2[:], ropeb[:, 1, nb * 512:(nb + 1) * 512], ALU.mult),
                     reads=bkb(b2) + [Brope], writes=[t2b])
                P.op("dve", lambda e: e.tensor_tensor(dst[:, cb, nb * 512:(nb + 1) * 512], t1[:], t2[:], ALU.add),
                     reads=[t1b, t2b], writes=[Bdst])
            return ev

        def ev_copy(dst, Bdst, func=None):
            def ev(cb, nb, pt, bi):
                if func is None:
                    P.op("act", lambda e: e.copy(dst[:, cb, nb * 512:(nb + 1) * 512], pt[:]), reads=bkb(bi), writes=[Bdst])
                else:
                    P.op("act", lambda e: e.activation(dst[:, cb, nb * 512:(nb + 1) * 512], pt[:], func), reads=bkb(bi), writes=[Bdst])
            return ev

        def proj_v():
            def ev_v(j, cbk, pt, bi, n):
                P.op("act", lambda e: e.copy(RV[:, j, cbk * 512:(cbk + 1) * 512], pt[:]), reads=bkb(bi), writes=[BRV])
                if kind == 0:
                    st, stb = stg2[(j * 2 + cbk) % 2]
                    P.op("dve", lambda e: e.tensor_copy(st[:, 0:512], pt[:]), reads=bkb(bi), writes=[stb])
                    P.dma("sp", nv_d[j // 2, l, (j % 2) * 128:(j % 2 + 1) * 128, cbk * 512:(cbk + 1) * 512], st[:, 0:512], reads=[stb])
            proj_tm(C_V, 1024, ev_v)

        def proj_k_tm_out():
            def ev_k(j, cbk, pt, bi, n):
                st, stb = stg2[(j * 2 + cbk) % 2]
                P.op("dve", lambda e: e.tensor_copy(st[:, 0:512], pt[:]), reads=bkb(bi), writes=[stb])
                P.dma("sp", nk_d[j // 2, l, (j % 2) * 128:(j % 2 + 1) * 128, cbk * 512:(cbk + 1) * 512], st[:, 0:512], reads=[stb])
            proj_tm(C_K, 1024, ev_k)

        if kind == 1:
            proj_fm(C_K, 1024, ev_rope(RK, BRK))
            proj_v()
            Bb = Bd[f"bkv{par}"]
            for cch in range(2):
                P.dma("sp", bkv[par][cch].rearrange("(h p) t -> p h t", p=128), RK[:, cch * 4:(cch + 1) * 4, :], reads=[BRK], writes=[Bb])
                P.dma("sp", bkv[par][2 + cch].rearrange("(j p) c -> p j c", p=128), RV[:, cch * 4:(cch + 1) * 4, :], reads=[BRV], writes=[Bb])
            for cch in range(4):
                P.cc("AllGather", GROUPS4, bkv[par][cch], gkv[par][cch], reads=[Bb], writes=[Bd[f"gkv{par}"]])

        ck(f"kvsend{l}{kind}")
        def ev_xbc(cb, nb, pt, bi):
            P.op("act", lambda e: e.copy(RAv[:, cb, nb * 2:nb * 2 + 2, 2:258], pt[:].rearrange("p (s t) -> p s t", s=2)),
                 reads=bkb(bi), writes=[BRA])
        proj_fm(C_XBC, 1280, ev_xbc)

        def ev_dt(j, cbk, pt, bi, n):
            P.op("dve", lambda e: e.tensor_tensor(small[:, 32:64], pt[:, 0:32], dtbrow[:], ALU.add), reads=bkb(bi) + [Bdtb], writes=[Bsmall])
            P.op("dve", lambda e: e.tensor_single_scalar(small[:, 0:32], small[:, 32:64], 30.0, ALU.min), reads=[Bsmall], writes=[Bsmall])
            P.op("act", lambda e: e.activation(small[:, 0:32], small[:, 0:32], AF.Exp), reads=[Bsmall], writes=[Bsmall])
            P.op("act", lambda e: e.activation(small[:, 0:32], small[:, 0:32], AF.Ln, bias=c_one), reads=[Bsmall, Bcst], writes=[Bsmall])
            P.op("dve", lambda e: e.tensor_tensor(dtt[:, j, :], small[:, 32:64], small[:, 0:32], ALU.max), reads=[Bsmall], writes=[Bdtt])
            P.op("dve", lambda e: e.tensor_tensor(at[:, j, :], dtt[:, j, :], arow[:], ALU.mult), reads=[Bdtt, Barow], writes=[Bat])
        proj_tm(C_DT, 32, ev_dt)

        ck(f"xbcdt{l}{kind}")
        if kind == 0:
            P.op("dve", lambda e: e.memset(RAv[:, :, :, 1:2], 0.0), writes=[BRA])
            P.op("dve", lambda e: e.memset(RAv[:, :, :, 258:259], 0.0), writes=[BRA])
        else:
            P.op("dve", lambda e: e.tensor_copy(RAv[:, :, 1:4, 1:2], RAv[:, :, 0:3, 257:258]), reads=[BRA], writes=[BRA])
            P.op("dve", lambda e: e.tensor_copy(RAv[:, :, 0:3, 258:259], RAv[:, :, 1:4, 2:3]), reads=[BRA], writes=[BRA])
            P.op("dve", lambda e: e.tensor_copy(halo[:, 0, 0:10], RAv[:, :, 0, 2]), reads=[BRA], writes=[Bhalo])
            P.op("dve", lambda e: e.tensor_copy(halo[:, 0, 10:20], RAv[:, :, 3, 257]), reads=[BRA], writes=[Bhalo])
            P.dma("sp", bh[par], halo[:, 0, :], reads=[Bhalo], writes=[Bd[f"bh{par}"]])
            P.cc("AllGather", GROUPS4, bh[par], gh[par], reads=[Bd[f"bh{par}"]], writes=[Bd[f"gh{par}"]])
            P.dma("sp", halo[:], gh[par].rearrange("(r p) c -> p r c", p=128), reads=[Bd[f"gh{par}"]], writes=[Bhalo])
            for side in range(2):
                for r in range(4):
                    src = halo[:, r, 10:20] if side == 0 else halo[:, r, 0:10]
                    selc = rkc[:, side * 4 + r: side * 4 + r + 1]
                    if r == 0:
                        P.op("dve", lambda e, src=src, selc=selc, side=side: e.tensor_single_scalar(halof[:, side, :], src, selc, ALU.mult),
                             reads=[Bhalo, Brkc], writes=[Bhalof])
                    else:
                        P.op("dve", lambda e, src=src, selc=selc, side=side: e.scalar_tensor_tensor(
                            halof[:, side, :], src, selc, halof[:, side, :], ALU.mult, ALU.add),
                            reads=[Bhalo, Brkc, Bhalof], writes=[Bhalof])
            P.op("dve", lambda e: e.tensor_copy(RAv[:, :, 0, 1], halof[:, 0, :]), reads=[Bhalof], writes=[BRA])
            P.op("dve", lambda e: e.tensor_copy(RAv[:, :, 3, 258], halof[:, 1, :]), reads=[Bhalof], writes=[BRA])

        ck(f"halo{l}{kind}")
        for kc in range(10):
            for hb in range(2):
                t1, t1b = tf[(kc * 2 + hb) % 2]
                t1v = t1[:].rearrange("p (s t) -> p s t", s=2)
                sl = slice(hb * 2, hb * 2 + 2)
                P.op("dve", lambda e, kc=kc, t1v=t1v, sl=sl: e.tensor_single_scalar(t1v, RAv[:, kc, sl, 1:257], cwT[:, kc, 0:1], ALU.mult),
                     reads=[BRA, BcwT], writes=[t1b])
                P.op("dve", lambda e, kc=kc, t1v=t1v, sl=sl: e.scalar_tensor_tensor(t1v, RAv[:, kc, sl, 2:258], cwT[:, kc, 1:2], t1v, ALU.mult, ALU.add),
                     reads=[BRA, BcwT, t1b], writes=[t1b])
                P.op("dve", lambda e, kc=kc, t1v=t1v, sl=sl: e.scalar_tensor_tensor(t1v, RAv[:, kc, sl, 3:259], cwT[:, kc, 2:3], t1v, ALU.mult, ALU.add),
                     reads=[BRA, BcwT, t1b], writes=[t1b])
                P.op("act", lambda e, kc=kc, t1v=t1v, sl=sl: e.activation(RAv[:, kc, sl, 2:258], t1v, AF.Silu, bias=cbT[:, kc:kc + 1]),
                     reads=[t1b, BcbT], writes=[BRA])

        def ssd_main():
            for j in range(8):
                xt, xtb = tmpb[0]
                xd, xdb = tmpb[1]
                bi = 3
                ptv = banks[bi][0][:].bitcast(BF16)
                P.mm([lambda e, kc=kc, ptv=ptv, j=j: e.transpose(ptv[:, kc * 128:(kc + 1) * 128], xc(kc, j), identb[:])
                      for kc in range(8)], reads=[BRA, Bidb], writes=bkb(bi))
                P.op("act", lambda e, ptv=ptv: e.copy(xt[:, 0, :], ptv), reads=bkb(bi), writes=[xtb])
                P.mm([lambda e, ptv=ptv, j=j: e.transpose(ptv[:, 0:128], xc(8, j), identb[:])],
                     reads=[BRA, Bidb], writes=bkb(bi))
                P.op("act", lambda e, ptv=ptv: e.copy(xt[:, 1, 0:128], ptv[:, 0:128]), reads=bkb(bi), writes=[xtb])
                ck2("ssd_T")
                pc = banks[2][0]
                aj = at[:, j, :]
                P.mm([lambda e, aj=aj: e.matmul(pc[:, 256:272], LE, aj[:, 0:16], start=True, stop=True),
                      lambda e, aj=aj: e.matmul(pc[:, 272:288], GE, aj[:, 16:32], start=True, stop=True),
                      lambda e, aj=aj: e.matmul(pc[:, 288:304], GT, aj[:, 0:16], start=True, stop=True),
                      lambda e, aj=aj: e.matmul(pc[:, 304:320], LT, aj[:, 16:32], start=True, stop=True),
                      lambda e, aj=aj: e.matmul(pc[:, 384:416], onesf, aj, start=True, stop=True)],
                     reads=[Bat, Bcst], writes=[bkb(2)[2], bkb(2)[3]])
                P.op("act", lambda e, j=j: e.activation(Et[:, j, :], pc[:, 256:320], AF.Exp), reads=[bkb(2)[2]], writes=[BEt])
                P.op("act", lambda e, j=j: e.activation(dch[:, j, :], pc[:, 384:416], AF.Exp), reads=[bkb(2)[3]], writes=[Bdch])
                P.op("dve", lambda e, j=j: e.tensor_copy(totl[:, j, :], pc[:, 384:416]), reads=[bkb(2)[3]], writes=[Btotl])
                ck2("ssd_cum")
                xtv = xt[:, 0, :].rearrange("p (h q) -> p h q", h=16)
                for d in range(2):
                    P.op("dve", lambda e, d=d, j=j: e.tensor_tensor(xd[:, d, :].rearrange("p (h q) -> p h q", h=16), xtv,
                                                                   dtt[:, j, d * 16:(d + 1) * 16].unsqueeze(2).to_broadcast([128, 16, 64]), ALU.mult),
                         reads=[xtb, Bdtt], writes=[xdb])
                for d in range(2):
                    P.op("dve", lambda e, d=d, j=j: e.tensor_tensor(
                        xeT[:, d, :].rearrange("p (h q) -> p h q", h=16), xd[:, d, :].rearrange("p (h q) -> p h q", h=16),
                        Et[:, j, 32 + d * 16: 48 + d * 16].unsqueeze(2).to_broadcast([128, 16, 64]), ALU.mult),
                        reads=[xdb, BEt], writes=[Bxe])
                ck2("ssd_xdt")
                pcg = [banks[2][0][:, 0:128], banks[3][0][:, 0:128]]
                P.mm([lambda e, g=g, j=j: e.matmul(pcg[g], xc(8, j, g * 64, (g + 1) * 64),
                                                  xc(9, j, g * 64, (g + 1) * 64), start=True, stop=True) for g in range(2)],
                     reads=[BRA], writes=[bkb(2)[0], bkb(3)[0]])
                cb_t, cb_b = cbm[0]
                for g in range(2):
                    for d in range(2):
                        P.op("dve", lambda e, g=g, d=d: e.tensor_tensor(cb_t[:, g * 2 + d, :], pcg[g], LE if d == 0 else GE, ALU.mult),
                             reads=[bkb(2 + g)[0], Bcst], writes=[cb_b])
                ck2("ssd_cb")
                ybank = (4, 5)
                idx = 0
                for h in range(16):
                    for d in range(2):
                        g = h // 8
                        lt_, lb_ = lhs[idx % 2]
                        ld_, ldb_ = ldec[idx % 2]
                        wm_, wmb_ = wmat[idx % 4]
                        dbi, dq = idx % 2, 0
                        pd = banks[dbi][0][:, dq * 128:(dq + 1) * 128]
                        P.op("dve", lambda e, lt_=lt_, d=d, h=h, j=j: e.tensor_single_scalar(lt_[:], GT if d == 0 else LT, at[:, j, d * 16 + h:d * 16 + h + 1], ALU.mult),
                             reads=[Bcst, Bat], writes=[lb_])
                        P.mm([lambda e, pd=pd, lt_=lt_, d=d: e.matmul(pd, lt_[:], LE if d == 0 else GE, start=True, stop=True)],
                             reads=[lb_, Bcst], writes=[bkb(dbi)[dq]])
                        P.op("act", lambda e, pd=pd, ld_=ld_: e.activation(ld_[:], pd, AF.Exp), reads=[bkb(dbi)[dq]], writes=[ldb_])
                        P.op("dve", lambda e, ld_=ld_, wm_=wm_, g=g, d=d: e.tensor_tensor(wm_[:], ld_[:], cb_t[:, g * 2 + d, :], ALU.mult),
                             reads=[ldb_, cb_b], writes=[wmb_])
                        yb = ybank[h // 8]
                        py = banks[yb][0][:, (h % 8) * 64:(h % 8 + 1) * 64]
                        P.mm([lambda e, py=py, wm_=wm_, d=d, h=h: e.matmul(py, wm_[:], xd[:, d, h * 64:(h + 1) * 64], start=(d == 0), stop=(d == 1))],
                             reads=[wmb_, xdb], writes=[bkb(yb)[(h % 8) // 2]])
                        idx += 1
                ck2("ssd_y")
                t1, t1b = tf[1]
                for half in range(2):
                    P.op("dve", lambda e, half=half: e.tensor_tensor(
                        t1[:].rearrange("p (h q) -> p h q", h=8), xt[:, 0, half * 512:(half + 1) * 512].rearrange("p (h q) -> p h q", h=8),
                        drow[:, 32 + half * 8: 40 + half * 8].unsqueeze(2).to_broadcast([128, 8, 64]), ALU.mult),
                        reads=[xtb, Bdrow], writes=[t1b])
                    P.op("dve", lambda e, half=half, j=j: e.tensor_tensor(RY[:, j, half * 512:(half + 1) * 512], banks[4 + half][0][:], t1[:], ALU.add),
                         reads=[t1b] + bkb(4 + half), writes=[BRY])
                ck2("ssd_dskip")
                for d in range(2):
                    for g in range(2):
                        sbk = 6 + g
                        ps_ = banks[sbk][0]
                        P.mm([lambda e, ps_=ps_, d=d, g=g: e.matmul(ps_[:], xt[:, 1, 0:128], xeT[:, d, g * 512:(g + 1) * 512], start=True, stop=True)],
                             reads=[xtb, Bxe], writes=bkb(sbk))
                        dstS = SFb if d == 0 else RSb
                        P.op("act", lambda e, ps_=ps_, g=g, j=j, dstS=dstS: e.copy(dstS[g * 64:(g + 1) * 64, j, :], ps_[g * 64:(g + 1) * 64, :]),
                             reads=bkb(sbk), writes=[BRK])
                ck2("ssd_S")

        def chain(d, init_ap, init_bufs, addS, finals):
            Ssrc = SFb if d == 0 else RSb
            ecol = 0 if d == 0 else 16
            for s in range(nseq):
                chunks = list(range(s * cps, (s + 1) * cps))
                if d == 1:
                    chunks = chunks[::-1]
                have = False
                for ci, j in enumerate(chunks):
                    if ci == 0 and init_ap is not None:
                        P.op("dve", lambda e: e.tensor_copy(state[:, d, :], init_ap), reads=init_bufs, writes=[Bstate])
                        have = True
                    if have:
                        P.op("act", lambda e: e.copy(stateb[:, d, :], state[:, d, :]), reads=[Bstate], writes=[Bstateb])
                        for g in range(2):
                            bo = 6 + g
                            po = banks[bo][0]
                            P.mm([lambda e, po=po, g=g, j=j: e.matmul(po[:], xc(9, j, g * 64, (g + 1) * 64),
                                                                     stateb[g * 64:(g + 1) * 64, d, :], start=True, stop=True)],
                                 reads=[BRA, Bstateb], writes=bkb(bo))
                            t1, t1b = tf[2 + g]
                            P.op("dve", lambda e, po=po, g=g, j=j, t1=t1: e.tensor_tensor(
                                t1[:].rearrange("p (h q) -> p h q", h=8), po[:].rearrange("p (h q) -> p h q", h=8),
                                Et[:, j, ecol + g * 8: ecol + g * 8 + 8].unsqueeze(2).to_broadcast([128, 8, 64]), ALU.mult),
                                reads=bkb(bo) + [BEt], writes=[t1b])
                            P.op("dve", lambda e, g=g, j=j, t1=t1: e.tensor_tensor(RY[:, j, g * 512:(g + 1) * 512], RY[:, j, g * 512:(g + 1) * 512], t1[:], ALU.add),
                                 reads=[t1b, BRY], writes=[BRY])
                        for g in range(2):
                            P.op("dve", lambda e, g=g, j=j: e.tensor_tensor(
                                state[g * 64:(g + 1) * 64, d, :].rearrange("p (h q) -> p h q", h=8),
                                state[g * 64:(g + 1) * 64, d, :].rearrange("p (h q) -> p h q", h=8),
                                dch[g * 64:(g + 1) * 64, j, d * 16 + g * 8: d * 16 + g * 8 + 8].unsqueeze(2).to_broadcast([64, 8, 64]), ALU.mult),
                                reads=[Bstate, Bdch], writes=[Bstate])
                        if addS:
                            P.op("dve", lambda e, j=j: e.tensor_tensor(state[:, d, :], state[:, d, :], Ssrc[:, j, :], ALU.add),
                                 reads=[Bstate, BRK], writes=[Bstate])
                    else:
                        P.op("dve", lambda e, j=j: e.tensor_copy(state[:, d, :], Ssrc[:, j, :]), reads=[BRK], writes=[Bstate])
                        have = True
                if finals is not None:
                    finals(s, d)

        def prompt_final(s, d):
            st, stb = stg2[(s * 2 + d) % 2]
            for blk in range(4):
                bi = nextbank(0, 4)
                pt = banks[bi][0]
                P.mm([lambda e, pt=pt, blk=blk: e.transpose(pt[:, 0:128], state[:, d, blk * 128:(blk + 1) * 128], identf)],
                     reads=[Bstate, Bcst], writes=[bkb(bi)[0]])
                P.op("act", lambda e, pt=pt, blk=blk, st=st: e.copy(st[:, blk * 128:(blk + 1) * 128], pt[:, 0:128]), reads=[bkb(bi)[0]], writes=[stb])
            for g in range(2):
                dst = nst_d[s, l, d, g * 8:(g + 1) * 8].rearrange("(b h2) p n -> (h2 p) b n", b=4)
                P.dma("sp", dst, st[:, 0:512].rearrange("p (b gn) -> p b gn", b=4)[:, :, g * 64:(g + 1) * 64], reads=[stb])

        def gate_norm():
            w0 = wload(win_d[l][:, C_Z:C_Z + 512], 512)
            w1 = wload(win_d[l][:, C_Z + 512:C_Z + 1024], 512)
            P.dma("sp", rowA[:], mnw_d[l:l + 1, :].partition_broadcast(128), writes=[BrowA])
            for j in range(8):
                st, stb = stg2[j % 2]
                for half, (wt, wbf) in enumerate((w0, w1)):
                    bi = nextbank()
                    pt = banks[bi][0]
                    P.mm([lambda e, kc=kc, pt=pt, wt=wt, j=j: e.matmul(pt[:], hT[:, kc, j * 128:(j + 1) * 128], wt[:, kc, :],
                                                                       start=(kc == 0), stop=(kc == 7)) for kc in range(8)],
                         reads=[wbf, BhT[j]], writes=bkb(bi))
                    t1, t1b = tf[half]
                    P.op("act", lambda e, pt=pt, t1=t1: e.activation(t1[:], pt[:], AF.Silu), reads=bkb(bi), writes=[t1b])
                    P.op("dve", lambda e, t1=t1, st=st, half=half, j=j: e.tensor_tensor(st[:, half * 512:(half + 1) * 512], t1[:], RY[:, j, half * 512:(half + 1) * 512], ALU.mult),
                         reads=[t1b, BRY], writes=[stb])
                P.op("act", lambda e, st=st: e.activation(stg[:, 1, :], st[:], AF.Square, accum_out=small[:, 0:1]), reads=[stb], writes=[Bstg, Bsmall])
                rstd_from(small[:, 0:1], 1024, small[:, 1:2], [Bsmall], Bsmall)
                gn, gnb = tmpb[j % 2]
                P.op("dve", lambda e, st=st, gn=gn: e.scalar_tensor_tensor(gn[:, 0, :], st[:], small[:, 1:2], rowA[:], ALU.mult, ALU.mult),
                     reads=[stb, Bsmall, BrowA], writes=[gnb])
                bi = nextbank(0, 4)
                ptv = banks[bi][0][:].bitcast(BF16)
                P.mm([lambda e, kc=kc, gn=gn, ptv=ptv: e.transpose(ptv[:, kc * 128:(kc + 1) * 128], gn[:, 0, kc * 128:(kc + 1) * 128], identb[:])
                      for kc in range(8)], reads=[gnb, Bidb], writes=bkb(bi))
                P.op("act", lambda e, ptv=ptv, j=j: e.copy(YA[:, :, j * 128:(j + 1) * 128], ptv.rearrange("p (k t) -> p k t", k=8)),
                     reads=bkb(bi), writes=[BRV])

        def merge(b, ysrc, Bys, first):
            for wblk in range(2):
                wg, wgb = wload(win_d[l][:, C_MG + b * 1024 + wblk * 512: C_MG + b * 1024 + (wblk + 1) * 512], 512)
                wr, wrb = wload(wbr_d[l, b][:, wblk * 512:(wblk + 1) * 512], 512)
                for s in range(4):
                    fo = wblk * 4 + s
                    for nb in range(2):
                        bi = nextbank()
                        pt = banks[bi][0]
                        P.mm([lambda e, kc=kc, s=s, nb=nb, pt=pt: e.matmul(
                            pt[:], wg[:, kc, s * 128:(s + 1) * 128], hT[:, kc, nb * 512:(nb + 1) * 512],
                            start=(kc == 0), stop=(kc == 7)) for kc in range(8)], reads=[wgb] + BhT[nb * 4:(nb + 1) * 4], writes=bkb(bi))
                        gt, gtb = tf[2 + (fo + nb) % 2]
                        P.op("act", lambda e, pt=pt, gt=gt: e.activation(gt[:], pt[:], AF.Sigmoid), reads=bkb(bi), writes=[gtb])
                        b2 = nextbank()
                        p2 = banks[b2][0]
                        P.mm([lambda e, kc=kc, s=s, nb=nb, p2=p2: e.matmul(
                            p2[:], wr[:, kc, s * 128:(s + 1) * 128], ysrc[:, kc, nb * 512:(nb + 1) * 512],
                            start=(kc == 0), stop=(kc == 7)) for kc in range(8)], reads=[wrb, Bys], writes=bkb(b2))
                        mdst = MG[:, fo, nb * 512:(nb + 1) * 512]
                        if first:
                            P.op("dve", lambda e, p2=p2, gt=gt, mdst=mdst: e.tensor_tensor(mdst, p2[:], gt[:], ALU.mult),
                                 reads=bkb(b2) + [gtb], writes=[BRY])
                        else:
                            P.op("dve", lambda e, p2=p2, gt=gt: e.tensor_tensor(gt[:], p2[:], gt[:], ALU.mult),
                                 reads=bkb(b2) + [gtb], writes=[gtb])
                            P.op("dve", lambda e, gt=gt, mdst=mdst: e.tensor_tensor(mdst, mdst, gt[:], ALU.add), reads=[gtb, BRY], writes=[BRY])

        def attention():
            proj_fm(C_Q, 1024, ev_rope(RQ, BRA) if kind == 1 else ev_copy(RQ, BRA))
            if kind == 0:
                proj_fm(C_K, 1024, ev_copy(RK, BRK))
                proj_k_tm_out()
                proj_v()
            proj_fm(C_GB, 1024, ev_copy(RG, BRG, AF.Silu))
            if kind == 1:
                for jt in range(2):
                    P.dma("sp", stg[:, jt, :], ck_d[l, jt * 128:(jt + 1) * 128, :], writes=[Bstg])
                for h in range(8):
                    for jt in range(2):
                        bi = nextbank()
                        pt = banks[bi][0]
                        P.mm([lambda e, h=h, jt=jt, pt=pt: e.transpose(pt[:, 0:128], stg[:, jt, h * 128:(h + 1) * 128], identf)],
                             reads=[Bstg, Bcst], writes=[bkb(bi)[0]])
                        P.op("act", lambda e, h=h, jt=jt, pt=pt: e.copy(ctk_ap(h, jt * 128, (jt + 1) * 128), pt[:, 0:128]),
                             reads=[bkb(bi)[0]], writes=[BRA])
                P.dma("pool", ctv[:], cv_d[l].rearrange("(j p) c -> p j c", p=128), writes=[Bctv])
            scale = 64 ** -0.5
            qblocks = [(s * 256, 256) for s in range(4)] if kind == 0 else [(0, 512), (512, 512)]
            it = 0
            for h in range(8):
                if kind == 1:
                    slab, slabB = (RK, BRK) if h % 2 == 0 else (RV, BRV)
                    kt = slab[:, 0:4, :].rearrange("p a t -> p (a t)")
                    vt = slab[:, 4:8, :].rearrange("p a (j d) -> p (a j) d", d=128)
                    g4 = gkv[par][h // 4].rearrange("(r x) t -> x r t", r=4)
                    P.dma("sp", slab[:, 0:4, :], g4[(h % 4) * 128:(h % 4 + 1) * 128, :, :],
                          reads=[Bd[f"gkv{par}"]], writes=[slabB])
                    for half in range(2):
                        gv = gkv[par][2 + half].rearrange("(r j p) c -> p r j c", r=4, j=4, p=128)
                        for r in range(4):
                            P.dma("sp", slab[:, 4 + r, half * 512:(half + 1) * 512].rearrange("p (j d) -> p j d", d=128),
                                  gv[:, r, :, h * 128:(h + 1) * 128], reads=[Bd[f"gkv{par}"]], writes=[slabB])
                for (q0, nq) in qblocks:
                    if kind == 0:
                        s = q0 // 256
                        ktiles = [(RK[:, h, s * 256 + jt * 128: s * 256 + (jt + 1) * 128], [BRK],
                                   RV[:, s * 2 + jt, h * 128:(h + 1) * 128], [BRV]) for jt in range(2)]
                    else:
                        ktiles = [(ctk_ap(h, jt * 128, (jt + 1) * 128), [BRA], ctv[:, jt, h * 128:(h + 1) * 128], [Bctv]) for jt in range(2)]
                        ktiles += [(kt[:, jt * 128:(jt + 1) * 128], [slabB], vt[:, jt, :], [slabB]) for jt in range(32)]
                    nkt = len(ktiles)
                    acc = [4, 5, 6, 7]
                    for ti, (kap, kbufs, vap, vbufs) in enumerate(ktiles):
                        sb0 = (it % 2) * 2
                        pb, pbb = ptb[it % 2]
                        it += 1
                        for m in range(2):
                            ps = banks[sb0 + m][0]
                            P.mm([lambda e, m=m, ps=ps, kap=kap: e.matmul(ps[:, 0:nq], kap[m * 64:(m + 1) * 64, :],
                                                                        RQ[m * 64:(m + 1) * 64, h, q0:q0 + nq], start=True, stop=True)],
                                 reads=kbufs + [BRA], writes=bkb(sb0 + m))
                            P.op("act", lambda e, m=m, ps=ps, pb=pb: e.activation(pb[:, m, 0:nq], ps[:, 0:nq], AF.Exp, scale=scale),
                                 reads=bkb(sb0 + m), writes=[pbb])
                        for m in range(2):
                            po = banks[acc[m]][0]
                            psm = banks[acc[2 + m]][0]
                            P.mm([lambda e, m=m, po=po, pb=pb, vap=vap, ti=ti: e.matmul(po[:, 0:nq], vap, pb[:, m, 0:nq], start=(ti == 0), stop=(ti == nkt - 1)),
                                  lambda e, m=m, psm=psm, pb=pb, ti=ti: e.matmul(psm[:, 0:nq], onesb[:], pb[:, m, 0:nq], start=(ti == 0), stop=(ti == nkt - 1))],
                                 reads=vbufs + [pbb, Bonesb], writes=bkb(acc[m]) + bkb(acc[2 + m]))
                    r1, r1b = tf[2]
                    r2, r2b = tf[3]
                    o1, o1b = tf[0]
                    o2, o2b = tf[1]
                    P.op("dve", lambda e: e.reciprocal(r1[:, 0:nq], banks[6][0][:, 0:nq]), reads=bkb(6), writes=[r1b])
                    P.op("dve", lambda e: e.reciprocal(r2[:, 0:nq], banks[7][0][:, 0:nq]), reads=bkb(7), writes=[r2b])
                    P.op("dve", lambda e: e.tensor_tensor(o1[:, 0:nq], banks[4][0][:, 0:nq], r1[:, 0:nq], ALU.mult), reads=bkb(4) + [r1b], writes=[o1b])
                    P.op("dve", lambda e: e.tensor_tensor(o2[:, 0:nq], banks[5][0][:, 0:nq], r2[:, 0:nq], ALU.mult), reads=bkb(5) + [r2b], writes=[o2b])
                    P.op("dve", lambda e: e.scalar_tensor_tensor(o1[:, 0:nq], o2[:, 0:nq], lamt[:, 261:262], o1[:, 0:nq], ALU.mult, ALU.add),
                         reads=[o1b, o2b, Blamt], writes=[o1b])
                    sq, sqb = tmpb[0]
                    P.op("act", lambda e: e.activation(sq[:, 0, 0:nq], o1[:, 0:nq], AF.Square), reads=[o1b], writes=[sqb])
                    P.mm([lambda e: e.matmul(banks[6][0][:, 0:nq], onesb[:], sq[:, 0, 0:nq], start=True, stop=True)],
                         reads=[sqb, Bonesb], writes=bkb(6))
                    P.op("act", lambda e: e.activation(r1[:, 0:nq], banks[6][0][:, 0:nq], AF.Sqrt, bias=c_eps, scale=1.0 / 128),
                         reads=bkb(6) + [Bcst], writes=[r1b])
                    P.op("dve", lambda e: e.reciprocal(r1[:, 0:nq], r1[:, 0:nq]), reads=[r1b], writes=[r1b])
                    P.op("dve", lambda e: e.scalar_tensor_tensor(o1[:, 0:nq], o1[:, 0:nq], hwT[:, 1:2], r1[:, 0:nq], ALU.mult, ALU.mult),
                         reads=[o1b, BhwT, r1b], writes=[o1b])
                    P.op("dve", lambda e, h=h, q0=q0, nq=nq: e.tensor_tensor(RG[:, h, q0:q0 + nq], o1[:, 0:nq], RG[:, h, q0:q0 + nq], ALU.mult),
                         reads=[o1b, BRG], writes=[BRG])

        def branch_c():
            proj_fm(C_U, 1024, ev_copy(RQ, BRA, AF.Gelu))
            proj_fm(C_GC, 1024, ev_copy(RK, BRK, AF.Silu))

            def ev_sv(j, cbk, pt, bi, n):
                P.op("act", lambda e: e.activation(RV[:, j, cbk * 512:(cbk + 1) * 512], pt[:], AF.Gelu), reads=bkb(bi), writes=[BRV])
            proj_tm(C_SV, 1024, ev_sv)
            P.dma("sp", rowA[:], vnw_d[l:l + 1, :].partition_broadcast(128), writes=[BrowA])
            P.dma("sp", rowB[:], sgb_d[l:l + 1, :].partition_broadcast(128), writes=[BrowB])
            for j in range(8):
                st, stb = stg2[j % 2]
                P.op("act", lambda e, st=st, j=j: e.activation(st[:], RV[:, j, :], AF.Identity, accum_out=small[:, 0:1]), reads=[BRV], writes=[stb, Bsmall])
                P.op("act", lambda e, st=st, j=j: e.activation(st[:], RV[:, j, :], AF.Square, accum_out=small[:, 1:2]), reads=[BRV], writes=[stb, Bsmall])
                P.op("dve", lambda e: e.tensor_single_scalar(small[:, 2:3], small[:, 0:1], 1.0 / 1024, ALU.mult), reads=[Bsmall], writes=[Bsmall])
                P.op("dve", lambda e: e.tensor_tensor(small[:, 3:4], small[:, 2:3], small[:, 2:3], ALU.mult), reads=[Bsmall], writes=[Bsmall])
                P.op("dve", lambda e: e.scalar_tensor_tensor(small[:, 4:5], small[:, 1:2], 1.0 / 1024, small[:, 3:4], ALU.mult, ALU.subtract),
                     reads=[Bsmall], writes=[Bsmall])
                P.op("act", lambda e: e.activation(small[:, 5:6], small[:, 4:5], AF.Sqrt, bias=c_eps, scale=1.0), reads=[Bsmall, Bcst], writes=[Bsmall])
                P.op("dve", lambda e: e.reciprocal(small[:, 5:6], small[:, 5:6]), reads=[Bsmall], writes=[Bsmall])
                P.op("dve", lambda e, st=st, j=j: e.tensor_scalar(st[:], RV[:, j, :], small[:, 2:3], small[:, 5:6], ALU.subtract, ALU.mult),
                     reads=[BRV, Bsmall], writes=[stb])
                vn, vnb = tmpb[j % 2]
                P.op("dve", lambda e, vn=vn, st=st: e.tensor_tensor(vn[:, 0, :], st[:], rowA[:], ALU.mult), reads=[stb, BrowA], writes=[vnb])
                b0 = (nextbank(0, 4) // 2) * 2
                for half in range(2):
                    pt = banks[b0 + half][0]
                    P.mm([lambda e, g=g, pt=pt, vn=vn: e.matmul(pt[:, (g % 4) * 128:(g % 4 + 1) * 128], vn[:, 0, g * 128:(g + 1) * 128], wsT[:, g, :],
                                                             start=True, stop=True) for g in range(half * 4, half * 4 + 4)],
                         reads=[vnb, BwsT], writes=bkb(b0 + half))
                    t1, t1b = tf[half]
                    t1v = t1[:].rearrange("p (g t) -> p g t", g=4)
                    P.op("dve", lambda e, pt=pt, t1v=t1v, half=half: e.tensor_tensor(
                        t1v, pt[:].rearrange("p (g t) -> p g t", g=4), rowB[:, half * 512:(half + 1) * 512].rearrange("p (g t) -> p g t", g=4), ALU.add),
                        reads=bkb(b0 + half) + [BrowB], writes=[t1b])
                    dst = RQ[:, half * 4:half * 4 + 4, j * 128:(j + 1) * 128]
                    P.op("dve", lambda e, t1v=t1v, dst=dst: e.tensor_tensor(t1v, t1v, dst, ALU.mult), reads=[t1b, BRA], writes=[t1b])
                    P.op("dve", lambda e, t1v=t1v, dst=dst, half=half, j=j: e.tensor_tensor(dst, t1v, RK[:, half * 4:half * 4 + 4, j * 128:(j + 1) * 128], ALU.mult),
                         reads=[t1b, BRK], writes=[BRA])

        def out_proj():
            P.dma("sp", rowB[:], bmod_d[l:l + 1, 2048:3072].partition_broadcast(128), writes=[BrowB])
            P.dma("sp", rowA[:], post_d[l:l + 1, :].partition_broadcast(128), writes=[BrowA])
            for nb in range(2):
                wt, wbf = wload(wmod_d[l][:, 2048 + nb * 512: 2048 + (nb + 1) * 512], 512)
                bi = nextbank()
                pt = banks[bi][0]
                P.mm([lambda e, kc=kc, pt=pt, wt=wt: e.matmul(pt[:], scRep[:, c, kc, :], wt[:, kc, :],
                                                              start=(kc == 0), stop=(kc == 7)) for kc in range(8)],
                     reads=[wbf, BscRep], writes=bkb(bi))
                P.op("dve", lambda e, nb=nb, pt=pt: e.tensor_tensor(gpw[:, nb * 512:(nb + 1) * 512], pt[:], rowB[:, nb * 512:(nb + 1) * 512], ALU.add),
                     reads=bkb(bi) + [BrowB], writes=[Bgpw])
            P.op("dve", lambda e: e.tensor_tensor(gpw[:], gpw[:], rowA[:], ALU.mult), reads=[Bgpw, BrowA], writes=[Bgpw])
            w0 = wload(wout_d[l][:, 0:512], 512)
            w1 = wload(wout_d[l][:, 512:1024], 512)
            for j in range(8):
                bis = []
                for half, (wt, wbf) in enumerate((w0, w1)):
                    bi = nextbank()
                    pt = banks[bi][0]
                    P.mm([lambda e, kc=kc, pt=pt, wt=wt, j=j: e.matmul(pt[:], MG[:, kc, j * 128:(j + 1) * 128], wt[:, kc, :],
                                                                       start=(kc == 0), stop=(kc == 7)) for kc in range(8)],
                         reads=[wbf, BRY], writes=bkb(bi))
                    bis.append(bi)
                    P.op("act", lambda e, pt=pt, half=half: e.activation(stg[:, 1, half * 512:(half + 1) * 512], pt[:], AF.Square,
                                                                          accum_out=small[:, 8 + half:9 + half]),
                         reads=bkb(bi), writes=[Bstg, Bsmall])
                P.op("dve", lambda e: e.tensor_tensor(small[:, 0:1], small[:, 8:9], small[:, 9:10], ALU.add), reads=[Bsmall], writes=[Bsmall])
                rstd_from(small[:, 0:1], 1024, small[:, 1:2], [Bsmall], Bsmall)
                st, stb = stg2[j % 2]
                P.dma("sp", stg[:, 0, :], xsrc[j * 128:(j + 1) * 128, :], reads=[Bxres] if l > 0 else [], writes=[Bstg])
                for half in range(2):
                    pt = banks[bis[half]][0]
                    P.op("dve", lambda e, pt=pt, half=half, st=st: e.scalar_tensor_tensor(
                        st[:, half * 512:(half + 1) * 512], pt[:], small[:, 1:2], gpw[:, half * 512:(half + 1) * 512], ALU.mult, ALU.mult),
                        reads=bkb(bis[half]) + [Bsmall, Bgpw], writes=[stb])
                P.op("dve", lambda e, st=st: e.tensor_tensor(st[:], st[:], stg[:, 0, :], ALU.add), reads=[stb, Bstg], writes=[stb])
                P.dma("sp", xdst[j * 128:(j + 1) * 128, :], st[:], reads=[stb], writes=[Bxres] if l < DEPTH - 1 else [])

        ck(f"conv{l}{kind}")
        ssd_main()
        ck(f"ssd_main{l}{kind}")
        if kind == 0:
            chain(0, None, [], True, prompt_final)
            chain(1, None, [], True, prompt_final)
        else:
            Bb = Bd[f"bst{par}"]
            chain(0, None, [], True, None)
            P.dma("sp", bst[par][:, 0:512], state[:, 0, :], reads=[Bstate], writes=[Bb])
            chain(1, None, [], True, None)
            P.dma("sp", bst[par][:, 512:1024], state[:, 1, :], reads=[Bstate], writes=[Bb])
            P.op("dve", lambda e: e.reduce_sum(small[:, 0:32], totl[:].rearrange("p j c -> p c j"), axis=AX.X), reads=[Btotl], writes=[Bsmall])
            P.dma("sp", bst[par][:, 1024:1056], small[:, 0:32], reads=[Bsmall], writes=[Bb])
            P.cc("AllGather", GROUPS4, bst[par], gst[par], reads=[Bb], writes=[Bd[f"gst{par}"]])
            for d in range(2):
                for g in range(2):
                    P.dma("sp", stg[:, d, 0:512].rearrange("p (b gn) -> p b gn", b=4)[:, :, g * 64:(g + 1) * 64],
                          sst_d[l, d, g * 8:(g + 1) * 8].rearrange("(b h2) p n -> (h2 p) b n", b=4), writes=[Bstg])
                for blk in range(4):
                    bi = nextbank(0, 4)
                    pt = banks[bi][0]
                    P.mm([lambda e, pt=pt, blk=blk, d=d: e.transpose(pt[:, 0:128], stg[:, d, blk * 128:(blk + 1) * 128], identf)],
                         reads=[Bstg, Bcst], writes=[bkb(bi)[0]])
                    P.op("act", lambda e, pt=pt, blk=blk, d=d: e.copy(hin[:, d, blk * 128:(blk + 1) * 128], pt[:, 0:128]),
                         reads=[bkb(bi)[0]], writes=[Bhin])
            for d in range(2):
                order = [0, 1, 2] if d == 0 else [3, 2, 1]
                for i in order:
                    ucol = rkc[:, 8 + d * 4 + i: 9 + d * 4 + i]
                    P.dma("sp", gall[:], gst[par][i * 128:(i + 1) * 128, :], reads=[Bd[f"gst{par}"]], writes=[Bgall])
                    P.op("act", lambda e, d=d: e.activation(small[:, 0:16], gall[:, 1024 + d * 16: 1040 + d * 16], AF.Exp),
                         reads=[Bgall], writes=[Bsmall])
                    P.op("dve", lambda e, ucol=ucol: e.tensor_scalar(small[:, 0:16], small[:, 0:16], -1.0, ucol, ALU.add, ALU.mult),
                         reads=[Bsmall, Brkc], writes=[Bsmall])
                    P.op("dve", lambda e: e.tensor_single_scalar(small[:, 0:16], small[:, 0:16], 1.0, ALU.add), reads=[Bsmall], writes=[Bsmall])
                    for g in range(2):
                        P.op("dve", lambda e, g=g, d=d: e.tensor_tensor(
                            hin[g * 64:(g + 1) * 64, d, :].rearrange("p (h q) -> p h q", h=8),
                            hin[g * 64:(g + 1) * 64, d, :].rearrange("p (h q) -> p h q", h=8),
                            small[g * 64:(g + 1) * 64, g * 8:g * 8 + 8].unsqueeze(2).to_broadcast([64, 8, 64]), ALU.mult),
                            reads=[Bhin, Bsmall], writes=[Bhin])
                    P.op("dve", lambda e, d=d, ucol=ucol: e.scalar_tensor_tensor(
                        hin[:, d, :], gall[:, d * 512:(d + 1) * 512], ucol, hin[:, d, :], ALU.mult, ALU.add),
                        reads=[Bgall, Brkc, Bhin], writes=[Bhin])
            chain(0, hin[:, 0, :], [Bhin], False, None)
            chain(1, hin[:, 1, :], [Bhin], False, None)
        if l == 0:
            dbg(f"y{kind}", RY[:], [BRY], [128, 8, 1024])
        ck(f"chains{l}{kind}")
        gate_norm()
        if l == 0:
            dbg(f"ya{kind}", YA[:], [BRV], [128, 8, 1024])
        ck(f"gate_norm{l}{kind}")
        merge(0, YA, BRV, True)
        ck(f"merge0{l}{kind}")
        attention()
        if l == 0:
            dbg(f"yb{kind}", RG[:], [BRG], [128, 8, 1024])
        ck(f"attn{l}{kind}")
        merge(1, RG, BRG, False)
        branch_c()
        if l == 0:
            dbg(f"yc{kind}", RQ[:, 0:8, 0:1024], [BRA], [128, 8, 1024])
        ck(f"brc{l}{kind}")
        merge(2, RQ, BRA, False)
        if l == 0:
            dbg(f"mg{kind}", MG[:], [BRY], [128, 8, 1024])
        out_proj()
        ck(f"out{l}{kind}")

    try:
        for l in range(DEPTH):
            layer_prep(l)
            ck(f"prep{l}")
            run_group(l, 0)
            run_group(l, 1)
    except _Stop:
        pass

    P.emit()
    P.stack.close()
    return nc


_NC = None
_STOP = int(os.environ["KSTOP"]) if os.environ.get("KSTOP") else None


def _consts():
    r = np.arange(128)
    c = np.zeros((128, 1024), np.float32)
    c[:, 0:128] = np.eye(128)
    c[:, 128:256] = (r[:, None] <= r[None, :])
    c[:, 256:384] = (r[:, None] >= r[None, :])
    c[:, 384:512] = (r[:, None] > r[None, :])
    c[:, 512:640] = (r[:, None] < r[None, :])
    Rm = np.zeros((128, 128), np.float32)
    for m in range(2):
        for j in range(32):
            Rm[m * 64 + j, m * 64 + 32 + j] = -1.0
            Rm[m * 64 + 32 + j, m * 64 + j] = 1.0
    c[:, 640:768] = Rm.T
    c[:, 768:896] = 1.0
    c[:, 896] = 1.0
    c[:, 897] = EPS
    c[:, 898] = 0.0
    return c


def _rope_tables():
    T_ = 4096
    rows = np.repeat(np.arange(T_ // 64, dtype=np.float32), 64)
    cols = np.tile(np.arange(64, dtype=np.float32), T_ // 64)
    inv = (10000.0 ** (-np.arange(16, dtype=np.float32) / 16)).astype(np.float32)
    ang = np.concatenate([rows[:, None] * inv, cols[:, None] * inv], -1).astype(np.float32)
    return np.cos(ang).astype(np.float32), np.sin(ang).astype(np.float32)


def kernel(**inp):
    global _NC
    if _NC is None:
        _NC = build(_STOP)
    nc = _NC
    f = lambda a: np.ascontiguousarray(np.asarray(a, dtype=np.float32))
    cst = _consts()
    cos, sin = _rope_tables()
    shared = {
        "pre_norm_w": f(inp["pre_norm_w"]), "post_norm_w": f(inp["post_norm_w"]),
        "w_mod": f(inp["w_mod"]), "b_mod": f(inp["b_mod"]), "w_in": f(inp["w_in"]),
        "m_conv_w": f(inp["m_conv_w"]), "m_conv_b": f(inp["m_conv_b"]),
        "m_A_log": f(inp["m_A_log"]).reshape(DEPTH, 32), "m_dt_bias": f(inp["m_dt_bias"]).reshape(DEPTH, 32),
        "m_D": f(inp["m_D"]).reshape(DEPTH, 32), "m_norm_w": f(inp["m_norm_w"]),
        "da_lambda": f(inp["da_lambda"]).reshape(DEPTH, 256), "da_head_norm_w": f(inp["da_head_norm_w"]),
        "sg_vnorm_w": f(inp["sg_vnorm_w"]), "sg_spatial_w": f(inp["sg_spatial_w"]),
        "sg_spatial_b": f(inp["sg_spatial_b"]).reshape(DEPTH, 1024),
        "w_branch": f(inp["w_branch"]), "w_out": f(inp["w_out"]), "cst": cst,
    }
    xp = f(inp["x_prompt"]); xs = f(inp["x_sample"])
    ck = f(inp["cache_k"]).reshape(2, DEPTH, 256, D); cv = f(inp["cache_v"]).reshape(2, DEPTH, 256, D)
    sst = f(inp["state_ssm"]); cc_ = f(inp["c"]); cctx = f(inp["c_ctx"])
    in_maps = []
    for core in range(8):
        b, j = core // 4, core % 4
        rk = np.zeros((128, 16), np.float32)
        if j > 0:
            rk[:, j - 1] = 1.0
        if j < 3:
            rk[:, 4 + j + 1] = 1.0
        for i in range(4):
            rk[:, 8 + i] = 1.0 if i < j else 0.0
            rk[:, 12 + i] = 1.0 if i > j else 0.0
        t0 = j * 1024
        ct = np.tile(cos[t0:t0 + 1024].T, (4, 1))
        stb = np.tile(sin[t0:t0 + 1024].T, (4, 1))
        m = dict(shared)
        m.update({
            "xp": np.ascontiguousarray(xp[core * 4:(core + 1) * 4].reshape(1024, D)),
            "xs": np.ascontiguousarray(xs[b, t0:t0 + 1024]),
            "ck": np.ascontiguousarray(ck[b]), "cv": np.ascontiguousarray(cv[b]),
            "sst": np.ascontiguousarray(sst[b]),
            "cvec": np.ascontiguousarray(np.stack([cctx, cc_[b]], 0)),
            "rkc": rk, "ropec": np.ascontiguousarray(np.stack([ct, stb], 1)),
        })
        in_maps.append(m)
    res = run_bass_kernel_spmd(nc, in_maps, core_ids=list(range(8)))
    R = res.results
    yp = np.concatenate([R[i]["yp"].reshape(4, 256, D) for i in range(8)], 0)
    ys = np.stack([np.concatenate([R[b * 4 + j]["ys"] for j in range(4)], 0) for b in range(2)], 0)
    nk = np.concatenate([R[i]["nk"] for i in range(8)], 0).reshape(32, DEPTH, 256, 8, 128)
    nv = np.concatenate([R[i]["nv"] for i in range(8)], 0).reshape(32, DEPTH, 256, 8, 128)
    nst = np.concatenate([R[i]["nst"] for i in range(8)], 0)
    return (yp.astype(np.float32), ys.astype(np.float32), nk.astype(np.float32), nv.astype(np.float32), nst.astype(np.float32))
```

```python
import contextlib
import math
import os
import numpy as np
import concourse.bass as bass
import concourse.mybir as mybir
from concourse.bass_utils import run_bass_kernel_spmd

F32 = mybir.dt.float32
BF16 = mybir.dt.bfloat16
AF = mybir.ActivationFunctionType
ALU = mybir.AluOpType
AX = mybir.AxisListType

ENGS = ("pe", "act", "dve", "pool", "sp")
NLANES = 32
DEPTH = 4
D = 1024
NIN = 12576
EPS = 1e-6
C_Z, C_XBC, C_DT, C_Q, C_K, C_V, C_GB, C_U, C_SV, C_GC, C_MG = (
    0, 1024, 2304, 2336, 3360, 4384, 5408, 6432, 7456, 8480, 9504)
GROUPS4 = [[0, 1, 2, 3], [4, 5, 6, 7]]


class Buf:
    __slots__ = ("name", "w", "r", "excl")

    def __init__(self, name):
        self.name = name
        self.w = None
        self.r = {}
        self.excl = False


class _Rec:
    def __init__(self):
        self.call = None

    def __getattr__(self, name):
        def f(*a, **k):
            assert self.call is None
            self.call = (name, a, k)
        return f


def _freeze(fn):
    r = _Rec()
    fn(r)
    name, a, k = r.call
    return lambda e: getattr(e, name)(*a, **k)


class Prog:
    def __init__(self, nc):
        self.nc = nc
        self.ops = {e: [] for e in ENGS}
        self.cnt = {e: 0 for e in ENGS}
        self.seen = {e: {} for e in ENGS}
        self.lane_cnt = [0] * NLANES
        self.lane_i = 0
        self.lane_q = {}
        self.cc_cnt = 0
        self.sem = {}
        self.stack = contextlib.ExitStack()
        self.nbuf = 0

    def sb(self, name, shape, dt):
        return self.stack.enter_context(self.nc.sbuf_tensor("sb_" + name, list(shape), dt))

    def ps(self, name, shape, dt):
        return self.stack.enter_context(self.nc.psum_tensor(name, list(shape), dt))

    def buf(self, name=None):
        self.nbuf += 1
        return Buf(name or f"b{self.nbuf}")

    def _waits(self, eng, reads, writes):
        need = {}

        def add(k, v):
            if k == "pe" and eng == "pe":
                return
            if need.get(k, 0) < v:
                need[k] = v
        for b in reads:
            if b.w is not None:
                add(*b.w)
            if b.excl:
                for k, v in b.r.items():
                    if k != eng:
                        add(k, v)
        for b in writes:
            if b.w is not None:
                add(*b.w)
            for k, v in b.r.items():
                add(k, v)
        out = []
        seen = self.seen[eng]
        for k, v in need.items():
            if seen.get(k, 0) < v:
                seen[k] = v
                out.append((k, v))
        return out

    def _mark(self, key, val, reads, writes):
        for b in reads:
            if b.r.get(key, 0) < val:
                b.r[key] = val
        for b in writes:
            b.w = (key, val)
            b.r = {}

    def op(self, eng, fn, reads=(), writes=()):
        waits = self._waits(eng, reads, writes)
        self.cnt[eng] += 1
        self.ops[eng].append((waits, _freeze(fn), (eng, 1)))
        self._mark(eng, self.cnt[eng], reads, writes)

    def mm(self, fns, reads=(), writes=()):
        waits = self._waits("pe", reads, writes)
        self.cnt["pe"] += 1
        n = len(fns)
        for i, fn in enumerate(fns):
            self.ops["pe"].append((waits if i == 0 else [], _freeze(fn), ("pe", 1) if i == n - 1 else None))
        self._mark("pe", self.cnt["pe"], reads, writes)

    def dma(self, q, out, in_, reads=(), writes=(), **kw):
        waits = self._waits(q, reads, writes)
        lo, hi = (0, 20) if q == "sp" else (20, NLANES)
        li = self.lane_q.get(q, 0)
        self.lane_q[q] = li + 1
        lane = lo + li % (hi - lo)
        key = f"d{lane}"
        prev = self.lane_cnt[lane]
        if prev > 0 and self.seen[q].get(key, 0) < prev:
            self.seen[q][key] = prev
            waits = waits + [(key, prev)]
        self.lane_cnt[lane] += 16
        self.ops[q].append((waits, lambda e: e.dma_start(out=out, in_=in_, **kw), (key, 16)))
        self._mark(key, self.lane_cnt[lane], reads, writes)

    def cc(self, kind, groups, in_ap, out_ap, reads=(), writes=()):
        waits = self._waits("pool", reads, writes)
        self.cc_cnt += 1
        self.ops["pool"].append((waits, lambda e: e.collective_compute(
            kind, ALU.bypass, replica_groups=groups, ins=[in_ap], outs=[out_ap]), ("cc", 1)))
        self._mark("cc", self.cc_cnt, reads, writes)

    def emit(self):
        nc = self.nc
        st = self.stack
        keys = list(ENGS[:4]) + [f"d{i}" for i in range(NLANES)] + ["cc"]
        for k in keys:
            self.sem[k] = st.enter_context(nc.semaphore("s_" + k))
        fin = [(f"d{i}", self.lane_cnt[i]) for i in range(NLANES) if self.lane_cnt[i] > 0]
        fin += [(e, self.cnt[e]) for e in ENGS[:4] if self.cnt[e] > 0]
        if self.cc_cnt:
            fin.append(("cc", self.cc_cnt))
        block = st.enter_context(nc.Block())
        sem = self.sem

        def run(e, lst, final=None):
            for waits, fn, inc in lst:
                for k, v in waits:
                    e.wait_ge(sem[k], v)
                ins = fn(e)
                if inc is not None:
                    if inc[0] == "cc":
                        ins.then_inc(sem["cc"])
                    else:
                        ins.then_inc(sem[inc[0]], inc[1])
            if final:
                for k, v in final:
                    e.wait_ge(sem[k], v)

        @block.tensor
        def _(e):
            run(e, self.ops["pe"])

        @block.scalar
        def _(e):
            run(e, self.ops["act"])

        @block.vector
        def _(e):
            run(e, self.ops["dve"])

        @block.gpsimd
        def _(e):
            run(e, self.ops["pool"])

        @block.sync
        def _(e):
            run(e, self.ops["sp"], fin)


class _Stop(Exception):
    pass


def build(stop=None):
    nc = bass.Bass("TRN2", target_bir_lowering=False)
    P = Prog(nc)
    ckn = [0]
    dbg_on = bool(os.environ.get("KDBG"))
    dbg_i = [0]

    def dbg(name, ap, bufs, shape, dt=BF16):
        if not dbg_on:
            return
        dbg_i[0] += 1
        d = nc.dram_tensor(f"dbg_{name}", list(shape), dt, kind="ExternalOutput").ap()
        P.dma("sp", d, ap, reads=bufs)

    sub = int(os.environ["KSUB"]) if os.environ.get("KSUB") else None
    subn = [0]

    def ck2(tag):
        subn[0] += 1
        if sub is not None and subn[0] == sub:
            print("SUBSTOP at", subn[0], tag)
            raise _Stop()

    def ck(tag):
        ckn[0] += 1
        if stop is not None and ckn[0] == stop:
            print("STOP at", ckn[0], tag)
            raise _Stop()

    def din(name, shape, dt=F32):
        return nc.dram_tensor(name, list(shape), dt, kind="ExternalInput").ap()

    def dout(name, shape):
        return nc.dram_tensor(name, list(shape), F32, kind="ExternalOutput").ap()

    xin = [din("xp", [1024, D]), din("xs", [1024, D])]
    ck_d = din("ck", [DEPTH, 256, D])
    cv_d = din("cv", [DEPTH, 256, D])
    sst_d = din("sst", [DEPTH, 2, 16, 64, 64])
    cvec_d = din("cvec", [2, D])
    pre_d = din("pre_norm_w", [DEPTH, D])
    post_d = din("post_norm_w", [DEPTH, D])
    wmod_d = din("w_mod", [DEPTH, D, 3 * D])
    bmod_d = din("b_mod", [DEPTH, 3 * D])
    win_d = din("w_in", [DEPTH, D, NIN])
    cw_d = din("m_conv_w", [DEPTH, 1280, 3])
    cb_d = din("m_conv_b", [DEPTH, 1280])
    alog_d = din("m_A_log", [DEPTH, 32])
    dtb_d = din("m_dt_bias", [DEPTH, 32])
    mD_d = din("m_D", [DEPTH, 32])
    mnw_d = din("m_norm_w", [DEPTH, D])
    lamv_d = din("da_lambda", [DEPTH, 256])
    hnw_d = din("da_head_norm_w", [DEPTH, 128])
    vnw_d = din("sg_vnorm_w", [DEPTH, D])
    sgw_d = din("sg_spatial_w", [DEPTH, 8, 128, 128])
    sgb_d = din("sg_spatial_b", [DEPTH, 1024])
    wbr_d = din("w_branch", [DEPTH, 3, D, D])
    wout_d = din("w_out", [DEPTH, D, D])
    cst_d = din("cst", [128, 1024])
    rkc_d = din("rkc", [128, 16])
    rope_d = din("ropec", [128, 2, 1024])

    yout = [dout("yp", [1024, D]), dout("ys", [1024, D])]
    nk_d = dout("nk", [4, DEPTH, 256, D])
    nv_d = dout("nv", [4, DEPTH, 256, D])
    nst_d = dout("nst", [4, DEPTH, 2, 16, 64, 64])

    xres = [nc.dram_tensor(f"xres{g}", [1024, D], F32).ap() for g in range(2)]
    bh = [nc.dram_tensor(f"bh{i}", [128, 20], BF16).ap() for i in range(2)]
    gh = [nc.dram_tensor(f"gh{i}", [512, 20], BF16).ap() for i in range(2)]
    bkv = [[nc.dram_tensor(f"bkv{i}_{c}", [512, 1024], BF16).ap() for c in range(4)] for i in range(2)]
    gkv = [[nc.dram_tensor(f"gkv{i}_{c}", [2048, 1024], BF16).ap() for c in range(4)] for i in range(2)]
    bst = [nc.dram_tensor(f"bst{i}", [128, 1056], F32).ap() for i in range(2)]
    gst = [nc.dram_tensor(f"gst{i}", [512, 1056], F32).ap() for i in range(2)]
    Bd = {n: P.buf(n) for n in ["xres0", "xres1", "bh0", "bh1", "gh0", "gh1", "bkv0", "bkv1",
                                 "gkv0", "gkv1", "bst0", "bst1", "gst0", "gst1"]}

    def T(name, shape, dt=F32):
        return P.sb(name, shape, dt), P.buf(name)

    cst, Bcst = T("cst", [128, 1024])
    rkc, Brkc = T("rkc", [128, 16])
    ropeb, Brope = T("ropeb", [128, 2, 1024], BF16)
    identb, Bidb = T("identb", [128, 128], BF16)
    rmtb, Brmt = T("rmtb", [128, 128], BF16)
    onesb, Bonesb = T("onesb", [128, 128], BF16)
    identf = cst[:, 0:128]
    LE = cst[:, 128:256]
    GE = cst[:, 256:384]
    GT = cst[:, 384:512]
    LT = cst[:, 512:640]
    onesf = cst[:, 768:896]
    c_one = cst[:, 896:897]
    c_eps = cst[:, 897:898]
    c_zero = cst[:, 898:899]

    scT, BscT = T("scT", [128, 8, 2], BF16)
    scRep, BscRep = T("scRep", [128, 2, 8, 128], BF16)
    modT, BmodT = T("modT", [128, 16, 2])
    bmT, BbmT = T("bmT", [128, 24])
    preT, BpreT = T("preT", [128, 8])
    gmul, Bgmul = T("gmul", [128, 8, 2])
    shiftT, Bshift = T("shiftT", [128, 8, 2])
    lamt, Blamt = T("lamt", [128, 264])
    cwT, BcwT = T("cwT", [128, 10, 3])
    cbT, BcbT = T("cbT", [128, 10])
    arow, Barow = T("arow", [128, 32])
    dtbrow, Bdtb = T("dtbrow", [128, 32])
    drow, Bdrow = T("drow", [128, 48])
    gpw, Bgpw = T("gpw", [128, 1024])
    hwT, BhwT = T("hwT", [128, 2])
    wsT, BwsT = T("wsT", [128, 8, 128], BF16)

    hT, _ = T("hT", [128, 8, 1024], BF16)
    BhT = [P.buf(f"hT{j}") for j in range(8)]
    RA, BRA = T("RA", [128, 10, 1040], BF16)
    RY, BRY = T("RY", [128, 8, 1024], BF16)
    RK, BRK = T("RK", [128, 8, 1024], BF16)
    RV, BRV = T("RV", [128, 8, 1024], BF16)
    RG, BRG = T("RG", [128, 8, 1024], BF16)
    NW = 2
    wb = [T(f"wb{i}", [128, 8, 512], BF16) for i in range(NW)]
    stg, Bstg = T("stg", [128, 2, 1024])
    stg2 = [T(f"stg2_{i}", [128, 1024]) for i in range(2)]
    tmpb = [T(f"tmpb{i}", [128, 2, 1024], BF16) for i in range(2)]
    tf = [T(f"tf{i}", [128, 512]) for i in range(4)]
    small, Bsmall = T("small", [128, 64])
    dtt, Bdtt = T("dtt", [128, 8, 32])
    at, Bat = T("at", [128, 8, 32])
    Et, BEt = T("Et", [128, 8, 64])
    dch, Bdch = T("dch", [128, 8, 32])
    totl, Btotl = T("totl", [128, 8, 32])
    state, Bstate = T("state", [128, 2, 512])
    stateb, Bstateb = T("stateb", [128, 2, 512], BF16)
    cbm = [T(f"cbm{i}", [128, 4, 128]) for i in range(1)]
    lhs = [T(f"lhs{i}", [128, 128]) for i in range(2)]
    ldec = [T(f"ldec{i}", [128, 128]) for i in range(2)]
    wmat = [T(f"wmat{i}", [128, 128], BF16) for i in range(4)]
    hin, Bhin = T("hin", [128, 2, 512])
    gall, Bgall = stg[:].rearrange("p a t -> p (a t)")[:, 0:1056], Bstg
    halo, Bhalo = T("halo", [128, 4, 20], BF16)
    halof, Bhalof = T("halof", [128, 2, 10])
    ptb = [T(f"ptb{i}", [128, 2, 512], BF16) for i in range(2)]
    xeT, Bxe = T("xeT", [128, 2, 1024], BF16)
    rowA, BrowA = T("rowA", [128, 1024])
    rowB, BrowB = T("rowB", [128, 1024])

    banks = []
    for i in range(8):
        t = P.ps(f"pb{i}", [128, 512], F32)
        _b = P.buf(f"pb{i}")
        _b.excl = True
        banks.append((t, [_b, _b, _b, _b]))

    def bkb(i):
        return banks[i][1]

    bank_rr = [0]

    def nextbank(lo=0, hi=8):
        i = lo + bank_rr[0] % (hi - lo)
        bank_rr[0] += 1
        return i

    w_i = [0]

    def wload(src_ap, ncols):
        t, b = wb[w_i[0] % NW]
        w_i[0] += 1
        P.dma("pool", t[:, :, 0:ncols], src_ap.rearrange("(kc p) c -> p kc c", p=128), writes=[b])
        return t, b

    def rstd_from(ssq_ap, n, out_ap, rb, wbuf):
        P.op("act", lambda e: e.activation(out_ap, ssq_ap, AF.Sqrt, bias=c_eps, scale=1.0 / n),
             reads=rb + [Bcst], writes=[wbuf])
        P.op("dve", lambda e: e.reciprocal(out_ap, out_ap), reads=[wbuf], writes=[wbuf])

    P.dma("sp", cst[:], cst_d, writes=[Bcst])
    P.dma("sp", rkc[:], rkc_d, writes=[Brkc])
    P.dma("pool", ropeb[:], rope_d, writes=[Brope])
    P.op("dve", lambda e: e.tensor_copy(identb[:], identf), reads=[Bcst], writes=[Bidb])
    P.op("dve", lambda e: e.tensor_copy(rmtb[:], cst[:, 640:768]), reads=[Bcst], writes=[Brmt])
    P.op("dve", lambda e: e.tensor_copy(onesb[:], onesf), reads=[Bcst], writes=[Bonesb])
    for c in range(2):
        P.dma("sp", modT[:, 0:8, c], cvec_d[c].rearrange("(kc p) -> p kc", p=128), writes=[BmodT],
              allow_slow_non_contiguous=True)
    P.op("act", lambda e: e.activation(scT[:], modT[:, 0:8, :], AF.Silu), reads=[BmodT], writes=[BscT])
    for c in range(2):
        P.op("dve", lambda e, c=c: e.tensor_copy(scRep[:, c], scT[:, :, c:c + 1].to_broadcast([128, 8, 128])),
             reads=[BscT], writes=[BscRep])

    def layer_prep(l):
        lam_init = 0.8 - 0.6 * math.exp(-0.3 * l)
        P.dma("sp", bmT[:], bmod_d[l].rearrange("(b p) -> p b", p=128), writes=[BbmT], allow_slow_non_contiguous=True)
        P.dma("sp", preT[:], pre_d[l].rearrange("(b p) -> p b", p=128), writes=[BpreT], allow_slow_non_contiguous=True)
        for wblk in range(4):
            wt, wbf = wload(wmod_d[l][:, wblk * 512:(wblk + 1) * 512], 512)
            for s in range(4):
                blk = wblk * 4 + s
                bi = nextbank()
                pt = banks[bi][0]
                P.mm([lambda e, kc=kc, s=s, pt=pt, wt=wt: e.matmul(pt[:, 0:2], wt[:, kc, s * 128:(s + 1) * 128],
                                                                  scT[:, kc, :], start=(kc == 0), stop=(kc == 7))
                      for kc in range(8)], reads=[wbf, BscT], writes=[bkb(bi)[0]])
                P.op("dve", lambda e, blk=blk, pt=pt: e.tensor_single_scalar(modT[:, blk, :], pt[:, 0:2], bmT[:, blk:blk + 1], ALU.add),
                     reads=[bkb(bi)[0], BbmT], writes=[BmodT])
        P.op("dve", lambda e: e.tensor_copy(shiftT[:], modT[:, 0:8, :]), reads=[BmodT], writes=[Bshift])
        P.op("dve", lambda e: e.tensor_single_scalar(gmul[:], modT[:, 8:16, :], 1.0, ALU.add), reads=[BmodT], writes=[Bgmul])
        P.op("dve", lambda e: e.tensor_tensor(gmul[:], gmul[:], preT[:].unsqueeze(2).to_broadcast([128, 8, 2]), ALU.mult),
             reads=[Bgmul, BpreT], writes=[Bgmul])
        P.dma("sp", lamt[:, 0:256], lamv_d[l:l + 1, :].partition_broadcast(128), writes=[Blamt])
        P.op("dve", lambda e: e.tensor_tensor(lamt[:, 0:64], lamt[:, 0:64], lamt[:, 64:128], ALU.mult), reads=[Blamt], writes=[Blamt])
        P.op("dve", lambda e: e.tensor_tensor(lamt[:, 128:192], lamt[:, 128:192], lamt[:, 192:256], ALU.mult), reads=[Blamt], writes=[Blamt])
        P.op("dve", lambda e: e.reduce_sum(lamt[:, 256:257], lamt[:, 0:64], axis=AX.X), reads=[Blamt], writes=[Blamt])
        P.op("dve", lambda e: e.reduce_sum(lamt[:, 257:258], lamt[:, 128:192], axis=AX.X), reads=[Blamt], writes=[Blamt])
        P.op("act", lambda e: e.activation(lamt[:, 258:260], lamt[:, 256:258], AF.Exp), reads=[Blamt], writes=[Blamt])
        P.op("dve", lambda e: e.tensor_tensor(lamt[:, 260:261], lamt[:, 259:260], lamt[:, 258:259], ALU.subtract), reads=[Blamt], writes=[Blamt])
        P.op("dve", lambda e: e.tensor_single_scalar(lamt[:, 261:262], lamt[:, 260:261], -lam_init, ALU.add), reads=[Blamt], writes=[Blamt])
        P.dma("sp", cwT[:], cw_d[l].rearrange("(kc p) k -> p kc k", p=128), writes=[BcwT], allow_slow_non_contiguous=True)
        P.dma("sp", cbT[:], cb_d[l].rearrange("(kc p) -> p kc", p=128), writes=[BcbT], allow_slow_non_contiguous=True)
        P.dma("sp", arow[:], alog_d[l:l + 1, :].partition_broadcast(128), writes=[Barow])
        P.op("act", lambda e: e.activation(arow[:], arow[:], AF.Exp), reads=[Barow], writes=[Barow])
        P.op("dve", lambda e: e.tensor_single_scalar(arow[:], arow[:], -1.0, ALU.mult), reads=[Barow], writes=[Barow])
        P.dma("sp", dtbrow[:], dtb_d[l:l + 1, :].partition_broadcast(128), writes=[Bdtb])
        P.dma("sp", drow[:, 0:32], mD_d[l:l + 1, :].partition_broadcast(128), writes=[Bdrow])
        P.op("dve", lambda e: e.tensor_tensor(drow[:, 32:48], drow[:, 0:16], drow[:, 16:32], ALU.add), reads=[Bdrow], writes=[Bdrow])
        P.dma("sp", hwT[:, 0:1], hnw_d[l].rearrange("(d o) -> d o", o=1), writes=[BhwT], allow_slow_non_contiguous=True)
        P.op("dve", lambda e: e.tensor_single_scalar(hwT[:, 1:2], hwT[:, 0:1], 1.0 - lam_init, ALU.mult), reads=[BhwT], writes=[BhwT])
        P.dma("sp", stg[:, 0, :].rearrange("p (g s) -> p g s", g=8), sgw_d[l].rearrange("g t s -> t g s"), writes=[Bstg])
        for g in range(8):
            bi = nextbank()
            pt = banks[bi][0]
            P.mm([lambda e, g=g, pt=pt: e.transpose(pt[:, 0:128], stg[:, 0, g * 128:(g + 1) * 128], identf)],
                 reads=[Bstg, Bcst], writes=[bkb(bi)[0]])
            P.op("act", lambda e, g=g, pt=pt: e.copy(wsT[:, g, :], pt[:, 0:128]), reads=[bkb(bi)[0]], writes=[BwsT])

    def run_group(l, kind):
        par = l % 2
        lam_init = 0.8 - 0.6 * math.exp(-0.3 * l)
        xsrc = xin[kind] if l == 0 else xres[kind]
        xdst = yout[kind] if l == DEPTH - 1 else xres[kind]
        Bxres = Bd[f"xres{kind}"]
        c = kind
        nseq = 4 if kind == 0 else 1
        cps = 2 if kind == 0 else 8
        RAv = RA[:].rearrange("p k (s t) -> p k s t", s=4)
        RQ = RA
        def ctk_ap(h, a, b):
            return RA[:, 8 + h // 4, (h % 4) * 256 + a:(h % 4) * 256 + b]
        ctv = tmpb[1][0]
        Bctv = tmpb[1][1]
        SFb = RK[:, :, 0:512]
        RSb = RK[:, :, 512:1024]
        YA = RV
        MG = RY

        def xc(kc, j, p0=0, p1=128):
            return RAv[p0:p1, kc, j // 2, 2 + (j % 2) * 128: 2 + (j % 2) * 128 + 128]

        for j in range(8):
            P.dma("sp", stg[:, 0, :], xsrc[j * 128:(j + 1) * 128, :], reads=[Bxres] if l > 0 else [], writes=[Bstg])
            P.op("act", lambda e: e.activation(stg[:, 1, :], stg[:, 0, :], AF.Square, accum_out=small[:, 0:1]),
                 reads=[Bstg], writes=[Bstg, Bsmall])
            rstd_from(small[:, 0:1], 1024, small[:, 1:2], [Bsmall], Bsmall)
            tb, tbb = tmpb[j % 2]
            P.op("dve", lambda e, tb=tb: e.tensor_single_scalar(tb[:, 0, :], stg[:, 0, :], small[:, 1:2], ALU.mult),
                 reads=[Bstg, Bsmall], writes=[tbb])
            bi = nextbank()
            ptv = banks[bi][0][:].bitcast(BF16)
            P.mm([lambda e, kc=kc, tb=tb, ptv=ptv: e.transpose(ptv[:, kc * 128:(kc + 1) * 128], tb[:, 0, kc * 128:(kc + 1) * 128], identb[:])
                  for kc in range(8)], reads=[tbb, Bidb], writes=bkb(bi))
            for kc in range(8):
                P.op("act", lambda e, kc=kc, j=j, ptv=ptv: e.activation(hT[:, kc, j * 128:(j + 1) * 128], ptv[:, kc * 128:(kc + 1) * 128],
                                                                       AF.Identity, scale=gmul[:, kc, c:c + 1], bias=shiftT[:, kc, c:c + 1]),
                     reads=bkb(bi) + [Bgmul, Bshift], writes=[BhT[j]])

        ck(f"P0{l}{kind}")

        def proj_fm(c0, ncols, evac):
            blocks = [(d0, min(512, ncols - d0)) for d0 in range(0, ncols, 512)]
            loaded = {0: wload(win_d[l][:, c0:c0 + blocks[0][1]], blocks[0][1])}
            for bi_, (done, n) in enumerate(blocks):
                if bi_ + 1 < len(blocks):
                    d1, n1 = blocks[bi_ + 1]
                    loaded[bi_ + 1] = wload(win_d[l][:, c0 + d1:c0 + d1 + n1], n1)
                wt, wbf = loaded.pop(bi_)
                for s in range((n + 127) // 128):
                    m = min(128, n - s * 128)
                    for nb in range(2):
                        bi = nextbank()
                        pt = banks[bi][0]
                        P.mm([lambda e, kc=kc, s=s, m=m, nb=nb, pt=pt, wt=wt: e.matmul(
                            pt[0:m, :], wt[:, kc, s * 128:s * 128 + m], hT[:, kc, nb * 512:(nb + 1) * 512],
                            start=(kc == 0), stop=(kc == 7)) for kc in range(8)],
                            reads=[wbf] + BhT[nb * 4:(nb + 1) * 4], writes=bkb(bi))
                        evac((done // 128) + s, nb, pt, bi)

        def proj_tm(c0, ncols, evac):
            blocks = [(d0, min(512, ncols - d0)) for d0 in range(0, ncols, 512)]
            loaded = {0: wload(win_d[l][:, c0:c0 + blocks[0][1]], blocks[0][1])}
            for bi_, (done, n) in enumerate(blocks):
                if bi_ + 1 < len(blocks):
                    d1, n1 = blocks[bi_ + 1]
                    loaded[bi_ + 1] = wload(win_d[l][:, c0 + d1:c0 + d1 + n1], n1)
                wt, wbf = loaded.pop(bi_)
                for j in range(8):
                    bi = nextbank()
                    pt = banks[bi][0]
                    P.mm([lambda e, kc=kc, j=j, n=n, pt=pt, wt=wt: e.matmul(
                        pt[:, 0:n], hT[:, kc, j * 128:(j + 1) * 128], wt[:, kc, 0:n],
                        start=(kc == 0), stop=(kc == 7)) for kc in range(8)],
                        reads=[wbf, BhT[j]], writes=bkb(bi))
                    evac(j, done // 512, pt, bi, n)

        def ev_rope(dst, Bdst):
            def ev(cb, nb, pt, bi):
                tb, tbb = tmpb[0]
                P.op("act", lambda e: e.copy(tb[:, 0, 0:512], pt[:]), reads=bkb(bi), writes=[tbb])
                b2 = nextbank()
                p2 = banks[b2][0]
                P.mm([lambda e: e.matmul(p2[:], rmtb[:], tb[:, 0, 0:512], start=True, stop=True)],
                     reads=[tbb, Brmt], writes=bkb(b2))
                t1, t1b = tf[0]
                t2, t2b = tf[1]
                P.op("dve", lambda e: e.tensor_tensor(t1[:], pt[:], ropeb[:, 0, nb * 512:(nb + 1) * 512], ALU.mult),
                     reads=bkb(bi) + [Brope], writes=[t1b])
                P.op("dve", lambda e: e.tensor_tensor(t2[:], p2[:], ropeb[:, 1, nb * 512:(nb + 1) * 512], ALU.mult),
                     reads=bkb(b2) + [Brope], writes=[t2b])
                P.op("dve", lambda e: e.tensor_tensor(dst[:, cb, nb * 512:(nb + 1) * 512], t1[:], t2[:], ALU.add),
                     reads=[t1b, t2b], writes=[Bdst])
            return ev

        def ev_copy(dst, Bdst, func=None):
            def ev(cb, nb, pt, bi):
                if func is None:
                    P.op("act", lambda e: e.copy(dst[:, cb, nb * 512:(nb + 1) * 512], pt[:]), reads=bkb(bi), writes=[Bdst])
                else:
                    P.op("act", lambda e: e.activation(dst[:, cb, nb * 512:(nb + 1) * 512], pt[:], func), reads=bkb(bi), writes=[Bdst])
            return ev

        def proj_v():
            def ev_v(j, cbk, pt, bi, n):
                P.op("act", lambda e: e.copy(RV[:, j, cbk * 512:(cbk + 1) * 512], pt[:]), reads=bkb(bi), writes=[BRV])
                if kind == 0:
                    st, stb = stg2[(j * 2 + cbk) % 2]
                    P.op("dve", lambda e: e.tensor_copy(st[:, 0:512], pt[:]), reads=bkb(bi), writes=[stb])
                    P.dma("sp", nv_d[j // 2, l, (j % 2) * 128:(j % 2 + 1) * 128, cbk * 512:(cbk + 1) * 512], st[:, 0:512], reads=[stb])
            proj_tm(C_V, 1024, ev_v)

        def proj_k_tm_out():
            def ev_k(j, cbk, pt, bi, n):
                st, stb = stg2[(j * 2 + cbk) % 2]
                P.op("dve", lambda e: e.tensor_copy(st[:, 0:512], pt[:]), reads=bkb(bi), writes=[stb])
                P.dma("sp", nk_d[j // 2, l, (j % 2) * 128:(j % 2 + 1) * 128, cbk * 512:(cbk + 1) * 512], st[:, 0:512], reads=[stb])
            proj_tm(C_K, 1024, ev_k)

        if kind == 1:
            proj_fm(C_K, 1024, ev_rope(RK, BRK))
            proj_v()
            Bb = Bd[f"bkv{par}"]
            for cch in range(2):
                P.dma("sp", bkv[par][cch].rearrange("(h p) t -> p h t", p=128), RK[:, cch * 4:(cch + 1) * 4, :], reads=[BRK], writes=[Bb])
                P.dma("sp", bkv[par][2 + cch].rearrange("(j p) c -> p j c", p=128), RV[:, cch * 4:(cch + 1) * 4, :], reads=[BRV], writes=[Bb])
            for cch in range(4):
                P.cc("AllGather", GROUPS4, bkv[par][cch], gkv[par][cch], reads=[Bb], writes=[Bd[f"gkv{par}"]])

        ck(f"kvsend{l}{kind}")
        def ev_xbc(cb, nb, pt, bi):
            P.op("act", lambda e: e.copy(RAv[:, cb, nb * 2:nb * 2 + 2, 2:258], pt[:].rearrange("p (s t) -> p s t", s=2)),
                 reads=bkb(bi), writes=[BRA])
        proj_fm(C_XBC, 1280, ev_xbc)

        def ev_dt(j, cbk, pt, bi, n):
            P.op("dve", lambda e: e.tensor_tensor(small[:, 32:64], pt[:, 0:32], dtbrow[:], ALU.add), reads=bkb(bi) + [Bdtb], writes=[Bsmall])
            P.op("dve", lambda e: e.tensor_single_scalar(small[:, 0:32], small[:, 32:64], 30.0, ALU.min), reads=[Bsmall], writes=[Bsmall])
            P.op("act", lambda e: e.activation(small[:, 0:32], small[:, 0:32], AF.Exp), reads=[Bsmall], writes=[Bsmall])
            P.op("act", lambda e: e.activation(small[:, 0:32], small[:, 0:32], AF.Ln, bias=c_one), reads=[Bsmall, Bcst], writes=[Bsmall])
            P.op("dve", lambda e: e.tensor_tensor(dtt[:, j, :], small[:, 32:64], small[:, 0:32], ALU.max), reads=[Bsmall], writes=[Bdtt])
            P.op("dve", lambda e: e.tensor_tensor(at[:, j, :], dtt[:, j, :], arow[:], ALU.mult), reads=[Bdtt, Barow], writes=[Bat])
        proj_tm(C_DT, 32, ev_dt)

        ck(f"xbcdt{l}{kind}")
        if kind == 0:
            P.op("dve", lambda e: e.memset(RAv[:, :, :, 1:2], 0.0), writes=[BRA])
            P.op("dve", lambda e: e.memset(RAv[:, :, :, 258:259], 0.0), writes=[BRA])
        else:
            P.op("dve", lambda e: e.tensor_copy(RAv[:, :, 1:4, 1:2], RAv[:, :, 0:3, 257:258]), reads=[BRA], writes=[BRA])
            P.op("dve", lambda e: e.tensor_copy(RAv[:, :, 0:3, 258:259], RAv[:, :, 1:4, 2:3]), reads=[BRA], writes=[BRA])
            P.op("dve", lambda e: e.tensor_copy(halo[:, 0, 0:10], RAv[:, :, 0, 2]), reads=[BRA], writes=[Bhalo])
            P.op("dve", lambda e: e.tensor_copy(halo[:, 0, 10:20], RAv[:, :, 3, 257]), reads=[BRA], writes=[Bhalo])
            P.dma("sp", bh[par], halo[:, 0, :], reads=[Bhalo], writes=[Bd[f"bh{par}"]])
            P.cc("AllGather", GROUPS4, bh[par], gh[par], reads=[Bd[f"bh{par}"]], writes=[Bd[f"gh{par}"]])
            P.dma("sp", halo[:], gh[par].rearrange("(r p) c -> p r c", p=128), reads=[Bd[f"gh{par}"]], writes=[Bhalo])
            for side in range(2):
                for r in range(4):
                    src = halo[:, r, 10:20] if side == 0 else halo[:, r, 0:10]
                    selc = rkc[:, side * 4 + r: side * 4 + r + 1]
                    if r == 0:
                        P.op("dve", lambda e, src=src, selc=selc, side=side: e.tensor_single_scalar(halof[:, side, :], src, selc, ALU.mult),
                             reads=[Bhalo, Brkc], writes=[Bhalof])
                    else:
                        P.op("dve", lambda e, src=src, selc=selc, side=side: e.scalar_tensor_tensor(
                            halof[:, side, :], src, selc, halof[:, side, :], ALU.mult, ALU.add),
                            reads=[Bhalo, Brkc, Bhalof], writes=[Bhalof])
            P.op("dve", lambda e: e.tensor_copy(RAv[:, :, 0, 1], halof[:, 0, :]), reads=[Bhalof], writes=[BRA])
            P.op("dve", lambda e: e.tensor_copy(RAv[:, :, 3, 258], halof[:, 1, :]), reads=[Bhalof], writes=[BRA])

        ck(f"halo{l}{kind}")
        for kc in range(10):
            for hb in range(2):
                t1, t1b = tf[(kc * 2 + hb) % 2]
                t1v = t1[:].rearrange("p (s t) -> p s t", s=2)
                sl = slice(hb * 2, hb * 2 + 2)
                P.op("dve", lambda e, kc=kc, t1v=t1v, sl=sl: e.tensor_single_scalar(t1v, RAv[:, kc, sl, 1:257], cwT[:, kc, 0:1], ALU.mult),
                     reads=[BRA, BcwT], writes=[t1b])
                P.op("dve", lambda e, kc=kc, t1v=t1v, sl=sl: e.scalar_tensor_tensor(t1v, RAv[:, kc, sl, 2:258], cwT[:, kc, 1:2], t1v, ALU.mult, ALU.add),
                     reads=[BRA, BcwT, t1b], writes=[t1b])
                P.op("dve", lambda e, kc=kc, t1v=t1v, sl=sl: e.scalar_tensor_tensor(t1v, RAv[:, kc, sl, 3:259], cwT[:, kc, 2:3], t1v, ALU.mult, ALU.add),
                     reads=[BRA, BcwT, t1b], writes=[t1b])
                P.op("act", lambda e, kc=kc, t1v=t1v, sl=sl: e.activation(RAv[:, kc, sl, 2:258], t1v, AF.Silu, bias=cbT[:, kc:kc + 1]),
                     reads=[t1b, BcbT], writes=[BRA])

        def ssd_main():
            for j in range(8):
                xt, xtb = tmpb[0]
                xd, xdb = tmpb[1]
                bi = 3
                ptv = banks[bi][0][:].bitcast(BF16)
                P.mm([lambda e, kc=kc, ptv=ptv, j=j: e.transpose(ptv[:, kc * 128:(kc + 1) * 128], xc(kc, j), identb[:])
                      for kc in range(8)], reads=[BRA, Bidb], writes=bkb(bi))
                P.op("act", lambda e, ptv=ptv: e.copy(xt[:, 0, :], ptv), reads=bkb(bi), writes=[xtb])
                P.mm([lambda e, ptv=ptv, j=j: e.transpose(ptv[:, 0:128], xc(8, j), identb[:])],
                     reads=[BRA, Bidb], writes=bkb(bi))
                P.op("act", lambda e, ptv=ptv: e.copy(xt[:, 1, 0:128], ptv[:, 0:128]), reads=bkb(bi), writes=[xtb])
                ck2("ssd_T")
                pc = banks[2][0]
                aj = at[:, j, :]
                P.mm([lambda e, aj=aj: e.matmul(pc[:, 256:272], LE, aj[:, 0:16], start=True, stop=True),
                      lambda e, aj=aj: e.matmul(pc[:, 272:288], GE, aj[:, 16:32], start=True, stop=True),
                      lambda e, aj=aj: e.matmul(pc[:, 288:304], GT, aj[:, 0:16], start=True, stop=True),
                      lambda e, aj=aj: e.matmul(pc[:, 304:320], LT, aj[:, 16:32], start=True, stop=True),
                      lambda e, aj=aj: e.matmul(pc[:, 384:416], onesf, aj, start=True, stop=True)],
                     reads=[Bat, Bcst], writes=[bkb(2)[2], bkb(2)[3]])
                P.op("act", lambda e, j=j: e.activation(Et[:, j, :], pc[:, 256:320], AF.Exp), reads=[bkb(2)[2]], writes=[BEt])
                P.op("act", lambda e, j=j: e.activation(dch[:, j, :], pc[:, 384:416], AF.Exp), reads=[bkb(2)[3]], writes=[Bdch])
                P.op("dve", lambda e, j=j: e.tensor_copy(totl[:, j, :], pc[:, 384:416]), reads=[bkb(2)[3]], writes=[Btotl])
                ck2("ssd_cum")
                xtv = xt[:, 0, :].rearrange("p (h q) -> p h q", h=16)
                for d in range(2):
                    P.op("dve", lambda e, d=d, j=j: e.tensor_tensor(xd[:, d, :].rearrange("p (h q) -> p h q", h=16), xtv,
                                                                   dtt[:, j, d * 16:(d + 1) * 16].unsqueeze(2).to_broadcast([128, 16, 64]), ALU.mult),
                         reads=[xtb, Bdtt], writes=[xdb])
                for d in range(2):
                    P.op("dve", lambda e, d=d, j=j: e.tensor_tensor(
                        xeT[:, d, :].rearrange("p (h q) -> p h q", h=16), xd[:, d, :].rearrange("p (h q) -> p h q", h=16),
                        Et[:, j, 32 + d * 16: 48 + d * 16].unsqueeze(2).to_broadcast([128, 16, 64]), ALU.mult),
                        reads=[xdb, BEt], writes=[Bxe])
                ck2("ssd_xdt")
                pcg = [banks[2][0][:, 0:128], banks[3][0][:, 0:128]]
                P.mm([lambda e, g=g, j=j: e.matmul(pcg[g], xc(8, j, g * 64, (g + 1) * 64),
                                                  xc(9, j, g * 64, (g + 1) * 64), start=True, stop=True) for g in range(2)],
                     reads=[BRA], writes=[bkb(2)[0], bkb(3)[0]])
                cb_t, cb_b = cbm[0]
                for g in range(2):
                    for d in range(2):
                        P.op("dve", lambda e, g=g, d=d: e.tensor_tensor(cb_t[:, g * 2 + d, :], pcg[g], LE if d == 0 else GE, ALU.mult),
                             reads=[bkb(2 + g)[0], Bcst], writes=[cb_b])
                ck2("ssd_cb")
                ybank = (4, 5)
                idx = 0
                for h in range(16):
                    for d in range(2):
                        g = h // 8
                        lt_, lb_ = lhs[idx % 2]
                        ld_, ldb_ = ldec[idx % 2]
                        wm_, wmb_ = wmat[idx % 4]
                        dbi, dq = idx % 2, 0
                        pd = banks[dbi][0][:, dq * 128:(dq + 1) * 128]
                        P.op("dve", lambda e, lt_=lt_, d=d, h=h, j=j: e.tensor_single_scalar(lt_[:], GT if d == 0 else LT, at[:, j, d * 16 + h:d * 16 + h + 1], ALU.mult),
                             reads=[Bcst, Bat], writes=[lb_])
                        P.mm([lambda e, pd=pd, lt_=lt_, d=d: e.matmul(pd, lt_[:], LE if d == 0 else GE, start=True, stop=True)],
                             reads=[lb_, Bcst], writes=[bkb(dbi)[dq]])
                        P.op("act", lambda e, pd=pd, ld_=ld_: e.activation(ld_[:], pd, AF.Exp), reads=[bkb(dbi)[dq]], writes=[ldb_])
                        P.op("dve", lambda e, ld_=ld_, wm_=wm_, g=g, d=d: e.tensor_tensor(wm_[:], ld_[:], cb_t[:, g * 2 + d, :], ALU.mult),
                             reads=[ldb_, cb_b], writes=[wmb_])
                        yb = ybank[h // 8]
                        py = banks[yb][0][:, (h % 8) * 64:(h % 8 + 1) * 64]
                        P.mm([lambda e, py=py, wm_=wm_, d=d, h=h: e.matmul(py, wm_[:], xd[:, d, h * 64:(h + 1) * 64], start=(d == 0), stop=(d == 1))],
                             reads=[wmb_, xdb], writes=[bkb(yb)[(h % 8) // 2]])
                        idx += 1
                ck2("ssd_y")
                t1, t1b = tf[1]
                for half in range(2):
                    P.op("dve", lambda e, half=half: e.tensor_tensor(
                        t1[:].rearrange("p (h q) -> p h q", h=8), xt[:, 0, half * 512:(half + 1) * 512].rearrange("p (h q) -> p h q", h=8),
                        drow[:, 32 + half * 8: 40 + half * 8].unsqueeze(2).to_broadcast([128, 8, 64]), ALU.mult),
                        reads=[xtb, Bdrow], writes=[t1b])
                    P.op("dve", lambda e, half=half, j=j: e.tensor_tensor(RY[:, j, half * 512:(half + 1) * 512], banks[4 + half][0][:], t1[:], ALU.add),
                         reads=[t1b] + bkb(4 + half), writes=[BRY])
                ck2("ssd_dskip")
                for d in range(2):
                    for g in range(2):
                        sbk = 6 + g
                        ps_ = banks[sbk][0]
                        P.mm([lambda e, ps_=ps_, d=d, g=g: e.matmul(ps_[:], xt[:, 1, 0:128], xeT[:, d, g * 512:(g + 1) * 512], start=True, stop=True)],
                             reads=[xtb, Bxe], writes=bkb(sbk))
                        dstS = SFb if d == 0 else RSb
                        P.op("act", lambda e, ps_=ps_, g=g, j=j, dstS=dstS: e.copy(dstS[g * 64:(g + 1) * 64, j, :], ps_[g * 64:(g + 1) * 64, :]),
                             reads=bkb(sbk), writes=[BRK])
                ck2("ssd_S")

        def chain(d, init_ap, init_bufs, addS, finals):
            Ssrc = SFb if d == 0 else RSb
            ecol = 0 if d == 0 else 16
            for s in range(nseq):
                chunks = list(range(s * cps, (s + 1) * cps))
                if d == 1:
                    chunks = chunks[::-1]
                have = False
                for ci, j in enumerate(chunks):
                    if ci == 0 and init_ap is not None:
                        P.op("dve", lambda e: e.tensor_copy(state[:, d, :], init_ap), reads=init_bufs, writes=[Bstate])
                        have = True
                    if have:
                        P.op("act", lambda e: e.copy(stateb[:, d, :], state[:, d, :]), reads=[Bstate], writes=[Bstateb])
                        for g in range(2):
                            bo = 6 + g
                            po = banks[bo][0]
                            P.mm([lambda e, po=po, g=g, j=j: e.matmul(po[:], xc(9, j, g * 64, (g + 1) * 64),
                                                                     stateb[g * 64:(g + 1) * 64, d, :], start=True, stop=True)],
                                 reads=[BRA, Bstateb], writes=bkb(bo))
                            t1, t1b = tf[2 + g]
                            P.op("dve", lambda e, po=po, g=g, j=j, t1=t1: e.tensor_tensor(
                                t1[:].rearrange("p (h q) -> p h q", h=8), po[:].rearrange("p (h q) -> p h q", h=8),
                                Et[:, j, ecol + g * 8: ecol + g * 8 + 8].unsqueeze(2).to_broadcast([128, 8, 64]), ALU.mult),
                                reads=bkb(bo) + [BEt], writes=[t1b])
                            P.op("dve", lambda e, g=g, j=j, t1=t1: e.tensor_tensor(RY[:, j, g * 512:(g + 1) * 512], RY[:, j, g * 512:(g + 1) * 512], t1[:], ALU.add),
                                 reads=[t1b, BRY], writes=[BRY])
                        for g in range(2):
                            P.op("dve", lambda e, g=g, j=j: e.tensor_tensor(
                                state[g * 64:(g + 1) * 64, d, :].rearrange("p (h q) -> p h q", h=8),
                                state[g * 64:(g + 1) * 64, d, :].rearrange("p (h q) -> p h q", h=8),
                                dch[g * 64:(g + 1) * 64, j, d * 16 + g * 8: d * 16 + g * 8 + 8].unsqueeze(2).to_broadcast([64, 8, 64]), ALU.mult),
                                reads=[Bstate, Bdch], writes=[Bstate])
                        if addS:
                            P.op("dve", lambda e, j=j: e.tensor_tensor(state[:, d, :], state[:, d, :], Ssrc[:, j, :], ALU.add),
                                 reads=[Bstate, BRK], writes=[Bstate])
                    else:
                        P.op("dve", lambda e, j=j: e.tensor_copy(state[:, d, :], Ssrc[:, j, :]), reads=[BRK], writes=[Bstate])
                        have = True
                if finals is not None:
                    finals(s, d)

        def prompt_final(s, d):
            st, stb = stg2[(s * 2 + d) % 2]
            for blk in range(4):
                bi = nextbank(0, 4)
                pt = banks[bi][0]
                P.mm([lambda e, pt=pt, blk=blk: e.transpose(pt[:, 0:128], state[:, d, blk * 128:(blk + 1) * 128], identf)],
                     reads=[Bstate, Bcst], writes=[bkb(bi)[0]])
                P.op("act", lambda e, pt=pt, blk=blk, st=st: e.copy(st[:, blk * 128:(blk + 1) * 128], pt[:, 0:128]), reads=[bkb(bi)[0]], writes=[stb])
            for g in range(2):
                dst = nst_d[s, l, d, g * 8:(g + 1) * 8].rearrange("(b h2) p n -> (h2 p) b n", b=4)
                P.dma("sp", dst, st[:, 0:512].rearrange("p (b gn) -> p b gn", b=4)[:, :, g * 64:(g + 1) * 64], reads=[stb])

        def gate_norm():
            w0 = wload(win_d[l][:, C_Z:C_Z + 512], 512)
            w1 = wload(win_d[l][:, C_Z + 512:C_Z + 1024], 512)
            P.dma("sp", rowA[:], mnw_d[l:l + 1, :].partition_broadcast(128), writes=[BrowA])
            for j in range(8):
                st, stb = stg2[j % 2]
                for half, (wt, wbf) in enumerate((w0, w1)):
                    bi = nextbank()
                    pt = banks[bi][0]
                    P.mm([lambda e, kc=kc, pt=pt, wt=wt, j=j: e.matmul(pt[:], hT[:, kc, j * 128:(j + 1) * 128], wt[:, kc, :],
                                                                       start=(kc == 0), stop=(kc == 7)) for kc in range(8)],
                         reads=[wbf, BhT[j]], writes=bkb(bi))
                    t1, t1b = tf[half]
                    P.op("act", lambda e, pt=pt, t1=t1: e.activation(t1[:], pt[:], AF.Silu), reads=bkb(bi), writes=[t1b])
                    P.op("dve", lambda e, t1=t1, st=st, half=half, j=j: e.tensor_tensor(st[:, half * 512:(half + 1) * 512], t1[:], RY[:, j, half * 512:(half + 1) * 512], ALU.mult),
                         reads=[t1b, BRY], writes=[stb])
                P.op("act", lambda e, st=st: e.activation(stg[:, 1, :], st[:], AF.Square, accum_out=small[:, 0:1]), reads=[stb], writes=[Bstg, Bsmall])
                rstd_from(small[:, 0:1], 1024, small[:, 1:2], [Bsmall], Bsmall)
                gn, gnb = tmpb[j % 2]
                P.op("dve", lambda e, st=st, gn=gn: e.scalar_tensor_tensor(gn[:, 0, :], st[:], small[:, 1:2], rowA[:], ALU.mult, ALU.mult),
                     reads=[stb, Bsmall, BrowA], writes=[gnb])
                bi = nextbank(0, 4)
                ptv = banks[bi][0][:].bitcast(BF16)
                P.mm([lambda e, kc=kc, gn=gn, ptv=ptv: e.transpose(ptv[:, kc * 128:(kc + 1) * 128], gn[:, 0, kc * 128:(kc + 1) * 128], identb[:])
                      for kc in range(8)], reads=[gnb, Bidb], writes=bkb(bi))
                P.op("act", lambda e, ptv=ptv, j=j: e.copy(YA[:, :, j * 128:(j + 1) * 128], ptv.rearrange("p (k t) -> p k t", k=8)),
                     reads=bkb(bi), writes=[BRV])

        def merge(b, ysrc, Bys, first):
            for wblk in range(2):
                wg, wgb = wload(win_d[l][:, C_MG + b * 1024 + wblk * 512: C_MG + b * 1024 + (wblk + 1) * 512], 512)
                wr, wrb = wload(wbr_d[l, b][:, wblk * 512:(wblk + 1) * 512], 512)
                for s in range(4):
                    fo = wblk * 4 + s
                    for nb in range(2):
                        bi = nextbank()
                        pt = banks[bi][0]
                        P.mm([lambda e, kc=kc, s=s, nb=nb, pt=pt: e.matmul(
                            pt[:], wg[:, kc, s * 128:(s + 1) * 128], hT[:, kc, nb * 512:(nb + 1) * 512],
                            start=(kc == 0), stop=(kc == 7)) for kc in range(8)], reads=[wgb] + BhT[nb * 4:(nb + 1) * 4], writes=bkb(bi))
                        gt, gtb = tf[2 + (fo + nb) % 2]
                        P.op("act", lambda e, pt=pt, gt=gt: e.activation(gt[:], pt[:], AF.Sigmoid), reads=bkb(bi), writes=[gtb])
                        b2 = nextbank()
                        p2 = banks[b2][0]
                        P.mm([lambda e, kc=kc, s=s, nb=nb, p2=p2: e.matmul(
                            p2[:], wr[:, kc, s * 128:(s + 1) * 128], ysrc[:, kc, nb * 512:(nb + 1) * 512],
                            start=(kc == 0), stop=(kc == 7)) for kc in range(8)], reads=[wrb, Bys], writes=bkb(b2))
                        mdst = MG[:, fo, nb * 512:(nb + 1) * 512]
                        if first:
                            P.op("dve", lambda e, p2=p2, gt=gt, mdst=mdst: e.tensor_tensor(mdst, p2[:], gt[:], ALU.mult),
                                 reads=bkb(b2) + [gtb], writes=[BRY])
                        else:
                            P.op("dve", lambda e, p2=p2, gt=gt: e.tensor_tensor(gt[:], p2[:], gt[:], ALU.mult),
                                 reads=bkb(b2) + [gtb], writes=[gtb])
                            P.op("dve", lambda e, gt=gt, mdst=mdst: e.tensor_tensor(mdst, mdst, gt[:], ALU.add), reads=[gtb, BRY], writes=[BRY])

        def attention():
            proj_fm(C_Q, 1024, ev_rope(RQ, BRA) if kind == 1 else ev_copy(RQ, BRA))
            if kind == 0:
                proj_fm(C_K, 1024, ev_copy(RK, BRK))
                proj_k_tm_out()
                proj_v()
            proj_fm(C_GB, 1024, ev_copy(RG, BRG, AF.Silu))
            if kind == 1:
                for jt in range(2):
                    P.dma("sp", stg[:, jt, :], ck_d[l, jt * 128:(jt + 1) * 128, :], writes=[Bstg])
                for h in range(8):
                    for jt in range(2):
                        bi = nextbank()
                        pt = banks[bi][0]
                        P.mm([lambda e, h=h, jt=jt, pt=pt: e.transpose(pt[:, 0:128], stg[:, jt, h * 128:(h + 1) * 128], identf)],
                             reads=[Bstg, Bcst], writes=[bkb(bi)[0]])
                        P.op("act", lambda e, h=h, jt=jt, pt=pt: e.copy(ctk_ap(h, jt * 128, (jt + 1) * 128), pt[:, 0:128]),
                             reads=[bkb(bi)[0]], writes=[BRA])
                P.dma("pool", ctv[:], cv_d[l].rearrange("(j p) c -> p j c", p=128), writes=[Bctv])
            scale = 64 ** -0.5
            qblocks = [(s * 256, 256) for s in range(4)] if kind == 0 else [(0, 512), (512, 512)]
            it = 0
            for h in range(8):
                if kind == 1:
                    slab, slabB = (RK, BRK) if h % 2 == 0 else (RV, BRV)
                    kt = slab[:, 0:4, :].rearrange("p a t -> p (a t)")
                    vt = slab[:, 4:8, :].rearrange("p a (j d) -> p (a j) d", d=128)
                    g4 = gkv[par][h // 4].rearrange("(r x) t -> x r t", r=4)
                    P.dma("sp", slab[:, 0:4, :], g4[(h % 4) * 128:(h % 4 + 1) * 128, :, :],
                          reads=[Bd[f"gkv{par}"]], writes=[slabB])
                    for half in range(2):
                        gv = gkv[par][2 + half].rearrange("(r j p) c -> p r j c", r=4, j=4, p=128)
                        for r in range(4):
                            P.dma("sp", slab[:, 4 + r, half * 512:(half + 1) * 512].rearrange("p (j d) -> p j d", d=128),
                                  gv[:, r, :, h * 128:(h + 1) * 128], reads=[Bd[f"gkv{par}"]], writes=[slabB])
                for (q0, nq) in qblocks:
                    if kind == 0:
                        s = q0 // 256
                        ktiles = [(RK[:, h, s * 256 + jt * 128: s * 256 + (jt + 1) * 128], [BRK],
                                   RV[:, s * 2 + jt, h * 128:(h + 1) * 128], [BRV]) for jt in range(2)]
                    else:
                        ktiles = [(ctk_ap(h, jt * 128, (jt + 1) * 128), [BRA], ctv[:, jt, h * 128:(h + 1) * 128], [Bctv]) for jt in range(2)]
                        ktiles += [(kt[:, jt * 128:(jt + 1) * 128], [slabB], vt[:, jt, :], [slabB]) for jt in range(32)]
                    nkt = len(ktiles)
                    acc = [4, 5, 6, 7]
                    for ti, (kap, kbufs, vap, vbufs) in enumerate(ktiles):
                        sb0 = (it % 2) * 2
                        pb, pbb = ptb[it % 2]
                        it += 1
                        for m in range(2):
                            ps = banks[sb0 + m][0]
                            P.mm([lambda e, m=m, ps=ps, kap=kap: e.matmul(ps[:, 0:nq], kap[m * 64:(m + 1) * 64, :],
                                                                        RQ[m * 64:(m + 1) * 64, h, q0:q0 + nq], start=True, stop=True)],
                                 reads=kbufs + [BRA], writes=bkb(sb0 + m))
                            P.op("act", lambda e, m=m, ps=ps, pb=pb: e.activation(pb[:, m, 0:nq], ps[:, 0:nq], AF.Exp, scale=scale),
                                 reads=bkb(sb0 + m), writes=[pbb])
                        for m in range(2):
                            po = banks[acc[m]][0]
                            psm = banks[acc[2 + m]][0]
                            P.mm([lambda e, m=m, po=po, pb=pb, vap=vap, ti=ti: e.matmul(po[:, 0:nq], vap, pb[:, m, 0:nq], start=(ti == 0), stop=(ti == nkt - 1)),
                                  lambda e, m=m, psm=psm, pb=pb, ti=ti: e.matmul(psm[:, 0:nq], onesb[:], pb[:, m, 0:nq], start=(ti == 0), stop=(ti == nkt - 1))],
                                 reads=vbufs + [pbb, Bonesb], writes=bkb(acc[m]) + bkb(acc[2 + m]))
                    r1, r1b = tf[2]
                    r2, r2b = tf[3]
                    o1, o1b = tf[0]
                    o2, o2b = tf[1]
                    P.op("dve", lambda e: e.reciprocal(r1[:, 0:nq], banks[6][0][:, 0:nq]), reads=bkb(6), writes=[r1b])
                    P.op("dve", lambda e: e.reciprocal(r2[:, 0:nq], banks[7][0][:, 0:nq]), reads=bkb(7), writes=[r2b])
                    P.op("dve", lambda e: e.tensor_tensor(o1[:, 0:nq], banks[4][0][:, 0:nq], r1[:, 0:nq], ALU.mult), reads=bkb(4) + [r1b], writes=[o1b])
                    P.op("dve", lambda e: e.tensor_tensor(o2[:, 0:nq], banks[5][0][:, 0:nq], r2[:, 0:nq], ALU.mult), reads=bkb(5) + [r2b], writes=[o2b])
                    P.op("dve", lambda e: e.scalar_tensor_tensor(o1[:, 0:nq], o2[:, 0:nq], lamt[:, 261:262], o1[:, 0:nq], ALU.mult, ALU.add),
                         reads=[o1b, o2b, Blamt], writes=[o1b])
                    sq, sqb = tmpb[0]
                    P.op("act", lambda e: e.activation(sq[:, 0, 0:nq], o1[:, 0:nq], AF.Square), reads=[o1b], writes=[sqb])
                    P.mm([lambda e: e.matmul(banks[6][0][:, 0:nq], onesb[:], sq[:, 0, 0:nq], start=True, stop=True)],
                         reads=[sqb, Bonesb], writes=bkb(6))
                    P.op("act", lambda e: e.activation(r1[:, 0:nq], banks[6][0][:, 0:nq], AF.Sqrt, bias=c_eps, scale=1.0 / 128),
                         reads=bkb(6) + [Bcst], writes=[r1b])
                    P.op("dve", lambda e: e.reciprocal(r1[:, 0:nq], r1[:, 0:nq]), reads=[r1b], writes=[r1b])
                    P.op("dve", lambda e: e.scalar_tensor_tensor(o1[:, 0:nq], o1[:, 0:nq], hwT[:, 1:2], r1[:, 0:nq], ALU.mult, ALU.mult),
                         reads=[o1b, BhwT, r1b], writes=[o1b])
                    P.op("dve", lambda e, h=h, q0=q0, nq=nq: e.tensor_tensor(RG[:, h, q0:q0 + nq], o1[:, 0:nq], RG[:, h, q0:q0 + nq], ALU.mult),
                         reads=[o1b, BRG], writes=[BRG])

        def branch_c():
            proj_fm(C_U, 1024, ev_copy(RQ, BRA, AF.Gelu))
            proj_fm(C_GC, 1024, ev_copy(RK, BRK, AF.Silu))

            def ev_sv(j, cbk, pt, bi, n):
                P.op("act", lambda e: e.activation(RV[:, j, cbk * 512:(cbk + 1) * 512], pt[:], AF.Gelu), reads=bkb(bi), writes=[BRV])
            proj_tm(C_SV, 1024, ev_sv)
            P.dma("sp", rowA[:], vnw_d[l:l + 1, :].partition_broadcast(128), writes=[BrowA])
            P.dma("sp", rowB[:], sgb_d[l:l + 1, :].partition_broadcast(128), writes=[BrowB])
            for j in range(8):
                st, stb = stg2[j % 2]
                P.op("act", lambda e, st=st, j=j: e.activation(st[:], RV[:, j, :], AF.Identity, accum_out=small[:, 0:1]), reads=[BRV], writes=[stb, Bsmall])
                P.op("act", lambda e, st=st, j=j: e.activation(st[:], RV[:, j, :], AF.Square, accum_out=small[:, 1:2]), reads=[BRV], writes=[stb, Bsmall])
                P.op("dve", lambda e: e.tensor_single_scalar(small[:, 2:3], small[:, 0:1], 1.0 / 1024, ALU.mult), reads=[Bsmall], writes=[Bsmall])
                P.op("dve", lambda e: e.tensor_tensor(small[:, 3:4], small[:, 2:3], small[:, 2:3], ALU.mult), reads=[Bsmall], writes=[Bsmall])
                P.op("dve", lambda e: e.scalar_tensor_tensor(small[:, 4:5], small[:, 1:2], 1.0 / 1024, small[:, 3:4], ALU.mult, ALU.subtract),
                     reads=[Bsmall], writes=[Bsmall])
                P.op("act", lambda e: e.activation(small[:, 5:6], small[:, 4:5], AF.Sqrt, bias=c_eps, scale=1.0), reads=[Bsmall, Bcst], writes=[Bsmall])
                P.op("dve", lambda e: e.reciprocal(small[:, 5:6], small[:, 5:6]), reads=[Bsmall], writes=[Bsmall])
                P.op("dve", lambda e, st=st, j=j: e.tensor_scalar(st[:], RV[:, j, :], small[:, 2:3], small[:, 5:6], ALU.subtract, ALU.mult),
                     reads=[BRV, Bsmall], writes=[stb])
                vn, vnb = tmpb[j % 2]
                P.op("dve", lambda e, vn=vn, st=st: e.tensor_tensor(vn[:, 0, :], st[:], rowA[:], ALU.mult), reads=[stb, BrowA], writes=[vnb])
                b0 = (nextbank(0, 4) // 2) * 2
                for half in range(2):
                    pt = banks[b0 + half][0]
                    P.mm([lambda e, g=g, pt=pt, vn=vn: e.matmul(pt[:, (g % 4) * 128:(g % 4 + 1) * 128], vn[:, 0, g * 128:(g + 1) * 128], wsT[:, g, :],
                                                             start=True, stop=True) for g in range(half * 4, half * 4 + 4)],
                         reads=[vnb, BwsT], writes=bkb(b0 + half))
                    t1, t1b = tf[half]
                    t1v = t1[:].rearrange("p (g t) -> p g t", g=4)
                    P.op("dve", lambda e, pt=pt, t1v=t1v, half=half: e.tensor_tensor(
                        t1v, pt[:].rearrange("p (g t) -> p g t", g=4), rowB[:, half * 512:(half + 1) * 512].rearrange("p (g t) -> p g t", g=4), ALU.add),
                        reads=bkb(b0 + half) + [BrowB], writes=[t1b])
                    dst = RQ[:, half * 4:half * 4 + 4, j * 128:(j + 1) * 128]
                    P.op("dve", lambda e, t1v=t1v, dst=dst: e.tensor_tensor(t1v, t1v, dst, ALU.mult), reads=[t1b, BRA], writes=[t1b])
                    P.op("dve", lambda e, t1v=t1v, dst=dst, half=half, j=j: e.tensor_tensor(dst, t1v, RK[:, half * 4:half * 4 + 4, j * 128:(j + 1) * 128], ALU.mult),
                         reads=[t1b, BRK], writes=[BRA])

        def out_proj():
            P.dma("sp", rowB[:], bmod_d[l:l + 1, 2048:3072].partition_broadcast(128), writes=[BrowB])
            P.dma("sp", rowA[:], post_d[l:l + 1, :].partition_broadcast(128), writes=[BrowA])
            for nb in range(2):
                wt, wbf = wload(wmod_d[l][:, 2048 + nb * 512: 2048 + (nb + 1) * 512], 512)
                bi = nextbank()
                pt = banks[bi][0]
                P.mm([lambda e, kc=kc, pt=pt, wt=wt: e.matmul(pt[:], scRep[:, c, kc, :], wt[:, kc, :],
                                                              start=(kc == 0), stop=(kc == 7)) for kc in range(8)],
                     reads=[wbf, BscRep], writes=bkb(bi))
                P.op("dve", lambda e, nb=nb, pt=pt: e.tensor_tensor(gpw[:, nb * 512:(nb + 1) * 512], pt[:], rowB[:, nb * 512:(nb + 1) * 512], ALU.add),
                     reads=bkb(bi) + [BrowB], writes=[Bgpw])
            P.op("dve", lambda e: e.tensor_tensor(gpw[:], gpw[:], rowA[:], ALU.mult), reads=[Bgpw, BrowA], writes=[Bgpw])
            w0 = wload(wout_d[l][:, 0:512], 512)
            w1 = wload(wout_d[l][:, 512:1024], 512)
            for j in range(8):
                bis = []
                for half, (wt, wbf) in enumerate((w0, w1)):
                    bi = nextbank()
                    pt = banks[bi][0]
                    P.mm([lambda e, kc=kc, pt=pt, wt=wt, j=j: e.matmul(pt[:], MG[:, kc, j * 128:(j + 1) * 128], wt[:, kc, :],
                                                                       start=(kc == 0), stop=(kc == 7)) for kc in range(8)],
                         reads=[wbf, BRY], writes=bkb(bi))
                    bis.append(bi)
                    P.op("act", lambda e, pt=pt, half=half: e.activation(stg[:, 1, half * 512:(half + 1) * 512], pt[:], AF.Square,
                                                                          accum_out=small[:, 8 + half:9 + half]),
                         reads=bkb(bi), writes=[Bstg, Bsmall])
                P.op("dve", lambda e: e.tensor_tensor(small[:, 0:1], small[:, 8:9], small[:, 9:10], ALU.add), reads=[Bsmall], writes=[Bsmall])
                rstd_from(small[:, 0:1], 1024, small[:, 1:2], [Bsmall], Bsmall)
                st, stb = stg2[j % 2]
                P.dma("sp", stg[:, 0, :], xsrc[j * 128:(j + 1) * 128, :], reads=[Bxres] if l > 0 else [], writes=[Bstg])
                for half in range(2):
                    pt = banks[bis[half]][0]
                    P.op("dve", lambda e, pt=pt, half=half, st=st: e.scalar_tensor_tensor(
                        st[:, half * 512:(half + 1) * 512], pt[:], small[:, 1:2], gpw[:, half * 512:(half + 1) * 512], ALU.mult, ALU.mult),
                        reads=bkb(bis[half]) + [Bsmall, Bgpw], writes=[stb])
                P.op("dve", lambda e, st=st: e.tensor_tensor(st[:], st[:], stg[:, 0, :], ALU.add), reads=[stb, Bstg], writes=[stb])
                P.dma("sp", xdst[j * 128:(j + 1) * 128, :], st[:], reads=[stb], writes=[Bxres] if l < DEPTH - 1 else [])

        ck(f"conv{l}{kind}")
        ssd_main()
        ck(f"ssd_main{l}{kind}")
        if kind == 0:
            chain(0, None, [], True, prompt_final)
            chain(1, None, [], True, prompt_final)
        else:
            Bb = Bd[f"bst{par}"]
            chain(0, None, [], True, None)
            P.dma("sp", bst[par][:, 0:512], state[:, 0, :], reads=[Bstate], writes=[Bb])
            chain(1, None, [], True, None)
            P.dma("sp", bst[par][:, 512:1024], state[:, 1, :], reads=[Bstate], writes=[Bb])
            P.op("dve", lambda e: e.reduce_sum(small[:, 0:32], totl[:].rearrange("p j c -> p c j"), axis=AX.X), reads=[Btotl], writes=[Bsmall])
            P.dma("sp", bst[par][:, 1024:1056], small[:, 0:32], reads=[Bsmall], writes=[Bb])
            P.cc("AllGather", GROUPS4, bst[par], gst[par], reads=[Bb], writes=[Bd[f"gst{par}"]])
            for d in range(2):
                for g in range(2):
                    P.dma("sp", stg[:, d, 0:512].rearrange("p (b gn) -> p b gn", b=4)[:, :, g * 64:(g + 1) * 64],
                          sst_d[l, d, g * 8:(g + 1) * 8].rearrange("(b h2) p n -> (h2 p) b n", b=4), writes=[Bstg])
                for blk in range(4):
                    bi = nextbank(0, 4)
                    pt = banks[bi][0]
                    P.mm([lambda e, pt=pt, blk=blk, d=d: e.transpose(pt[:, 0:128], stg[:, d, blk * 128:(blk + 1) * 128], identf)],
                         reads=[Bstg, Bcst], writes=[bkb(bi)[0]])
                    P.op("act", lambda e, pt=pt, blk=blk, d=d: e.copy(hin[:, d, blk * 128:(blk + 1) * 128], pt[:, 0:128]),
                         reads=[bkb(bi)[0]], writes=[Bhin])
            for d in range(2):
                order = [0, 1, 2] if d == 0 else [3, 2, 1]
                for i in order:
                    ucol = rkc[:, 8 + d * 4 + i: 9 + d * 4 + i]
                    P.dma("sp", gall[:], gst[par][i * 128:(i + 1) * 128, :], reads=[Bd[f"gst{par}"]], writes=[Bgall])
                    P.op("act", lambda e, d=d: e.activation(small[:, 0:16], gall[:, 1024 + d * 16: 1040 + d * 16], AF.Exp),
                         reads=[Bgall], writes=[Bsmall])
                    P.op("dve", lambda e, ucol=ucol: e.tensor_scalar(small[:, 0:16], small[:, 0:16], -1.0, ucol, ALU.add, ALU.mult),
                         reads=[Bsmall, Brkc], writes=[Bsmall])
                    P.op("dve", lambda e: e.tensor_single_scalar(small[:, 0:16], small[:, 0:16], 1.0, ALU.add), reads=[Bsmall], writes=[Bsmall])
                    for g in range(2):
                        P.op("dve", lambda e, g=g, d=d: e.tensor_tensor(
                            hin[g * 64:(g + 1) * 64, d, :].rearrange("p (h q) -> p h q", h=8),
                            hin[g * 64:(g + 1) * 64, d, :].rearrange("p (h q) -> p h q", h=8),
                            small[g * 64:(g + 1) * 64, g * 8:g * 8 + 8].unsqueeze(2).to_broadcast([64, 8, 64]), ALU.mult),
                            reads=[Bhin, Bsmall], writes=[Bhin])
                    P.op("dve", lambda e, d=d, ucol=ucol: e.scalar_tensor_tensor(
                        hin[:, d, :], gall[:, d * 512:(d + 1) * 512], ucol, hin[:, d, :], ALU.mult, ALU.add),
                        reads=[Bgall, Brkc, Bhin], writes=[Bhin])
            chain(0, hin[:, 0, :], [Bhin], False, None)
            chain(1, hin[:, 1, :], [Bhin], False, None)
        if l == 0:
            dbg(f"y{kind}", RY[:], [BRY], [128, 8, 1024])
        ck(f"chains{l}{kind}")
        gate_norm()
        if l == 0:
            dbg(f"ya{kind}", YA[:], [BRV], [128, 8, 1024])
        ck(f"gate_norm{l}{kind}")
        merge(0, YA, BRV, True)
        ck(f"merge0{l}{kind}")
        attention()
        if l == 0:
            dbg(f"yb{kind}", RG[:], [BRG], [128, 8, 1024])
        ck(f"attn{l}{kind}")
        merge(1, RG, BRG, False)
        branch_c()
        if l == 0:
            dbg(f"yc{kind}", RQ[:, 0:8, 0:1024], [BRA], [128, 8, 1024])
        ck(f"brc{l}{kind}")
        merge(2, RQ, BRA, False)
        if l == 0:
            dbg(f"mg{kind}", MG[:], [BRY], [128, 8, 1024])
        out_proj()
        ck(f"out{l}{kind}")

    try:
        for l in range(DEPTH):
            layer_prep(l)
            ck(f"prep{l}")
            run_group(l, 0)
            run_group(l, 1)
    except _Stop:
        pass

    P.emit()
    P.stack.close()
    return nc


_NC = None
_STOP = int(os.environ["KSTOP"]) if os.environ.get("KSTOP") else None


def _consts():
    r = np.arange(128)
    c = np.zeros((128, 1024), np.float32)
    c[:, 0:128] = np.eye(128)
    c[:, 128:256] = (r[:, None] <= r[None, :])
    c[:, 256:384] = (r[:, None] >= r[None, :])
    c[:, 384:512] = (r[:, None] > r[None, :])
    c[:, 512:640] = (r[:, None] < r[None, :])
    Rm = np.zeros((128, 128), np.float32)
    for m in range(2):
        for j in range(32):
            Rm[m * 64 + j, m * 64 + 32 + j] = -1.0
            Rm[m * 64 + 32 + j, m * 64 + j] = 1.0
    c[:, 640:768] = Rm.T
    c[:, 768:896] = 1.0
    c[:, 896] = 1.0
    c[:, 897] = EPS
    c[:, 898] = 0.0
    return c


def _rope_tables():
    T_ = 4096
    rows = np.repeat(np.arange(T_ // 64, dtype=np.float32), 64)
    cols = np.tile(np.arange(64, dtype=np.float32), T_ // 64)
    inv = (10000.0 ** (-np.arange(16, dtype=np.float32) / 16)).astype(np.float32)
    ang = np.concatenate([rows[:, None] * inv, cols[:, None] * inv], -1).astype(np.float32)
    return np.cos(ang).astype(np.float32), np.sin(ang).astype(np.float32)


def kernel(**inp):
    global _NC
    if _NC is None:
        _NC = build(_STOP)
    nc = _NC
    f = lambda a: np.ascontiguousarray(np.asarray(a, dtype=np.float32))
    cst = _consts()
    cos, sin = _rope_tables()
    shared = {
        "pre_norm_w": f(inp["pre_norm_w"]), "post_norm_w": f(inp["post_norm_w"]),
        "w_mod": f(inp["w_mod"]), "b_mod": f(inp["b_mod"]), "w_in": f(inp["w_in"]),
        "m_conv_w": f(inp["m_conv_w"]), "m_conv_b": f(inp["m_conv_b"]),
        "m_A_log": f(inp["m_A_log"]).reshape(DEPTH, 32), "m_dt_bias": f(inp["m_dt_bias"]).reshape(DEPTH, 32),
        "m_D": f(inp["m_D"]).reshape(DEPTH, 32), "m_norm_w": f(inp["m_norm_w"]),
        "da_lambda": f(inp["da_lambda"]).reshape(DEPTH, 256), "da_head_norm_w": f(inp["da_head_norm_w"]),
        "sg_vnorm_w": f(inp["sg_vnorm_w"]), "sg_spatial_w": f(inp["sg_spatial_w"]),
        "sg_spatial_b": f(inp["sg_spatial_b"]).reshape(DEPTH, 1024),
        "w_branch": f(inp["w_branch"]), "w_out": f(inp["w_out"]), "cst": cst,
    }
    xp = f(inp["x_prompt"]); xs = f(inp["x_sample"])
    ck = f(inp["cache_k"]).reshape(2, DEPTH, 256, D); cv = f(inp["cache_v"]).reshape(2, DEPTH, 256, D)
    sst = f(inp["state_ssm"]); cc_ = f(inp["c"]); cctx = f(inp["c_ctx"])
    in_maps = []
    for core in range(8):
        b, j = core // 4, core % 4
        rk = np.zeros((128, 16), np.float32)
        if j > 0:
            rk[:, j - 1] = 1.0
        if j < 3:
            rk[:, 4 + j + 1] = 1.0
        for i in range(4):
            rk[:, 8 + i] = 1.0 if i < j else 0.0
            rk[:, 12 + i] = 1.0 if i > j else 0.0
        t0 = j * 1024
        ct = np.tile(cos[t0:t0 + 1024].T, (4, 1))
        stb = np.tile(sin[t0:t0 + 1024].T, (4, 1))
        m = dict(shared)
        m.update({
            "xp": np.ascontiguousarray(xp[core * 4:(core + 1) * 4].reshape(1024, D)),
            "xs": np.ascontiguousarray(xs[b, t0:t0 + 1024]),
            "ck": np.ascontiguousarray(ck[b]), "cv": np.ascontiguousarray(cv[b]),
            "sst": np.ascontiguousarray(sst[b]),
            "cvec": np.ascontiguousarray(np.stack([cctx, cc_[b]], 0)),
            "rkc": rk, "ropec": np.ascontiguousarray(np.stack([ct, stb], 1)),
        })
        in_maps.append(m)
    res = run_bass_kernel_spmd(nc, in_maps, core_ids=list(range(8)))
    R = res.results
    yp = np.concatenate([R[i]["yp"].reshape(4, 256, D) for i in range(8)], 0)
    ys = np.stack([np.concatenate([R[b * 4 + j]["ys"] for j in range(4)], 0) for b in range(2)], 0)
    nk = np.concatenate([R[i]["nk"] for i in range(8)], 0).reshape(32, DEPTH, 256, 8, 128)
    nv = np.concatenate([R[i]["nv"] for i in range(8)], 0).reshape(32, DEPTH, 256, 8, 128)
    nst = np.concatenate([R[i]["nst"] for i in range(8)], 0)
    return (yp.astype(np.float32), ys.astype(np.float32), nk.astype(np.float32), nv.astype(np.float32), nst.astype(np.float32))
```

```python
import contextlib
import math
import os
import numpy as np
import concourse.bass as bass
import concourse.mybir as mybir
from concourse.bass_utils import run_bass_kernel_spmd

F32 = mybir.dt.float32
BF16 = mybir.dt.bfloat16
AF = mybir.ActivationFunctionType
ALU = mybir.AluOpType
AX = mybir.AxisListType

ENGS = ("pe", "act", "dve", "pool", "sp")
NLANES = 32
DEPTH = 4
D = 1024
NIN = 12576
EPS = 1e-6
C_Z, C_XBC, C_DT, C_Q, C_K, C_V, C_GB, C_U, C_SV, C_GC, C_MG = (
    0, 1024, 2304, 2336, 3360, 4384, 5408, 6432, 7456, 8480, 9504)
GROUPS4 = [[0, 1, 2, 3], [4, 5, 6, 7]]


class Buf:
    __slots__ = ("name", "w", "r", "excl")

    def __init__(self, name):
        self.name = name
        self.w = None
        self.r = {}
        self.excl = False


class _Rec:
    def __init__(self):
        self.call = None

    def __getattr__(self, name):
        def f(*a, **k):
            assert self.call is None
            self.call = (name, a, k)
        return f


def _freeze(fn):
    r = _Rec()
    fn(r)
    name, a, k = r.call
    return lambda e: getattr(e, name)(*a, **k)


class Prog:
    def __init__(self, nc):
        self.nc = nc
        self.ops = {e: [] for e in ENGS}
        self.cnt = {e: 0 for e in ENGS}
        self.seen = {e: {} for e in ENGS}
        self.lane_cnt = [0] * NLANES
        self.lane_i = 0
        self.lane_q = {}
        self.cc_cnt = 0
        self.sem = {}
        self.stack = contextlib.ExitStack()
        self.nbuf = 0

    def sb(self, name, shape, dt):
        return self.stack.enter_context(self.nc.sbuf_tensor("sb_" + name, list(shape), dt))

    def ps(self, name, shape, dt):
        return self.stack.enter_context(self.nc.psum_tensor(name, list(shape), dt))

    def buf(self, name=None):
        self.nbuf += 1
        return Buf(name or f"b{self.nbuf}")

    def _waits(self, eng, reads, writes):
        need = {}

        def add(k, v):
            if k == "pe" and eng == "pe":
                return
            if need.get(k, 0) < v:
                need[k] = v
        for b in reads:
            if b.w is not None:
                add(*b.w)
            if b.excl:
                for k, v in b.r.items():
                    if k != eng:
                        add(k, v)
        for b in writes:
            if b.w is not None:
                add(*b.w)
            for k, v in b.r.items():
                add(k, v)
        out = []
        seen = self.seen[eng]
        for k, v in need.items():
            if seen.get(k, 0) < v:
                seen[k] = v
                out.append((k, v))
        return out

    def _mark(self, key, val, reads, writes):
        for b in reads:
            if b.r.get(key, 0) < val:
                b.r[key] = val
        for b in writes:
            b.w = (key, val)
            b.r = {}

    def op(self, eng, fn, reads=(), writes=()):
        waits = self._waits(eng, reads, writes)
        self.cnt[eng] += 1
        self.ops[eng].append((waits, _freeze(fn), (eng, 1)))
        self._mark(eng, self.cnt[eng], reads, writes)

    def mm(self, fns, reads=(), writes=()):
        waits = self._waits("pe", reads, writes)
        self.cnt["pe"] += 1
        n = len(fns)
        for i, fn in enumerate(fns):
            self.ops["pe"].append((waits if i == 0 else [], _freeze(fn), ("pe", 1) if i == n - 1 else None))
        self._mark("pe", self.cnt["pe"], reads, writes)

    def dma(self, q, out, in_, reads=(), writes=(), **kw):
        waits = self._waits(q, reads, writes)
        lo, hi = (0, 20) if q == "sp" else (20, NLANES)
        li = self.lane_q.get(q, 0)
        self.lane_q[q] = li + 1
        lane = lo + li % (hi - lo)
        key = f"d{lane}"
        prev = self.lane_cnt[lane]
        if prev > 0 and self.seen[q].get(key, 0) < prev:
            self.seen[q][key] = prev
            waits = waits + [(key, prev)]
        self.lane_cnt[lane] += 16
        self.ops[q].append((waits, lambda e: e.dma_start(out=out, in_=in_, **kw), (key, 16)))
        self._mark(key, self.lane_cnt[lane], reads, writes)

    def cc(self, kind, groups, in_ap, out_ap, reads=(), writes=()):
        waits = self._waits("pool", reads, writes)
        self.cc_cnt += 1
        self.ops["pool"].append((waits, lambda e: e.collective_compute(
            kind, ALU.bypass, replica_groups=groups, ins=[in_ap], outs=[out_ap]), ("cc", 1)))
        self._mark("cc", self.cc_cnt, reads, writes)

    def emit(self):
        nc = self.nc
        st = self.stack
        keys = list(ENGS[:4]) + [f"d{i}" for i in range(NLANES)] + ["cc"]
        for k in keys:
            self.sem[k] = st.enter_context(nc.semaphore("s_" + k))
        fin = [(f"d{i}", self.lane_cnt[i]) for i in range(NLANES) if self.lane_cnt[i] > 0]
        fin += [(e, self.cnt[e]) for e in ENGS[:4] if self.cnt[e] > 0]
        if self.cc_cnt:
            fin.append(("cc", self.cc_cnt))
        block = st.enter_context(nc.Block())
        sem = self.sem

        def run(e, lst, final=None):
            for waits, fn, inc in lst:
                for k, v in waits:
                    e.wait_ge(sem[k], v)
                ins = fn(e)
                if inc is not None:
                    if inc[0] == "cc":
                        ins.then_inc(sem["cc"])
                    else:
                        ins.then_inc(sem[inc[0]], inc[1])
            if final:
                for k, v in final:
                    e.wait_ge(sem[k], v)

        @block.tensor
        def _(e):
            run(e, self.ops["pe"])

        @block.scalar
        def _(e):
            run(e, self.ops["act"])

        @block.vector
        def _(e):
            run(e, self.ops["dve"])

        @block.gpsimd
        def _(e):
            run(e, self.ops["pool"])

        @block.sync
        def _(e):
            run(e, self.ops["sp"], fin)


class _Stop(Exception):
    pass


def build(stop=None):
    nc = bass.Bass("TRN2", target_bir_lowering=False)
    P = Prog(nc)
    ckn = [0]
    dbg_on = bool(os.environ.get("KDBG"))
    dbg_i = [0]

    def dbg(name, ap, bufs, shape, dt=BF16):
        if not dbg_on:
            return
        dbg_i[0] += 1
        d = nc.dram_tensor(f"dbg_{name}", list(shape), dt, kind="ExternalOutput").ap()
        P.dma("sp", d, ap, reads=bufs)

    sub = int(os.environ["KSUB"]) if os.environ.get("KSUB") else None
    subn = [0]

    def ck2(tag):
        subn[0] += 1
        if sub is not None and subn[0] == sub:
            print("SUBSTOP at", subn[0], tag)
            raise _Stop()

    def ck(tag):
        ckn[0] += 1
        if stop is not None and ckn[0] == stop:
            print("STOP at", ckn[0], tag)
            raise _Stop()

    def din(name, shape, dt=F32):
        return nc.dram_tensor(name, list(shape), dt, kind="ExternalInput").ap()

    def dout(name, shape):
        return nc.dram_tensor(name, list(shape), F32, kind="ExternalOutput").ap()

    xin = [din("xp", [1024, D]), din("xs", [1024, D])]
    ck_d = din("ck", [DEPTH, 256, D])
    cv_d = din("cv", [DEPTH, 256, D])
    sst_d = din("sst", [DEPTH, 2, 16, 64, 64])
    cvec_d = din("cvec", [2, D])
    pre_d = din("pre_norm_w", [DEPTH, D])
    post_d = din("post_norm_w", [DEPTH, D])
    wmod_d = din("w_mod", [DEPTH, D, 3 * D])
    bmod_d = din("b_mod", [DEPTH, 3 * D])
    win_d = din("w_in", [DEPTH, D, NIN])
    cw_d = din("m_conv_w", [DEPTH, 1280, 3])
    cb_d = din("m_conv_b", [DEPTH, 1280])
    alog_d = din("m_A_log", [DEPTH, 32])
    dtb_d = din("m_dt_bias", [DEPTH, 32])
    mD_d = din("m_D", [DEPTH, 32])
    mnw_d = din("m_norm_w", [DEPTH, D])
    lamv_d = din("da_lambda", [DEPTH, 256])
    hnw_d = din("da_head_norm_w", [DEPTH, 128])
    vnw_d = din("sg_vnorm_w", [DEPTH, D])
    sgw_d = din("sg_spatial_w", [DEPTH, 8, 128, 128])
    sgb_d = din("sg_spatial_b", [DEPTH, 1024])
    wbr_d = din("w_branch", [DEPTH, 3, D, D])
    wout_d = din("w_out", [DEPTH, D, D])
    cst_d = din("cst", [128, 1024])
    rkc_d = din("rkc", [128, 16])
    rope_d = din("ropec", [128, 2, 1024])

    yout = [dout("yp", [1024, D]), dout("ys", [1024, D])]
    nk_d = dout("nk", [4, DEPTH, 256, D])
    nv_d = dout("nv", [4, DEPTH, 256, D])
    nst_d = dout("nst", [4, DEPTH, 2, 16, 64, 64])

    xres = [nc.dram_tensor(f"xres{g}", [1024, D], F32).ap() for g in range(2)]
    bh = [nc.dram_tensor(f"bh{i}", [128, 20], BF16).ap() for i in range(2)]
    gh = [nc.dram_tensor(f"gh{i}", [512, 20], BF16).ap() for i in range(2)]
    bkv = [[nc.dram_tensor(f"bkv{i}_{c}", [512, 1024], BF16).ap() for c in range(4)] for i in range(2)]
    gkv = [[nc.dram_tensor(f"gkv{i}_{c}", [2048, 1024], BF16).ap() for c in range(4)] for i in range(2)]
    bst = [nc.dram_tensor(f"bst{i}", [128, 1056], F32).ap() for i in range(2)]
    gst = [nc.dram_tensor(f"gst{i}", [512, 1056], F32).ap() for i in range(2)]
    Bd = {n: P.buf(n) for n in ["xres0", "xres1", "bh0", "bh1", "gh0", "gh1", "bkv0", "bkv1",
                                 "gkv0", "gkv1", "bst0", "bst1", "gst0", "gst1"]}

    def T(name, shape, dt=F32):
        return P.sb(name, shape, dt), P.buf(name)

    cst, Bcst = T("cst", [128, 1024])
    rkc, Brkc = T("rkc", [128, 16])
    ropeb, Brope = T("ropeb", [128, 2, 1024], BF16)
    identb, Bidb = T("identb", [128, 128], BF16)
    rmtb, Brmt = T("rmtb", [128, 128], BF16)
    onesb, Bonesb = T("onesb", [128, 128], BF16)
    identf = cst[:, 0:128]
    LE = cst[:, 128:256]
    GE = cst[:, 256:384]
    GT = cst[:, 384:512]
    LT = cst[:, 512:640]
    onesf = cst[:, 768:896]
    c_one = cst[:, 896:897]
    c_eps = cst[:, 897:898]
    c_zero = cst[:, 898:899]

    scT, BscT = T("scT", [128, 8, 2], BF16)
    scRep, BscRep = T("scRep", [128, 2, 8, 128], BF16)
    modT, BmodT = T("modT", [128, 16, 2])
    bmT, BbmT = T("bmT", [128, 24])
    preT, BpreT = T("preT", [128, 8])
    gmul, Bgmul = T("gmul", [128, 8, 2])
    shiftT, Bshift = T("shiftT", [128, 8, 2])
    lamt, Blamt = T("lamt", [128, 264])
    cwT, BcwT = T("cwT", [128, 10, 3])
    cbT, BcbT = T("cbT", [128, 10])
    arow, Barow = T("arow", [128, 32])
    dtbrow, Bdtb = T("dtbrow", [128, 32])
    drow, Bdrow = T("drow", [128, 48])
    gpw, Bgpw = T("gpw", [128, 1024])
    hwT, BhwT = T("hwT", [128, 2])
    wsT, BwsT = T("wsT", [128, 8, 128], BF16)

    hT, _ = T("hT", [128, 8, 1024], BF16)
    BhT = [P.buf(f"hT{j}") for j in range(8)]
    RA, BRA = T("RA", [128, 10, 1040], BF16)
    RY, BRY = T("RY", [128, 8, 1024], BF16)
    RK, BRK = T("RK", [128, 8, 1024], BF16)
    RV, BRV = T("RV", [128, 8, 1024], BF16)
    RG, BRG = T("RG", [128, 8, 1024], BF16)
    NW = 2
    wb = [T(f"wb{i}", [128, 8, 512], BF16) for i in range(NW)]
    stg, Bstg = T("stg", [128, 2, 1024])
    stg2 = [T(f"stg2_{i}", [128, 1024]) for i in range(2)]
    tmpb = [T(f"tmpb{i}", [128, 2, 1024], BF16) for i in range(2)]
    tf = [T(f"tf{i}", [128, 512]) for i in range(4)]
    small, Bsmall = T("small", [128, 64])
    dtt, Bdtt = T("dtt", [128, 8, 32])
    at, Bat = T("at", [128, 8, 32])
    Et, BEt = T("Et", [128, 8, 64])
    dch, Bdch = T("dch", [128, 8, 32])
    totl, Btotl = T("totl", [128, 8, 32])
    state, Bstate = T("state", [128, 2, 512])
    stateb, Bstateb = T("stateb", [128, 2, 512], BF16)
    cbm = [T(f"cbm{i}", [128, 4, 128]) for i in range(1)]
    lhs = [T(f"lhs{i}", [128, 128]) for i in range(2)]
    ldec = [T(f"ldec{i}", [128, 128]) for i in range(2)]
    wmat = [T(f"wmat{i}", [128, 128], BF16) for i in range(4)]
    hin, Bhin = T("hin", [128, 2, 512])
    gall, Bgall = stg[:].rearrange("p a t -> p (a t)")[:, 0:1056], Bstg
    halo, Bhalo = T("halo", [128, 4, 20], BF16)
    halof, Bhalof = T("halof", [128, 2, 10])
    ptb = [T(f"ptb{i}", [128, 2, 512], BF16) for i in range(2)]
    xeT, Bxe = T("xeT", [128, 2, 1024], BF16)
    rowA, BrowA = T("rowA", [128, 1024])
    rowB, BrowB = T("rowB", [128, 1024])

    banks = []
    for i in range(8):
        t = P.ps(f"pb{i}", [128, 512], F32)
        _b = P.buf(f"pb{i}")
        _b.excl = True
        banks.append((t, [_b, _b, _b, _b]))

    def bkb(i):
        return banks[i][1]

    bank_rr = [0]

    def nextbank(lo=0, hi=8):
        i = lo + bank_rr[0] % (hi - lo)
        bank_rr[0] += 1
        return i

    w_i = [0]

    def wload(src_ap, ncols):
        t, b = wb[w_i[0] % NW]
        w_i[0] += 1
        P.dma("pool", t[:, :, 0:ncols], src_ap.rearrange("(kc p) c -> p kc c", p=128), writes=[b])
        return t, b

    def rstd_from(ssq_ap, n, out_ap, rb, wbuf):
        P.op("act", lambda e: e.activation(out_ap, ssq_ap, AF.Sqrt, bias=c_eps, scale=1.0 / n),
             reads=rb + [Bcst], writes=[wbuf])
        P.op("dve", lambda e: e.reciprocal(out_ap, out_ap), reads=[wbuf], writes=[wbuf])

    P.dma("sp", cst[:], cst_d, writes=[Bcst])
    P.dma("sp", rkc[:], rkc_d, writes=[Brkc])
    P.dma("pool", ropeb[:], rope_d, writes=[Brope])
    P.op("dve", lambda e: e.tensor_copy(identb[:], identf), reads=[Bcst], writes=[Bidb])
    P.op("dve", lambda e: e.tensor_copy(rmtb[:], cst[:, 640:768]), reads=[Bcst], writes=[Brmt])
    P.op("dve", lambda e: e.tensor_copy(onesb[:], onesf), reads=[Bcst], writes=[Bonesb])
    for c in range(2):
        P.dma("sp", modT[:, 0:8, c], cvec_d[c].rearrange("(kc p) -> p kc", p=128), writes=[BmodT],
              allow_slow_non_contiguous=True)
    P.op("act", lambda e: e.activation(scT[:], modT[:, 0:8, :], AF.Silu), reads=[BmodT], writes=[BscT])
    for c in range(2):
        P.op("dve", lambda e, c=c: e.tensor_copy(scRep[:, c], scT[:, :, c:c + 1].to_broadcast([128, 8, 128])),
             reads=[BscT], writes=[BscRep])

    def layer_prep(l):
        lam_init = 0.8 - 0.6 * math.exp(-0.3 * l)
        P.dma("sp", bmT[:], bmod_d[l].rearrange("(b p) -> p b", p=128), writes=[BbmT], allow_slow_non_contiguous=True)
        P.dma("sp", preT[:], pre_d[l].rearrange("(b p) -> p b", p=128), writes=[BpreT], allow_slow_non_contiguous=True)
        for wblk in range(4):
            wt, wbf = wload(wmod_d[l][:, wblk * 512:(wblk + 1) * 512], 512)
            for s in range(4):
                blk = wblk * 4 + s
                bi = nextbank()
                pt = banks[bi][0]
                P.mm([lambda e, kc=kc, s=s, pt=pt, wt=wt: e.matmul(pt[:, 0:2], wt[:, kc, s * 128:(s + 1) * 128],
                                                                  scT[:, kc, :], start=(kc == 0), stop=(kc == 7))
                      for kc in range(8)], reads=[wbf, BscT], writes=[bkb(bi)[0]])
                P.op("dve", lambda e, blk=blk, pt=pt: e.tensor_single_scalar(modT[:, blk, :], pt[:, 0:2], bmT[:, blk:blk + 1], ALU.add),
                     reads=[bkb(bi)[0], BbmT], writes=[BmodT])
        P.op("dve", lambda e: e.tensor_copy(shiftT[:], modT[:, 0:8, :]), reads=[BmodT], writes=[Bshift])
        P.op("dve", lambda e: e.tensor_single_scalar(gmul[:], modT[:, 8:16, :], 1.0, ALU.add), reads=[BmodT], writes=[Bgmul])
        P.op("dve", lambda e: e.tensor_tensor(gmul[:], gmul[:], preT[:].unsqueeze(2).to_broadcast([128, 8, 2]), ALU.mult),
             reads=[Bgmul, BpreT], writes=[Bgmul])
        P.dma("sp", lamt[:, 0:256], lamv_d[l:l + 1, :].partition_broadcast(128), writes=[Blamt])
        P.op("dve", lambda e: e.tensor_tensor(lamt[:, 0:64], lamt[:, 0:64], lamt[:, 64:128], ALU.mult), reads=[Blamt], writes=[Blamt])
        P.op("dve", lambda e: e.tensor_tensor(lamt[:, 128:192], lamt[:, 128:192], lamt[:, 192:256], ALU.mult), reads=[Blamt], writes=[Blamt])
        P.op("dve", lambda e: e.reduce_sum(lamt[:, 256:257], lamt[:, 0:64], axis=AX.X), reads=[Blamt], writes=[Blamt])
        P.op("dve", lambda e: e.reduce_sum(lamt[:, 257:258], lamt[:, 128:192], axis=AX.X), reads=[Blamt], writes=[Blamt])
        P.op("act", lambda e: e.activation(lamt[:, 258:260], lamt[:, 256:258], AF.Exp), reads=[Blamt], writes=[Blamt])
        P.op("dve", lambda e: e.tensor_tensor(lamt[:, 260:261], lamt[:, 259:260], lamt[:, 258:259], ALU.subtract), reads=[Blamt], writes=[Blamt])
        P.op("dve", lambda e: e.tensor_single_scalar(lamt[:, 261:262], lamt[:, 260:261], -lam_init, ALU.add), reads=[Blamt], writes=[Blamt])
        P.dma("sp", cwT[:], cw_d[l].rearrange("(kc p) k -> p kc k", p=128), writes=[BcwT], allow_slow_non_contiguous=True)
        P.dma("sp", cbT[:], cb_d[l].rearrange("(kc p) -> p kc", p=128), writes=[BcbT], allow_slow_non_contiguous=True)
        P.dma("sp", arow[:], alog_d[l:l + 1, :].partition_broadcast(128), writes=[Barow])
        P.op("act", lambda e: e.activation(arow[:], arow[:], AF.Exp), reads=[Barow], writes=[Barow])
        P.op("dve", lambda e: e.tensor_single_scalar(arow[:], arow[:], -1.0, ALU.mult), reads=[Barow], writes=[Barow])
        P.dma("sp", dtbrow[:], dtb_d[l:l + 1, :].partition_broadcast(128), writes=[Bdtb])
        P.dma("sp", drow[:, 0:32], mD_d[l:l + 1, :].partition_broadcast(128), writes=[Bdrow])
        P.op("dve", lambda e: e.tensor_tensor(drow[:, 32:48], drow[:, 0:16], drow[:, 16:32], ALU.add), reads=[Bdrow], writes=[Bdrow])
        P.dma("sp", hwT[:, 0:1], hnw_d[l].rearrange("(d o) -> d o", o=1), writes=[BhwT], allow_slow_non_contiguous=True)
        P.op("dve", lambda e: e.tensor_single_scalar(hwT[:, 1:2], hwT[:, 0:1], 1.0 - lam_init, ALU.mult), reads=[BhwT], writes=[BhwT])
        P.dma("sp", stg[:, 0, :].rearrange("p (g s) -> p g s", g=8), sgw_d[l].rearrange("g t s -> t g s"), writes=[Bstg])
        for g in range(8):
            bi = nextbank()
            pt = banks[bi][0]
            P.mm([lambda e, g=g, pt=pt: e.transpose(pt[:, 0:128], stg[:, 0, g * 128:(g + 1) * 128], identf)],
                 reads=[Bstg, Bcst], writes=[bkb(bi)[0]])
            P.op("act", lambda e, g=g, pt=pt: e.copy(wsT[:, g, :], pt[:, 0:128]), reads=[bkb(bi)[0]], writes=[BwsT])

    def run_group(l, kind):
        par = l % 2
        lam_init = 0.8 - 0.6 * math.exp(-0.3 * l)
        xsrc = xin[kind] if l == 0 else xres[kind]
        xdst = yout[kind] if l == DEPTH - 1 else xres[kind]
        Bxres = Bd[f"xres{kind}"]
        c = kind
        nseq = 4 if kind == 0 else 1
        cps = 2 if kind == 0 else 8
        RAv = RA[:].rearrange("p k (s t) -> p k s t", s=4)
        RQ = RA
        def ctk_ap(h, a, b):
            return RA[:, 8 + h // 4, (h % 4) * 256 + a:(h % 4) * 256 + b]
        ctv = tmpb[1][0]
        Bctv = tmpb[1][1]
        SFb = RK[:, :, 0:512]
        RSb = RK[:, :, 512:1024]
        YA = RV
        MG = RY

        def xc(kc, j, p0=0, p1=128):
            return RAv[p0:p1, kc, j // 2, 2 + (j % 2) * 128: 2 + (j % 2) * 128 + 128]

        for j in range(8):
            P.dma("sp", stg[:, 0, :], xsrc[j * 128:(j + 1) * 128, :], reads=[Bxres] if l > 0 else [], writes=[Bstg])
            P.op("act", lambda e: e.activation(stg[:, 1, :], stg[:, 0, :], AF.Square, accum_out=small[:, 0:1]),
                 reads=[Bstg], writes=[Bstg, Bsmall])
            rstd_from(small[:, 0:1], 1024, small[:, 1:2], [Bsmall], Bsmall)
            tb, tbb = tmpb[j % 2]
            P.op("dve", lambda e, tb=tb: e.tensor_single_scalar(tb[:, 0, :], stg[:, 0, :], small[:, 1:2], ALU.mult),
                 reads=[Bstg, Bsmall], writes=[tbb])
            bi = nextbank()
            ptv = banks[bi][0][:].bitcast(BF16)
            P.mm([lambda e, kc=kc, tb=tb, ptv=ptv: e.transpose(ptv[:, kc * 128:(kc + 1) * 128], tb[:, 0, kc * 128:(kc + 1) * 128], identb[:])
                  for kc in range(8)], reads=[tbb, Bidb], writes=bkb(bi))
            for kc in range(8):
                P.op("act", lambda e, kc=kc, j=j, ptv=ptv: e.activation(hT[:, kc, j * 128:(j + 1) * 128], ptv[:, kc * 128:(kc + 1) * 128],
                                                                       AF.Identity, scale=gmul[:, kc, c:c + 1], bias=shiftT[:, kc, c:c + 1]),
                     reads=bkb(bi) + [Bgmul, Bshift], writes=[BhT[j]])

        ck(f"P0{l}{kind}")

        def proj_fm(c0, ncols, evac):
            blocks = [(d0, min(512, ncols - d0)) for d0 in range(0, ncols, 512)]
            loaded = {0: wload(win_d[l][:, c0:c0 + blocks[0][1]], blocks[0][1])}
            for bi_, (done, n) in enumerate(blocks):
                if bi_ + 1 < len(blocks):
                    d1, n1 = blocks[bi_ + 1]
                    loaded[bi_ + 1] = wload(win_d[l][:, c0 + d1:c0 + d1 + n1], n1)
                wt, wbf = loaded.pop(bi_)
                for s in range((n + 127) // 128):
                    m = min(128, n - s * 128)
                    for nb in range(2):
                        bi = nextbank()
                        pt = banks[bi][0]
                        P.mm([lambda e, kc=kc, s=s, m=m, nb=nb, pt=pt, wt=wt: e.matmul(
                            pt[0:m, :], wt[:, kc, s * 128:s * 128 + m], hT[:, kc, nb * 512:(nb + 1) * 512],
                            start=(kc == 0), stop=(kc == 7)) for kc in range(8)],
                            reads=[wbf] + BhT[nb * 4:(nb + 1) * 4], writes=bkb(bi))
                        evac((done // 128) + s, nb, pt, bi)

        def proj_tm(c0, ncols, evac):
            blocks = [(d0, min(512, ncols - d0)) for d0 in range(0, ncols, 512)]
            loaded = {0: wload(win_d[l][:, c0:c0 + blocks[0][1]], blocks[0][1])}
            for bi_, (done, n) in enumerate(blocks):
                if bi_ + 1 < len(blocks):
                    d1, n1 = blocks[bi_ + 1]
                    loaded[bi_ + 1] = wload(win_d[l][:, c0 + d1:c0 + d1 + n1], n1)
                wt, wbf = loaded.pop(bi_)
                for j in range(8):
                    bi = nextbank()
                    pt = banks[bi][0]
                    P.mm([lambda e, kc=kc, j=j, n=n, pt=pt, wt=wt: e.matmul(
                        pt[:, 0:n], hT[:, kc, j * 128:(j + 1) * 128], wt[:, kc, 0:n],
                        start=(kc == 0), stop=(kc == 7)) for kc in range(8)],
                        reads=[wbf, BhT[j]], writes=bkb(bi))
                    evac(j, done // 512, pt, bi, n)

        def ev_rope(dst, Bdst):
            def ev(cb, nb, pt, bi):
                tb, tbb = tmpb[0]
                P.op("act", lambda e: e.copy(tb[:, 0, 0:512], pt[:]), reads=bkb(bi), writes=[tbb])
                b2 = nextbank()
                p2 = banks[b2][0]
                P.mm([lambda e: e.matmul(p2[:], rmtb[:], tb[:, 0, 0:512], start=True, stop=True)],
                     reads=[tbb, Brmt], writes=bkb(b2))
                t1, t1b = tf[0]
                t2, t2b = tf[1]
                P.op("dve", lambda e: e.tensor_tensor(t1[:], pt[:], ropeb[:, 0, nb * 512:(nb + 1) * 512], ALU.mult),
                     reads=bkb(bi) + [Brope], writes=[t1b])
                P.op("dve", lambda e: e.tensor_tensor(t2[:], p2[:], ropeb[:, 1, nb * 512:(nb + 1) * 512], ALU.mult),
                     reads=bkb(b2) + [Brope], writes=[t2b])
                P.op("dve", lambda e: e.tensor_tensor(dst[:, cb, nb * 512:(nb + 1) * 512], t1[:], t2[:], ALU.add),
                     reads=[t1b, t2b], writes=[Bdst])
            return ev

        def ev_copy(dst, Bdst, func=None):
            def ev(cb, nb, pt, bi):
                if func is None:
                    P.op("act", lambda e: e.copy(dst[:, cb, nb * 512:(nb + 1) * 512], pt[:]), reads=bkb(bi), writes=[Bdst])
                else:
                    P.op("act", lambda e: e.activation(dst[:, cb, nb * 512:(nb + 1) * 512], pt[:], func), reads=bkb(bi), writes=[Bdst])
            return ev

        def proj_v():
            def ev_v(j, cbk, pt, bi, n):
                P.op("act", lambda e: e.copy(RV[:, j, cbk * 512:(cbk + 1) * 512], pt[:]), reads=bkb(bi), writes=[BRV])
                if kind == 0:
                    st, stb = stg2[(j * 2 + cbk) % 2]
                    P.op("dve", lambda e: e.tensor_copy(st[:, 0:512], pt[:]), reads=bkb(bi), writes=[stb])
                    P.dma("sp", nv_d[j // 2, l, (j % 2) * 128:(j % 2 + 1) * 128, cbk * 512:(cbk + 1) * 512], st[:, 0:512], reads=[stb])
            proj_tm(C_V, 1024, ev_v)

        def proj_k_tm_out():
            def ev_k(j, cbk, pt, bi, n):
                st, stb = stg2[(j * 2 + cbk) % 2]
                P.op("dve", lambda e: e.tensor_copy(st[:, 0:512], pt[:]), reads=bkb(bi), writes=[stb])
                P.dma("sp", nk_d[j // 2, l, (j % 2) * 128:(j % 2 + 1) * 128, cbk * 512:(cbk + 1) * 512], st[:, 0:512], reads=[stb])
            proj_tm(C_K, 1024, ev_k)

        if kind == 1:
            proj_fm(C_K, 1024, ev_rope(RK, BRK))
            proj_v()
            Bb = Bd[f"bkv{par}"]
            for cch in range(2):
                P.dma("sp", bkv[par][cch].rearrange("(h p) t -> p h t", p=128), RK[:, cch * 4:(cch + 1) * 4, :], reads=[BRK], writes=[Bb])
                P.dma("sp", bkv[par][2 + cch].rearrange("(j p) c -> p j c", p=128), RV[:, cch * 4:(cch + 1) * 4, :], reads=[BRV], writes=[Bb])
            for cch in range(4):
                P.cc("AllGather", GROUPS4, bkv[par][cch], gkv[par][cch], reads=[Bb], writes=[Bd[f"gkv{par}"]])

        ck(f"kvsend{l}{kind}")
        def ev_xbc(cb, nb, pt, bi):
            P.op("act", lambda e: e.copy(RAv[:, cb, nb * 2:nb * 2 + 2, 2:258], pt[:].rearrange("p (s t) -> p s t", s=2)),
                 reads=bkb(bi), writes=[BRA])
        proj_fm(C_XBC, 1280, ev_xbc)

        def ev_dt(j, cbk, pt, bi, n):
            P.op("dve", lambda e: e.tensor_tensor(small[:, 32:64], pt[:, 0:32], dtbrow[:], ALU.add), reads=bkb(bi) + [Bdtb], writes=[Bsmall])
            P.op("dve", lambda e: e.tensor_single_scalar(small[:, 0:32], small[:, 32:64], 30.0, ALU.min), reads=[Bsmall], writes=[Bsmall])
            P.op("act", lambda e: e.activation(small[:, 0:32], small[:, 0:32], AF.Exp), reads=[Bsmall], writes=[Bsmall])
            P.op("act", lambda e: e.activation(small[:, 0:32], small[:, 0:32], AF.Ln, bias=c_one), reads=[Bsmall, Bcst], writes=[Bsmall])
            P.op("dve", lambda e: e.tensor_tensor(dtt[:, j, :], small[:, 32:64], small[:, 0:32], ALU.max), reads=[Bsmall], writes=[Bdtt])
            P.op("dve", lambda e: e.tensor_tensor(at[:, j, :], dtt[:, j, :], arow[:], ALU.mult), reads=[Bdtt, Barow], writes=[Bat])
        proj_tm(C_DT, 32, ev_dt)

        ck(f"xbcdt{l}{kind}")
        if kind == 0:
            P.op("dve", lambda e: e.memset(RAv[:, :, :, 1:2], 0.0), writes=[BRA])
            P.op("dve", lambda e: e.memset(RAv[:, :, :, 258:259], 0.0), writes=[BRA])
        else:
            P.op("dve", lambda e: e.tensor_copy(RAv[:, :, 1:4, 1:2], RAv[:, :, 0:3, 257:258]), reads=[BRA], writes=[BRA])
            P.op("dve", lambda e: e.tensor_copy(RAv[:, :, 0:3, 258:259], RAv[:, :, 1:4, 2:3]), reads=[BRA], writes=[BRA])
            P.op("dve", lambda e: e.tensor_copy(halo[:, 0, 0:10], RAv[:, :, 0, 2]), reads=[BRA], writes=[Bhalo])
            P.op("dve", lambda e: e.tensor_copy(halo[:, 0, 10:20], RAv[:, :, 3, 257]), reads=[BRA], writes=[Bhalo])
            P.dma("sp", bh[par], halo[:, 0, :], reads=[Bhalo], writes=[Bd[f"bh{par}"]])
            P.cc("AllGather", GROUPS4, bh[par], gh[par], reads=[Bd[f"bh{par}"]], writes=[Bd[f"gh{par}"]])
            P.dma("sp", halo[:], gh[par].rearrange("(r p) c -> p r c", p=128), reads=[Bd[f"gh{par}"]], writes=[Bhalo])
            for side in range(2):
                for r in range(4):
                    src = halo[:, r, 10:20] if side == 0 else halo[:, r, 0:10]
                    selc = rkc[:, side * 4 + r: side * 4 + r + 1]
                    if r == 0:
                        P.op("dve", lambda e, src=src, selc=selc, side=side: e.tensor_single_scalar(halof[:, side, :], src, selc, ALU.mult),
                             reads=[Bhalo, Brkc], writes=[Bhalof])
                    else:
                        P.op("dve", lambda e, src=src, selc=selc, side=side: e.scalar_tensor_tensor(
                            halof[:, side, :], src, selc, halof[:, side, :], ALU.mult, ALU.add),
                            reads=[Bhalo, Brkc, Bhalof], writes=[Bhalof])
            P.op("dve", lambda e: e.tensor_copy(RAv[:, :, 0, 1], halof[:, 0, :]), reads=[Bhalof], writes=[BRA])
            P.op("dve", lambda e: e.tensor_copy(RAv[:, :, 3, 258], halof[:, 1, :]), reads=[Bhalof], writes=[BRA])

        ck(f"halo{l}{kind}")
        for kc in range(10):
            for hb in range(2):
                t1, t1b = tf[(kc * 2 + hb) % 2]
                t1v = t1[:].rearrange("p (s t) -> p s t", s=2)
                sl = slice(hb * 2, hb * 2 + 2)
                P.op("dve", lambda e, kc=kc, t1v=t1v, sl=sl: e.tensor_single_scalar(t1v, RAv[:, kc, sl, 1:257], cwT[:, kc, 0:1], ALU.mult),
                     reads=[BRA, BcwT], writes=[t1b])
                P.op("dve", lambda e, kc=kc, t1v=t1v, sl=sl: e.scalar_tensor_tensor(t1v, RAv[:, kc, sl, 2:258], cwT[:, kc, 1:2], t1v, ALU.mult, ALU.add),
                     reads=[BRA, BcwT, t1b], writes=[t1b])
                P.op("dve", lambda e, kc=kc, t1v=t1v, sl=sl: e.scalar_tensor_tensor(t1v, RAv[:, kc, sl, 3:259], cwT[:, kc, 2:3], t1v, ALU.mult, ALU.add),
                     reads=[BRA, BcwT, t1b], writes=[t1b])
                P.op("act", lambda e, kc=kc, t1v=t1v, sl=sl: e.activation(RAv[:, kc, sl, 2:258], t1v, AF.Silu, bias=cbT[:, kc:kc + 1]),
                     reads=[t1b, BcbT], writes=[BRA])

        def ssd_main():
            for j in range(8):
                xt, xtb = tmpb[0]
                xd, xdb = tmpb[1]
                bi = 3
                ptv = banks[bi][0][:].bitcast(BF16)
                P.mm([lambda e, kc=kc, ptv=ptv, j=j: e.transpose(ptv[:, kc * 128:(kc + 1) * 128], xc(kc, j), identb[:])
                      for kc in range(8)], reads=[BRA, Bidb], writes=bkb(bi))
                P.op("act", lambda e, ptv=ptv: e.copy(xt[:, 0, :], ptv), reads=bkb(bi), writes=[xtb])
                P.mm([lambda e, ptv=ptv, j=j: e.transpose(ptv[:, 0:128], xc(8, j), identb[:])],
                     reads=[BRA, Bidb], writes=bkb(bi))
                P.op("act", lambda e, ptv=ptv: e.copy(xt[:, 1, 0:128], ptv[:, 0:128]), reads=bkb(bi), writes=[xtb])
                ck2("ssd_T")
                pc = banks[2][0]
                aj = at[:, j, :]
                P.mm([lambda e, aj=aj: e.matmul(pc[:, 256:272], LE, aj[:, 0:16], start=True, stop=True),
                      lambda e, aj=aj: e.matmul(pc[:, 272:288], GE, aj[:, 16:32], start=True, stop=True),
                      lambda e, aj=aj: e.matmul(pc[:, 288:304], GT, aj[:, 0:16], start=True, stop=True),
                      lambda e, aj=aj: e.matmul(pc[:, 304:320], LT, aj[:, 16:32], start=True, stop=True),
                      lambda e, aj=aj: e.matmul(pc[:, 384:416], onesf, aj, start=True, stop=True)],
                     reads=[Bat, Bcst], writes=[bkb(2)[2], bkb(2)[3]])
                P.op("act", lambda e, j=j: e.activation(Et[:, j, :], pc[:, 256:320], AF.Exp), reads=[bkb(2)[2]], writes=[BEt])
                P.op("act", lambda e, j=j: e.activation(dch[:, j, :], pc[:, 384:416], AF.Exp), reads=[bkb(2)[3]], writes=[Bdch])
                P.op("dve", lambda e, j=j: e.tensor_copy(totl[:, j, :], pc[:, 384:416]), reads=[bkb(2)[3]], writes=[Btotl])
                ck2("ssd_cum")
                xtv = xt[:, 0, :].rearrange("p (h q) -> p h q", h=16)
                for d in range(2):
                    P.op("dve", lambda e, d=d, j=j: e.tensor_tensor(xd[:, d, :].rearrange("p (h q) -> p h q", h=16), xtv,
                                                                   dtt[:, j, d * 16:(d + 1) * 16].unsqueeze(2).to_broadcast([128, 16, 64]), ALU.mult),
                         reads=[xtb, Bdtt], writes=[xdb])
                for d in range(2):
                    P.op("dve", lambda e, d=d, j=j: e.tensor_tensor(
                        xeT[:, d, :].rearrange("p (h q) -> p h q", h=16), xd[:, d, :].rearrange("p (h q) -> p h q", h=16),
                        Et[:, j, 32 + d * 16: 48 + d * 16].unsqueeze(2).to_broadcast([128, 16, 64]), ALU.mult),
                        reads=[xdb, BEt], writes=[Bxe])
                ck2("ssd_xdt")
                pcg = [banks[2][0][:, 0:128], banks[3][0][:, 0:128]]
                P.mm([lambda e, g=g, j=j: e.matmul(pcg[g], xc(8, j, g * 64, (g + 1) * 64),
                                                  xc(9, j, g * 64, (g + 1) * 64), start=True, stop=True) for g in range(2)],
                     reads=[BRA], writes=[bkb(2)[0], bkb(3)[0]])
                cb_t, cb_b = cbm[0]
                for g in range(2):
                    for d in range(2):
                        P.op("dve", lambda e, g=g, d=d: e.tensor_tensor(cb_t[:, g * 2 + d, :], pcg[g], LE if d == 0 else GE, ALU.mult),
                             reads=[bkb(2 + g)[0], Bcst], writes=[cb_b])
                ck2("ssd_cb")
                ybank = (4, 5)
                idx = 0
                for h in range(16):
                    for d in range(2):
                        g = h // 8
                        lt_, lb_ = lhs[idx % 2]
                        ld_, ldb_ = ldec[idx % 2]
                        wm_, wmb_ = wmat[idx % 4]
                        dbi, dq = idx % 2, 0
                        pd = banks[dbi][0][:, dq * 128:(dq + 1) * 128]
                        P.op("dve", lambda e, lt_=lt_, d=d, h=h, j=j: e.tensor_single_scalar(lt_[:], GT if d == 0 else LT, at[:, j, d * 16 + h:d * 16 + h + 1], ALU.mult),
                             reads=[Bcst, Bat], writes=[lb_])
                        P.mm([lambda e, pd=pd, lt_=lt_, d=d: e.matmul(pd, lt_[:], LE if d == 0 else GE, start=True, stop=True)],
                             reads=[lb_, Bcst], writes=[bkb(dbi)[dq]])
                        P.op("act", lambda e, pd=pd, ld_=ld_: e.activation(ld_[:], pd, AF.Exp), reads=[bkb(dbi)[dq]], writes=[ldb_])
                        P.op("dve", lambda e, ld_=ld_, wm_=wm_, g=g, d=d: e.tensor_tensor(wm_[:], ld_[:], cb_t[:, g * 2 + d, :], ALU.mult),
                             reads=[ldb_, cb_b], writes=[wmb_])
                        yb = ybank[h // 8]
                        py = banks[yb][0][:, (h % 8) * 64:(h % 8 + 1) * 64]
                        P.mm([lambda e, py=py, wm_=wm_, d=d, h=h: e.matmul(py, wm_[:], xd[:, d, h * 64:(h + 1) * 64], start=(d == 0), stop=(d == 1))],
                             reads=[wmb_, xdb], writes=[bkb(yb)[(h % 8) // 2]])
                        idx += 1
                ck2("ssd_y")
                t1, t1b = tf[1]
                for half in range(2):
                    P.op("dve", lambda e, half=half: e.tensor_tensor(
                        t1[:].rearrange("p (h q) -> p h q", h=8), xt[:, 0, half * 512:(half + 1) * 512].rearrange("p (h q) -> p h q", h=8),
                        drow[:, 32 + half * 8: 40 + half * 8].unsqueeze(2).to_broadcast([128, 8, 64]), ALU.mult),
                        reads=[xtb, Bdrow], writes=[t1b])
                    P.op("dve", lambda e, half=half, j=j: e.tensor_tensor(RY[:, j, half * 512:(half + 1) * 512], banks[4 + half][0][:], t1[:], ALU.add),
                         reads=[t1b] + bkb(4 + half), writes=[BRY])
                ck2("ssd_dskip")
                for d in range(2):
                    for g in range(2):
                        sbk = 6 + g
                        ps_ = banks[sbk][0]
                        P.mm([lambda e, ps_=ps_, d=d, g=g: e.matmul(ps_[:], xt[:, 1, 0:128], xeT[:, d, g * 512:(g + 1) * 512], start=True, stop=True)],
                             reads=[xtb, Bxe], writes=bkb(sbk))
                        dstS = SFb if d == 0 else RSb
                        P.op("act", lambda e, ps_=ps_, g=g, j=j, dstS=dstS: e.copy(dstS[g * 64:(g + 1) * 64, j, :], ps_[g * 64:(g + 1) * 64, :]),
                             reads=bkb(sbk), writes=[BRK])
                ck2("ssd_S")

        def chain(d, init_ap, init_bufs, addS, finals):
            Ssrc = SFb if d == 0 else RSb
            ecol = 0 if d == 0 else 16
            for s in range(nseq):
                chunks = list(range(s * cps, (s + 1) * cps))
                if d == 1:
                    chunks = chunks[::-1]
                have = False
                for ci, j in enumerate(chunks):
                    if ci == 0 and init_ap is not None:
                        P.op("dve", lambda e: e.tensor_copy(state[:, d, :], init_ap), reads=init_bufs, writes=[Bstate])
                        have = True
                    if have:
                        P.op("act", lambda e: e.copy(stateb[:, d, :], state[:, d, :]), reads=[Bstate], writes=[Bstateb])
                        for g in range(2):
                            bo = 6 + g
                            po = banks[bo][0]
                            P.mm([lambda e, po=po, g=g, j=j: e.matmul(po[:], xc(9, j, g * 64, (g + 1) * 64),
                                                                     stateb[g * 64:(g + 1) * 64, d, :], start=True, stop=True)],
                                 reads=[BRA, Bstateb], writes=bkb(bo))
                            t1, t1b = tf[2 + g]
                            P.op("dve", lambda e, po=po, g=g, j=j, t1=t1: e.tensor_tensor(
                                t1[:].rearrange("p (h q) -> p h q", h=8), po[:].rearrange("p (h q) -> p h q", h=8),
                                Et[:, j, ecol + g * 8: ecol + g * 8 + 8].unsqueeze(2).to_broadcast([128, 8, 64]), ALU.mult),
                                reads=bkb(bo) + [BEt], writes=[t1b])
                            P.op("dve", lambda e, g=g, j=j, t1=t1: e.tensor_tensor(RY[:, j, g * 512:(g + 1) * 512], RY[:, j, g * 512:(g + 1) * 512], t1[:], ALU.add),
                                 reads=[t1b, BRY], writes=[BRY])
                        for g in range(2):
                            P.op("dve", lambda e, g=g, j=j: e.tensor_tensor(
                                state[g * 64:(g + 1) * 64, d, :].rearrange("p (h q) -> p h q", h=8),
                                state[g * 64:(g + 1) * 64, d, :].rearrange("p (h q) -> p h q", h=8),
                                dch[g * 64:(g + 1) * 64, j, d * 16 + g * 8: d * 16 + g * 8 + 8].unsqueeze(2).to_broadcast([64, 8, 64]), ALU.mult),
                                reads=[Bstate, Bdch], writes=[Bstate])
                        if addS:
                            P.op("dve", lambda e, j=j: e.tensor_tensor(state[:, d, :], state[:, d, :], Ssrc[:, j, :], ALU.add),
                                 reads=[Bstate, BRK], writes=[Bstate])
                    else:
                        P.op("dve", lambda e, j=j: e.tensor_copy(state[:, d, :], Ssrc[:, j, :]), reads=[BRK], writes=[Bstate])
                        have = True
                if finals is not None:
                    finals(s, d)

        def prompt_final(s, d):
            st, stb = stg2[(s * 2 + d) % 2]
            for blk in range(4):
                bi = nextbank(0, 4)
                pt = banks[bi][0]
                P.mm([lambda e, pt=pt, blk=blk: e.transpose(pt[:, 0:128], state[:, d, blk * 128:(blk + 1) * 128], identf)],
                     reads=[Bstate, Bcst], writes=[bkb(bi)[0]])
                P.op("act", lambda e, pt=pt, blk=blk, st=st: e.copy(st[:, blk * 128:(blk + 1) * 128], pt[:, 0:128]), reads=[bkb(bi)[0]], writes=[stb])
            for g in range(2):
                dst = nst_d[s, l, d, g * 8:(g + 1) * 8].rearrange("(b h2) p n -> (h2 p) b n", b=4)
                P.dma("sp", dst, st[:, 0:512].rearrange("p (b gn) -> p b gn", b=4)[:, :, g * 64:(g + 1) * 64], reads=[stb])

        def gate_norm():
            w0 = wload(win_d[l][:, C_Z:C_Z + 512], 512)
            w1 = wload(win_d[l][:, C_Z + 512:C_Z + 1024], 512)
            P.dma("sp", rowA[:], mnw_d[l:l + 1, :].partition_broadcast(128), writes=[BrowA])
            for j in range(8):
                st, stb = stg2[j % 2]
                for half, (wt, wbf) in enumerate((w0, w1)):
                    bi = nextbank()
                    pt = banks[bi][0]
                    P.mm([lambda e, kc=kc, pt=pt, wt=wt, j=j: e.matmul(pt[:], hT[:, kc, j * 128:(j + 1) * 128], wt[:, kc, :],
                                                                       start=(kc == 0), stop=(kc == 7)) for kc in range(8)],
                         reads=[wbf, BhT[j]], writes=bkb(bi))
                    t1, t1b = tf[half]
                    P.op("act", lambda e, pt=pt, t1=t1: e.activation(t1[:], pt[:], AF.Silu), reads=bkb(bi), writes=[t1b])
                    P.op("dve", lambda e, t1=t1, st=st, half=half, j=j: e.tensor_tensor(st[:, half * 512:(half + 1) * 512], t1[:], RY[:, j, half * 512:(half + 1) * 512], ALU.mult),
                         reads=[t1b, BRY], writes=[stb])
                P.op("act", lambda e, st=st: e.activation(stg[:, 1, :], st[:], AF.Square, accum_out=small[:, 0:1]), reads=[stb], writes=[Bstg, Bsmall])
                rstd_from(small[:, 0:1], 1024, small[:, 1:2], [Bsmall], Bsmall)
                gn, gnb = tmpb[j % 2]
                P.op("dve", lambda e, st=st, gn=gn: e.scalar_tensor_tensor(gn[:, 0, :], st[:], small[:, 1:2], rowA[:], ALU.mult, ALU.mult),
                     reads=[stb, Bsmall, BrowA], writes=[gnb])
                bi = nextbank(0, 4)
                ptv = banks[bi][0][:].bitcast(BF16)
                P.mm([lambda e, kc=kc, gn=gn, ptv=ptv: e.transpose(ptv[:, kc * 128:(kc + 1) * 128], gn[:, 0, kc * 128:(kc + 1) * 128], identb[:])
                      for kc in range(8)], reads=[gnb, Bidb], writes=bkb(bi))
                P.op("act", lambda e, ptv=ptv, j=j: e.copy(YA[:, :, j * 128:(j + 1) * 128], ptv.rearrange("p (k t) -> p k t", k=8)),
                     reads=bkb(bi), writes=[BRV])

        def merge(b, ysrc, Bys, first):
            for wblk in range(2):
                wg, wgb = wload(win_d[l][:, C_MG + b * 1024 + wblk * 512: C_MG + b * 1024 + (wblk + 1) * 512], 512)
                wr, wrb = wload(wbr_d[l, b][:, wblk * 512:(wblk + 1) * 512], 512)
                for s in range(4):
                    fo = wblk * 4 + s
                    for nb in range(2):
                        bi = nextbank()
                        pt = banks[bi][0]
                        P.mm([lambda e, kc=kc, s=s, nb=nb, pt=pt: e.matmul(
                            pt[:], wg[:, kc, s * 128:(s + 1) * 128], hT[:, kc, nb * 512:(nb + 1) * 512],
                            start=(kc == 0), stop=(kc == 7)) for kc in range(8)], reads=[wgb] + BhT[nb * 4:(nb + 1) * 4], writes=bkb(bi))
                        gt, gtb = tf[2 + (fo + nb) % 2]
                        P.op("act", lambda e, pt=pt, gt=gt: e.activation(gt[:], pt[:], AF.Sigmoid), reads=bkb(bi), writes=[gtb])
                        b2 = nextbank()
                        p2 = banks[b2][0]
                        P.mm([lambda e, kc=kc, s=s, nb=nb, p2=p2: e.matmul(
                            p2[:], wr[:, kc, s * 128:(s + 1) * 128], ysrc[:, kc, nb * 512:(nb + 1) * 512],
                            start=(kc == 0), stop=(kc == 7)) for kc in range(8)], reads=[wrb, Bys], writes=bkb(b2))
                        mdst = MG[:, fo, nb * 512:(nb + 1) * 512]
                        if first:
                            P.op("dve", lambda e, p2=p2, gt=gt, mdst=mdst: e.tensor_tensor(mdst, p2[:], gt[:], ALU.mult),
                                 reads=bkb(b2) + [gtb], writes=[BRY])
                        else:
                            P.op("dve", lambda e, p2=p2, gt=gt: e.tensor_tensor(gt[:], p2[:], gt[:], ALU.mult),
                                 reads=bkb(b2) + [gtb], writes=[gtb])
                            P.op("dve", lambda e, gt=gt, mdst=mdst: e.tensor_tensor(mdst, mdst, gt[:], ALU.add), reads=[gtb, BRY], writes=[BRY])

        def attention():
            proj_fm(C_Q, 1024, ev_rope(RQ, BRA) if kind == 1 else ev_copy(RQ, BRA))
            if kind == 0:
                proj_fm(C_K, 1024, ev_copy(RK, BRK))
                proj_k_tm_out()
                proj_v()
            proj_fm(C_GB, 1024, ev_copy(RG, BRG, AF.Silu))
            if kind == 1:
                for jt in range(2):
                    P.dma("sp", stg[:, jt, :], ck_d[l, jt * 128:(jt + 1) * 128, :], writes=[Bstg])
                for h in range(8):
                    for jt in range(2):
                        bi = nextbank()
                        pt = banks[bi][0]
                        P.mm([lambda e, h=h, jt=jt, pt=pt: e.transpose(pt[:, 0:128], stg[:, jt, h * 128:(h + 1) * 128], identf)],
                             reads=[Bstg, Bcst], writes=[bkb(bi)[0]])
                        P.op("act", lambda e, h=h, jt=jt, pt=pt: e.copy(ctk_ap(h, jt * 128, (jt + 1) * 128), pt[:, 0:128]),
                             reads=[bkb(bi)[0]], writes=[BRA])
                P.dma("pool", ctv[:], cv_d[l].rearrange("(j p) c -> p j c", p=128), writes=[Bctv])
            scale = 64 ** -0.5
            qblocks = [(s * 256, 256) for s in range(4)] if kind == 0 else [(0, 512), (512, 512)]
            it = 0
            for h in range(8):
                if kind == 1:
                    slab, slabB = (RK, BRK) if h % 2 == 0 else (RV, BRV)
                    kt = slab[:, 0:4, :].rearrange("p a t -> p (a t)")
                    vt = slab[:, 4:8, :].rearrange("p a (j d) -> p (a j) d", d=128)
                    g4 = gkv[par][h // 4].rearrange("(r x) t -> x r t", r=4)
                    P.dma("sp", slab[:, 0:4, :], g4[(h % 4) * 128:(h % 4 + 1) * 128, :, :],
                          reads=[Bd[f"gkv{par}"]], writes=[slabB])
                    for half in range(2):
                        gv = gkv[par][2 + half].rearrange("(r j p) c -> p r j c", r=4, j=4, p=128)
                        for r in range(4):
                            P.dma("sp", slab[:, 4 + r, half * 512:(half + 1) * 512].rearrange("p (j d) -> p j d", d=128),
                                  gv[:, r, :, h * 128:(h + 1) * 128], reads=[Bd[f"gkv{par}"]], writes=[slabB])
                for (q0, nq) in qblocks:
                    if kind == 0:
                        s = q0 // 256
                        ktiles = [(RK[:, h, s * 256 + jt * 128: s * 256 + (jt + 1) * 128], [BRK],
                                   RV[:, s * 2 + jt, h * 128:(h + 1) * 128], [BRV]) for jt in range(2)]
                    else:
                        ktiles = [(ctk_ap(h, jt * 128, (jt + 1) * 128), [BRA], ctv[:, jt, h * 128:(h + 1) * 128], [Bctv]) for jt in range(2)]
                        ktiles += [(kt[:, jt * 128:(jt + 1) * 128], [slabB], vt[:, jt, :], [slabB]) for jt in range(32)]
                    nkt = len(ktiles)
                    acc = [4, 5, 6, 7]
                    def qk_exp(ti):
                        kap, kbufs, _, _ = ktiles[ti]
                        sb0 = (ti % 2) * 2
                        pb, pbb = ptb[ti % 2]
                        for m in range(2):
                            ps = banks[sb0 + m][0]
                            P.mm([lambda e, m=m, ps=ps, kap=kap: e.matmul(ps[:, 0:nq], kap[m * 64:(m + 1) * 64, :],
                                                                        RQ[m * 64:(m + 1) * 64, h, q0:q0 + nq], start=True, stop=True)],
                                 reads=kbufs + [BRA], writes=bkb(sb0 + m))
                            P.op("act", lambda e, m=m, ps=ps, pb=pb: e.activation(pb[:, m, 0:nq], ps[:, 0:nq], AF.Exp, scale=scale),
                                 reads=bkb(sb0 + m), writes=[pbb])
                    qk_exp(0)
                    for ti, (kap, kbufs, vap, vbufs) in enumerate(ktiles):
                        if ti + 1 < nkt:
                            qk_exp(ti + 1)
                        pb, pbb = ptb[ti % 2]
                        for m in range(2):
                            po = banks[acc[m]][0]
                            psm = banks[acc[2 + m]][0]
                            P.mm([lambda e, m=m, po=po, pb=pb, vap=vap, ti=ti: e.matmul(po[:, 0:nq], vap, pb[:, m, 0:nq], start=(ti == 0), stop=(ti == nkt - 1)),
                                  lambda e, m=m, psm=psm, pb=pb, ti=ti: e.matmul(psm[:, 0:nq], onesb[:], pb[:, m, 0:nq], start=(ti == 0), stop=(ti == nkt - 1))],
                                 reads=vbufs + [pbb, Bonesb], writes=bkb(acc[m]) + bkb(acc[2 + m]))
                    r1, r1b = tf[2]
                    r2, r2b = tf[3]
                    o1, o1b = tf[0]
                    o2, o2b = tf[1]
                    P.op("dve", lambda e: e.reciprocal(r1[:, 0:nq], banks[6][0][:, 0:nq]), reads=bkb(6), writes=[r1b])
                    P.op("dve", lambda e: e.reciprocal(r2[:, 0:nq], banks[7][0][:, 0:nq]), reads=bkb(7), writes=[r2b])
                    P.op("dve", lambda e: e.tensor_tensor(o1[:, 0:nq], banks[4][0][:, 0:nq], r1[:, 0:nq], ALU.mult), reads=bkb(4) + [r1b], writes=[o1b])
                    P.op("dve", lambda e: e.tensor_tensor(o2[:, 0:nq], banks[5][0][:, 0:nq], r2[:, 0:nq], ALU.mult), reads=bkb(5) + [r2b], writes=[o2b])
                    P.op("dve", lambda e: e.scalar_tensor_tensor(o1[:, 0:nq], o2[:, 0:nq], lamt[:, 261:262], o1[:, 0:nq], ALU.mult, ALU.add),
                         reads=[o1b, o2b, Blamt], writes=[o1b])
                    sq, sqb = tmpb[0]
                    P.op("act", lambda e: e.activation(sq[:, 0, 0:nq], o1[:, 0:nq], AF.Square), reads=[o1b], writes=[sqb])
                    P.mm([lambda e: e.matmul(banks[6][0][:, 0:nq], onesb[:], sq[:, 0, 0:nq], start=True, stop=True)],
                         reads=[sqb, Bonesb], writes=bkb(6))
                    P.op("act", lambda e: e.activation(r1[:, 0:nq], banks[6][0][:, 0:nq], AF.Sqrt, bias=c_eps, scale=1.0 / 128),
                         reads=bkb(6) + [Bcst], writes=[r1b])
                    P.op("dve", lambda e: e.reciprocal(r1[:, 0:nq], r1[:, 0:nq]), reads=[r1b], writes=[r1b])
                    P.op("dve", lambda e: e.scalar_tensor_tensor(o1[:, 0:nq], o1[:, 0:nq], hwT[:, 1:2], r1[:, 0:nq], ALU.mult, ALU.mult),
                         reads=[o1b, BhwT, r1b], writes=[o1b])
                    P.op("dve", lambda e, h=h, q0=q0, nq=nq: e.tensor_tensor(RG[:, h, q0:q0 + nq], o1[:, 0:nq], RG[:, h, q0:q0 + nq], ALU.mult),
                         reads=[o1b, BRG], writes=[BRG])

        def branch_c():
            proj_fm(C_U, 1024, ev_copy(RQ, BRA, AF.Gelu))
            proj_fm(C_GC, 1024, ev_copy(RK, BRK, AF.Silu))

            def ev_sv(j, cbk, pt, bi, n):
                P.op("act", lambda e: e.activation(RV[:, j, cbk * 512:(cbk + 1) * 512], pt[:], AF.Gelu), reads=bkb(bi), writes=[BRV])
            proj_tm(C_SV, 1024, ev_sv)
            P.dma("sp", rowA[:], vnw_d[l:l + 1, :].partition_broadcast(128), writes=[BrowA])
            P.dma("sp", rowB[:], sgb_d[l:l + 1, :].partition_broadcast(128), writes=[BrowB])
            for j in range(8):
                st, stb = stg2[j % 2]
                P.op("act", lambda e, st=st, j=j: e.activation(st[:], RV[:, j, :], AF.Identity, accum_out=small[:, 0:1]), reads=[BRV], writes=[stb, Bsmall])
                P.op("act", lambda e, st=st, j=j: e.activation(st[:], RV[:, j, :], AF.Square, accum_out=small[:, 1:2]), reads=[BRV], writes=[stb, Bsmall])
                P.op("dve", lambda e: e.tensor_single_scalar(small[:, 2:3], small[:, 0:1], 1.0 / 1024, ALU.mult), reads=[Bsmall], writes=[Bsmall])
                P.op("dve", lambda e: e.tensor_tensor(small[:, 3:4], small[:, 2:3], small[:, 2:3], ALU.mult), reads=[Bsmall], writes=[Bsmall])
                P.op("dve", lambda e: e.scalar_tensor_tensor(small[:, 4:5], small[:, 1:2], 1.0 / 1024, small[:, 3:4], ALU.mult, ALU.subtract),
                     reads=[Bsmall], writes=[Bsmall])
                P.op("act", lambda e: e.activation(small[:, 5:6], small[:, 4:5], AF.Sqrt, bias=c_eps, scale=1.0), reads=[Bsmall, Bcst], writes=[Bsmall])
                P.op("dve", lambda e: e.reciprocal(small[:, 5:6], small[:, 5:6]), reads=[Bsmall], writes=[Bsmall])
                P.op("dve", lambda e, st=st, j=j: e.tensor_scalar(st[:], RV[:, j, :], small[:, 2:3], small[:, 5:6], ALU.subtract, ALU.mult),
                     reads=[BRV, Bsmall], writes=[stb])
                vn, vnb = tmpb[j % 2]
                P.op("dve", lambda e, vn=vn, st=st: e.tensor_tensor(vn[:, 0, :], st[:], rowA[:], ALU.mult), reads=[stb, BrowA], writes=[vnb])
                b0 = (nextbank(0, 4) // 2) * 2
                for half in range(2):
                    pt = banks[b0 + half][0]
                    P.mm([lambda e, g=g, pt=pt, vn=vn: e.matmul(pt[:, (g % 4) * 128:(g % 4 + 1) * 128], vn[:, 0, g * 128:(g + 1) * 128], wsT[:, g, :],
                                                             start=True, stop=True) for g in range(half * 4, half * 4 + 4)],
                         reads=[vnb, BwsT], writes=bkb(b0 + half))
                    t1, t1b = tf[half]
                    t1v = t1[:].rearrange("p (g t) -> p g t", g=4)
                    P.op("dve", lambda e, pt=pt, t1v=t1v, half=half: e.tensor_tensor(
                        t1v, pt[:].rearrange("p (g t) -> p g t", g=4), rowB[:, half * 512:(half + 1) * 512].rearrange("p (g t) -> p g t", g=4), ALU.add),
                        reads=bkb(b0 + half) + [BrowB], writes=[t1b])
                    dst = RQ[:, half * 4:half * 4 + 4, j * 128:(j + 1) * 128]
                    P.op("dve", lambda e, t1v=t1v, dst=dst: e.tensor_tensor(t1v, t1v, dst, ALU.mult), reads=[t1b, BRA], writes=[t1b])
                    P.op("dve", lambda e, t1v=t1v, dst=dst, half=half, j=j: e.tensor_tensor(dst, t1v, RK[:, half * 4:half * 4 + 4, j * 128:(j + 1) * 128], ALU.mult),
                         reads=[t1b, BRK], writes=[BRA])

        def out_proj():
            P.dma("sp", rowB[:], bmod_d[l:l + 1, 2048:3072].partition_broadcast(128), writes=[BrowB])
            P.dma("sp", rowA[:], post_d[l:l + 1, :].partition_broadcast(128), writes=[BrowA])
            for nb in range(2):
                wt, wbf = wload(wmod_d[l][:, 2048 + nb * 512: 2048 + (nb + 1) * 512], 512)
                bi = nextbank()
                pt = banks[bi][0]
                P.mm([lambda e, kc=kc, pt=pt, wt=wt: e.matmul(pt[:], scRep[:, c, kc, :], wt[:, kc, :],
                                                              start=(kc == 0), stop=(kc == 7)) for kc in range(8)],
                     reads=[wbf, BscRep], writes=bkb(bi))
                P.op("dve", lambda e, nb=nb, pt=pt: e.tensor_tensor(gpw[:, nb * 512:(nb + 1) * 512], pt[:], rowB[:, nb * 512:(nb + 1) * 512], ALU.add),
                     reads=bkb(bi) + [BrowB], writes=[Bgpw])
            P.op("dve", lambda e: e.tensor_tensor(gpw[:], gpw[:], rowA[:], ALU.mult), reads=[Bgpw, BrowA], writes=[Bgpw])
            w0 = wload(wout_d[l][:, 0:512], 512)
            w1 = wload(wout_d[l][:, 512:1024], 512)
            for j in range(8):
                bis = []
                for half, (wt, wbf) in enumerate((w0, w1)):
                    bi = nextbank()
                    pt = banks[bi][0]
                    P.mm([lambda e, kc=kc, pt=pt, wt=wt, j=j: e.matmul(pt[:], MG[:, kc, j * 128:(j + 1) * 128], wt[:, kc, :],
                                                                       start=(kc == 0), stop=(kc == 7)) for kc in range(8)],
                         reads=[wbf, BRY], writes=bkb(bi))
                    bis.append(bi)
                    P.op("act", lambda e, pt=pt, half=half: e.activation(stg[:, 1, half * 512:(half + 1) * 512], pt[:], AF.Square,
                                                                          accum_out=small[:, 8 + half:9 + half]),
                         reads=bkb(bi), writes=[Bstg, Bsmall])
                P.op("dve", lambda e: e.tensor_tensor(small[:, 0:1], small[:, 8:9], small[:, 9:10], ALU.add), reads=[Bsmall], writes=[Bsmall])
                rstd_from(small[:, 0:1], 1024, small[:, 1:2], [Bsmall], Bsmall)
                st, stb = stg2[j % 2]
                P.dma("sp", stg[:, 0, :], xsrc[j * 128:(j + 1) * 128, :], reads=[Bxres] if l > 0 else [], writes=[Bstg])
                for half in range(2):
                    pt = banks[bis[half]][0]
                    P.op("dve", lambda e, pt=pt, half=half, st=st: e.scalar_tensor_tensor(
                        st[:, half * 512:(half + 1) * 512], pt[:], small[:, 1:2], gpw[:, half * 512:(half + 1) * 512], ALU.mult, ALU.mult),
                        reads=bkb(bis[half]) + [Bsmall, Bgpw], writes=[stb])
                P.op("dve", lambda e, st=st: e.tensor_tensor(st[:], st[:], stg[:, 0, :], ALU.add), reads=[stb, Bstg], writes=[stb])
                P.dma("sp", xdst[j * 128:(j + 1) * 128, :], st[:], reads=[stb], writes=[Bxres] if l < DEPTH - 1 else [])

        ck(f"conv{l}{kind}")
        ssd_main()
        ck(f"ssd_main{l}{kind}")
        if kind == 0:
            chain(0, None, [], True, prompt_final)
            chain(1, None, [], True, prompt_final)
        else:
            Bb = Bd[f"bst{par}"]
            chain(0, None, [], True, None)
            P.dma("sp", bst[par][:, 0:512], state[:, 0, :], reads=[Bstate], writes=[Bb])
            chain(1, None, [], True, None)
            P.dma("sp", bst[par][:, 512:1024], state[:, 1, :], reads=[Bstate], writes=[Bb])
            P.op("dve", lambda e: e.reduce_sum(small[:, 0:32], totl[:].rearrange("p j c -> p c j"), axis=AX.X), reads=[Btotl], writes=[Bsmall])
            P.dma("sp", bst[par][:, 1024:1056], small[:, 0:32], reads=[Bsmall], writes=[Bb])
            P.cc("AllGather", GROUPS4, bst[par], gst[par], reads=[Bb], writes=[Bd[f"gst{par}"]])
            for d in range(2):
                for g in range(2):
                    P.dma("sp", stg[:, d, 0:512].rearrange("p (b gn) -> p b gn", b=4)[:, :, g * 64:(g + 1) * 64],
                          sst_d[l, d, g * 8:(g + 1) * 8].rearrange("(b h2) p n -> (h2 p) b n", b=4), writes=[Bstg])
                for blk in range(4):
                    bi = nextbank(0, 4)
                    pt = banks[bi][0]
                    P.mm([lambda e, pt=pt, blk=blk, d=d: e.transpose(pt[:, 0:128], stg[:, d, blk * 128:(blk + 1) * 128], identf)],
                         reads=[Bstg, Bcst], writes=[bkb(bi)[0]])
                    P.op("act", lambda e, pt=pt, blk=blk, d=d: e.copy(hin[:, d, blk * 128:(blk + 1) * 128], pt[:, 0:128]),
                         reads=[bkb(bi)[0]], writes=[Bhin])
            for d in range(2):
                order = [0, 1, 2] if d == 0 else [3, 2, 1]
                for i in order:
                    ucol = rkc[:, 8 + d * 4 + i: 9 + d * 4 + i]
                    P.dma("sp", gall[:], gst[par][i * 128:(i + 1) * 128, :], reads=[Bd[f"gst{par}"]], writes=[Bgall])
                    P.op("act", lambda e, d=d: e.activation(small[:, 0:16], gall[:, 1024 + d * 16: 1040 + d * 16], AF.Exp),
                         reads=[Bgall], writes=[Bsmall])
                    P.op("dve", lambda e, ucol=ucol: e.tensor_scalar(small[:, 0:16], small[:, 0:16], -1.0, ucol, ALU.add, ALU.mult),
                         reads=[Bsmall, Brkc], writes=[Bsmall])
                    P.op("dve", lambda e: e.tensor_single_scalar(small[:, 0:16], small[:, 0:16], 1.0, ALU.add), reads=[Bsmall], writes=[Bsmall])
                    for g in range(2):
                        P.op("dve", lambda e, g=g, d=d: e.tensor_tensor(
                            hin[g * 64:(g + 1) * 64, d, :].rearrange("p (h q) -> p h q", h=8),
                            hin[g * 64:(g + 1) * 64, d, :].rearrange("p (h q) -> p h q", h=8),
                            small[g * 64:(g + 1) * 64, g * 8:g * 8 + 8].unsqueeze(2).to_broadcast([64, 8, 64]), ALU.mult),
                            reads=[Bhin, Bsmall], writes=[Bhin])
                    P.op("dve", lambda e, d=d, ucol=ucol: e.scalar_tensor_tensor(
                        hin[:, d, :], gall[:, d * 512:(d + 1) * 512], ucol, hin[:, d, :], ALU.mult, ALU.add),
                        reads=[Bgall, Brkc, Bhin], writes=[Bhin])
            chain(0, hin[:, 0, :], [Bhin], False, None)
            chain(1, hin[:, 1, :], [Bhin], False, None)
        if l == 0:
            dbg(f"y{kind}", RY[:], [BRY], [128, 8, 1024])
        ck(f"chains{l}{kind}")
        gate_norm()
        if l == 0:
            dbg(f"ya{kind}", YA[:], [BRV], [128, 8, 1024])
        ck(f"gate_norm{l}{kind}")
        merge(0, YA, BRV, True)
        ck(f"merge0{l}{kind}")
        attention()
        if l == 0:
            dbg(f"yb{kind}", RG[:], [BRG], [128, 8, 1024])
        ck(f"attn{l}{kind}")
        merge(1, RG, BRG, False)
        branch_c()
        if l == 0:
            dbg(f"yc{kind}", RQ[:, 0:8, 0:1024], [BRA], [128, 8, 1024])
        ck(f"brc{l}{kind}")
        merge(2, RQ, BRA, False)
        if l == 0:
            dbg(f"mg{kind}", MG[:], [BRY], [128, 8, 1024])
        out_proj()
        ck(f"out{l}{kind}")

    try:
        for l in range(DEPTH):
            layer_prep(l)
            ck(f"prep{l}")
            run_group(l, 0)
            run_group(l, 1)
    except _Stop:
        pass

    P.emit()
    P.stack.close()
    return nc


_NC = None
_STOP = int(os.environ["KSTOP"]) if os.environ.get("KSTOP") else None


def _consts():
    r = np.arange(128)
    c = np.zeros((128, 1024), np.float32)
    c[:, 0:128] = np.eye(128)
    c[:, 128:256] = (r[:, None] <= r[None, :])
    c[:, 256:384] = (r[:, None] >= r[None, :])
    c[:, 384:512] = (r[:, None] > r[None, :])
    c[:, 512:640] = (r[:, None] < r[None, :])
    Rm = np.zeros((128, 128), np.float32)
    for m in range(2):
        for j in range(32):
            Rm[m * 64 + j, m * 64 + 32 + j] = -1.0
            Rm[m * 64 + 32 + j, m * 64 + j] = 1.0
    c[:, 640:768] = Rm.T
    c[:, 768:896] = 1.0
    c[:, 896] = 1.0
    c[:, 897] = EPS
    c[:, 898] = 0.0
    return c


def _rope_tables():
    T_ = 4096
    rows = np.repeat(np.arange(T_ // 64, dtype=np.float32), 64)
    cols = np.tile(np.arange(64, dtype=np.float32), T_ // 64)
    inv = (10000.0 ** (-np.arange(16, dtype=np.float32) / 16)).astype(np.float32)
    ang = np.concatenate([rows[:, None] * inv, cols[:, None] * inv], -1).astype(np.float32)
    return np.cos(ang).astype(np.float32), np.sin(ang).astype(np.float32)


def kernel(**inp):
    global _NC
    if _NC is None:
        _NC = build(_STOP)
    nc = _NC
    f = lambda a: np.ascontiguousarray(np.asarray(a, dtype=np.float32))
    cst = _consts()
    cos, sin = _rope_tables()
    shared = {
        "pre_norm_w": f(inp["pre_norm_w"]), "post_norm_w": f(inp["post_norm_w"]),
        "w_mod": f(inp["w_mod"]), "b_mod": f(inp["b_mod"]), "w_in": f(inp["w_in"]),
        "m_conv_w": f(inp["m_conv_w"]), "m_conv_b": f(inp["m_conv_b"]),
        "m_A_log": f(inp["m_A_log"]).reshape(DEPTH, 32), "m_dt_bias": f(inp["m_dt_bias"]).reshape(DEPTH, 32),
        "m_D": f(inp["m_D"]).reshape(DEPTH, 32), "m_norm_w": f(inp["m_norm_w"]),
        "da_lambda": f(inp["da_lambda"]).reshape(DEPTH, 256), "da_head_norm_w": f(inp["da_head_norm_w"]),
        "sg_vnorm_w": f(inp["sg_vnorm_w"]), "sg_spatial_w": f(inp["sg_spatial_w"]),
        "sg_spatial_b": f(inp["sg_spatial_b"]).reshape(DEPTH, 1024),
        "w_branch": f(inp["w_branch"]), "w_out": f(inp["w_out"]), "cst": cst,
    }
    xp = f(inp["x_prompt"]); xs = f(inp["x_sample"])
    ck = f(inp["cache_k"]).reshape(2, DEPTH, 256, D); cv = f(inp["cache_v"]).reshape(2, DEPTH, 256, D)
    sst = f(inp["state_ssm"]); cc_ = f(inp["c"]); cctx = f(inp["c_ctx"])
    in_maps = []
    for core in range(8):
        b, j = core // 4, core % 4
        rk = np.zeros((128, 16), np.float32)
        if j > 0:
            rk[:, j - 1] = 1.0
        if j < 3:
            rk[:, 4 + j + 1] = 1.0
        for i in range(4):
            rk[:, 8 + i] = 1.0 if i < j else 0.0
            rk[:, 12 + i] = 1.0 if i > j else 0.0
        t0 = j * 1024
        ct = np.tile(cos[t0:t0 + 1024].T, (4, 1))
        stb = np.tile(sin[t0:t0 + 1024].T, (4, 1))
        m = dict(shared)
        m.update({
            "xp": np.ascontiguousarray(xp[core * 4:(core + 1) * 4].reshape(1024, D)),
            "xs": np.ascontiguousarray(xs[b, t0:t0 + 1024]),
            "ck": np.ascontiguousarray(ck[b]), "cv": np.ascontiguousarray(cv[b]),
            "sst": np.ascontiguousarray(sst[b]),
            "cvec": np.ascontiguousarray(np.stack([cctx, cc_[b]], 0)),
            "rkc": rk, "ropec": np.ascontiguousarray(np.stack([ct, stb], 1)),
        })
        in_maps.append(m)
    res = run_bass_kernel_spmd(nc, in_maps, core_ids=list(range(8)))
    R = res.results
    yp = np.concatenate([R[i]["yp"].reshape(4, 256, D) for i in range(8)], 0)
    ys = np.stack([np.concatenate([R[b * 4 + j]["ys"] for j in range(4)], 0) for b in range(2)], 0)
    nk = np.concatenate([R[i]["nk"] for i in range(8)], 0).reshape(32, DEPTH, 256, 8, 128)
    nv = np.concatenate([R[i]["nv"] for i in range(8)], 0).reshape(32, DEPTH, 256, 8, 128)
    nst = np.concatenate([R[i]["nst"] for i in range(8)], 0)
    return (yp.astype(np.float32), ys.astype(np.float32), nk.astype(np.float32), nv.astype(np.float32), nst.astype(np.float32))
```
